# Optimizing a Trainium2 kernel written in Bass

```python
import jax, jax.numpy as jnp
from jax import lax
import numpy as np

D_MODEL = 2048
BATCH = 8
SEQ = 4096
DEPTH = 4
DEC_BATCH = 32
DEC_SEQ = 16
PAST_LEN = 2048

CHUNK = 64
Q_BLOCK = 128
N_HEADS = 16
Q_LORA = 512
KV_LORA = 512
D_NOPE = 128
D_ROPE = 64
D_V = 128
ROPE_BASE = 10000.0
ATTN_SCALE = (D_NOPE + D_ROPE) ** -0.5
POOL_WINDOWS = (2, 4, 8, 16)
N_POOL_GROUPS = len(POOL_WINDOWS)
POOL_GROUP_DIM = D_MODEL // N_POOL_GROUPS
POOL_HIST = max(POOL_WINDOWS) - 1
D_FF = 5632
CONV_WIDTH = 3
N_MLA_LAYERS = (DEPTH + 1) // 2
N_POOL_LAYERS = DEPTH // 2
EPS = 1e-6

kernel_name = "hybrid_mla_pool_convffn_stream_step"


def rms_norm(x, g):
    xf = x.astype(jnp.float32)
    y = xf * lax.rsqrt(jnp.mean(xf * xf, axis=-1, keepdims=True) + EPS)
    return (y * g.astype(jnp.float32)).astype(x.dtype)


def rope(x, pos):
    half = D_ROPE // 2
    inv = ROPE_BASE ** (-jnp.arange(half, dtype=jnp.float32) / half)
    ang = pos[:, None] * inv[None, :]
    shape = (1, pos.shape[0]) + (1,) * (x.ndim - 3) + (half,)
    c = jnp.cos(ang).reshape(shape)
    s = jnp.sin(ang).reshape(shape)
    xf = x.astype(jnp.float32)
    x1, x2 = xf[..., :half], xf[..., half:]
    return jnp.concatenate([x1 * c - x2 * s, x1 * s + x2 * c], axis=-1).astype(x.dtype)


def chunk_mask(qpos, kpos):
    return (kpos // CHUNK)[None, :] <= (qpos // CHUNK)[:, None]


def masked_softmax(s, mask):
    s = s.astype(jnp.float32) * ATTN_SCALE
    s = jnp.where(mask[None, None], s, -jnp.inf)
    return jax.nn.softmax(s, axis=-1)


def mla_project(h, pos, w_dq, q_norm, w_uq, w_dkv, kv_norm):
    B, S, _ = h.shape
    cq = rms_norm(h @ w_dq, q_norm)
    q = (cq @ w_uq).reshape(B, S, N_HEADS, D_NOPE + D_ROPE)
    q_nope = q[..., :D_NOPE]
    q_rope = rope(q[..., D_NOPE:], pos)
    kv = h @ w_dkv
    ckv = rms_norm(kv[..., :KV_LORA], kv_norm)
    kr = rope(kv[..., KV_LORA:], pos)
    return q_nope, q_rope, ckv, kr


def mla_prompt(h, w_dq, q_norm, w_uq, w_dkv, kv_norm, w_uk, w_uv, w_o):
    B, S, _ = h.shape
    pos_i = jnp.arange(S, dtype=jnp.int32)
    q_nope, q_rope, ckv, kr = mla_project(h, pos_i.astype(jnp.float32), w_dq, q_norm, w_uq, w_dkv, kv_norm)
    k_nope = jnp.einsum('bkc,chd->bkhd', ckv, w_uk)
    v = jnp.einsum('bkc,chd->bkhd', ckv, w_uv)
    outs = []
    for j in range(S // Q_BLOCK):
        q0, q1 = j * Q_BLOCK, (j + 1) * Q_BLOCK
        s = (jnp.einsum('bqhd,bkhd->bhqk', q_nope[:, q0:q1], k_nope[:, :q1])
             + jnp.einsum('bqhr,bkr->bhqk', q_rope[:, q0:q1], kr[:, :q1]))
        p = masked_softmax(s, chunk_mask(pos_i[q0:q1], pos_i[:q1])).astype(v.dtype)
        outs.append(jnp.einsum('bhqk,bkhd->bqhd', p, v[:, :q1]))
    o = jnp.concatenate(outs, axis=1).reshape(B, S, N_HEADS * D_V)
    return o @ w_o, ckv, kr


def mla_sample(h, cache_ckv, cache_kr, w_dq, q_norm, w_uq, w_dkv, kv_norm, w_uk, w_uv, w_o):
    B, S, _ = h.shape
    P = cache_ckv.shape[1]
    qpos = P + jnp.arange(S, dtype=jnp.int32)
    kpos = jnp.arange(P + S, dtype=jnp.int32)
    q_nope, q_rope, ckv, kr = mla_project(h, qpos.astype(jnp.float32), w_dq, q_norm, w_uq, w_dkv, kv_norm)
    ckv_all = jnp.concatenate([cache_ckv, ckv], axis=1)
    kr_all = jnp.concatenate([cache_kr, kr], axis=1)
    q_lat = jnp.einsum('bqhd,chd->bqhc', q_nope, w_uk)
    s = (jnp.einsum('bqhc,bkc->bhqk', q_lat, ckv_all)
         + jnp.einsum('bqhr,bkr->bhqk', q_rope, kr_all))
    p = masked_softmax(s, chunk_mask(qpos, kpos)).astype(ckv_all.dtype)
    o_lat = jnp.einsum('bhqk,bkc->bqhc', p, ckv_all)
    o = jnp.einsum('bqhc,chd->bqhd', o_lat, w_uv).reshape(B, S, N_HEADS * D_V)
    return o @ w_o, ckv, kr


def pool_mix(h, hist, pos0, w_pool, scale):
    B, S, D = h.shape
    Hh = hist.shape[1]
    xcat = jnp.concatenate([hist, h], axis=1)
    T = Hh + S
    pos = (pos0 - Hh) + jnp.arange(T, dtype=jnp.float32)
    xg = xcat.astype(jnp.float32).reshape(B, T, N_POOL_GROUPS, POOL_GROUP_DIM)
    cs = jnp.cumsum(xg, axis=1)
    means = []
    for g, w in enumerate(POOL_WINDOWS):
        csg = cs[:, :, g]
        lag = jnp.pad(csg, ((0, 0), (w, 0), (0, 0)))[:, :T]
        cnt = jnp.minimum(jnp.float32(w), pos + 1.0)
        means.append(((csg - lag) / cnt[None, :, None])[:, Hh:])
    mean = jnp.stack(means, axis=2)
    d = (mean - xg[:, Hh:]).astype(h.dtype)
    y = jnp.einsum('bsgc,gcd->bsgd', d, w_pool).reshape(B, S, D) * scale
    return y, xcat[:, T - POOL_HIST:]


def conv_ffn(h, hist, w_up, conv_w, conv_b, w_down):
    S = h.shape[1]
    up = h @ w_up
    upad = jnp.concatenate([hist, up], axis=1)
    c = conv_b
    for k in range(CONV_WIDTH):
        c = c + conv_w[k] * upad[:, k:k + S]
    gate, val = c[..., :D_FF], c[..., D_FF:]
    out = (jax.nn.silu(gate) * val) @ w_down
    return out, upad[:, upad.shape[1] - (CONV_WIDTH - 1):]


def setup_inputs(seed: int = 0) -> dict:
    key = jax.random.key(seed)
    ks = jax.random.split(key, 21)
    f32 = jnp.float32

    def nrm(k, shape, scale=1.0):
        return jax.random.normal(k, shape, f32) * scale

    NM, NP = N_MLA_LAYERS, N_POOL_LAYERS
    return {
        "x_prompt": nrm(ks[0], (BATCH, SEQ, D_MODEL)),
        "x_sample": nrm(ks[1], (DEC_BATCH, DEC_SEQ, D_MODEL)),
        "cache_ckv": nrm(ks[2], (NM, DEC_BATCH, PAST_LEN, KV_LORA)),
        "cache_krope": nrm(ks[3], (NM, DEC_BATCH, PAST_LEN, D_ROPE)),
        "state_pool": nrm(ks[4], (NP, DEC_BATCH, POOL_HIST, D_MODEL)),
        "state_conv": nrm(ks[5], (DEPTH, DEC_BATCH, CONV_WIDTH - 1, 2 * D_FF)),
        "norm_g": 1.0 + nrm(ks[6], (DEPTH, 4, D_MODEL), 0.05),
        "mla_w_dq": nrm(ks[7], (NM, D_MODEL, Q_LORA), D_MODEL ** -0.5),
        "mla_q_norm": 1.0 + nrm(ks[8], (NM, Q_LORA), 0.05),
        "mla_w_uq": nrm(ks[9], (NM, Q_LORA, N_HEADS * (D_NOPE + D_ROPE)), Q_LORA ** -0.5),
        "mla_w_dkv": nrm(ks[10], (NM, D_MODEL, KV_LORA + D_ROPE), D_MODEL ** -0.5),
        "mla_kv_norm": 1.0 + nrm(ks[11], (NM, KV_LORA), 0.05),
        "mla_w_uk": nrm(ks[12], (NM, KV_LORA, N_HEADS, D_NOPE), KV_LORA ** -0.5),
        "mla_w_uv": nrm(ks[13], (NM, KV_LORA, N_HEADS, D_V), KV_LORA ** -0.5),
        "mla_w_o": nrm(ks[14], (NM, N_HEADS * D_V, D_MODEL), (N_HEADS * D_V) ** -0.5),
        "pool_w": nrm(ks[15], (NP, N_POOL_GROUPS, POOL_GROUP_DIM, POOL_GROUP_DIM), POOL_GROUP_DIM ** -0.5),
        "pool_scale": 1.0 + nrm(ks[16], (NP, D_MODEL), 0.1),
        "ffn_w_up": nrm(ks[17], (DEPTH, D_MODEL, 2 * D_FF), D_MODEL ** -0.5),
        "ffn_conv_w": nrm(ks[18], (DEPTH, CONV_WIDTH, 2 * D_FF), CONV_WIDTH ** -0.5),
        "ffn_conv_b": nrm(ks[19], (DEPTH, 2 * D_FF), 0.01),
        "ffn_w_down": nrm(ks[20], (DEPTH, D_FF, D_MODEL), D_FF ** -0.5),
    }


def reference(x_prompt, x_sample, cache_ckv, cache_krope, state_pool, state_conv, norm_g,
              mla_w_dq, mla_q_norm, mla_w_uq, mla_w_dkv, mla_kv_norm, mla_w_uk, mla_w_uv, mla_w_o,
              pool_w, pool_scale, ffn_w_up, ffn_conv_w, ffn_conv_b, ffn_w_down):
    xp, xs = x_prompt, x_sample
    B = xp.shape[0]
    ckv_p, kr_p, ckv_s, kr_s = [], [], [], []
    pool_p, pool_s, conv_p, conv_s = [], [], [], []
    for i in range(DEPTH):
        g = norm_g[i]
        hp = rms_norm(xp, g[0])
        hs = rms_norm(xs, g[0])
        j = i // 2
        if i % 2 == 0:
            w = (mla_w_dq[j], mla_q_norm[j], mla_w_uq[j], mla_w_dkv[j], mla_kv_norm[j],
                 mla_w_uk[j], mla_w_uv[j], mla_w_o[j])
            mp, c_p, r_p = mla_prompt(hp, *w)
            ms, c_s, r_s = mla_sample(hs, cache_ckv[j], cache_krope[j], *w)
            ckv_p.append(c_p); kr_p.append(r_p); ckv_s.append(c_s); kr_s.append(r_s)
        else:
            mp, st_p = pool_mix(hp, hp[:, :0], 0, pool_w[j], pool_scale[j])
            ms, st_s = pool_mix(hs, state_pool[j], PAST_LEN, pool_w[j], pool_scale[j])
            pool_p.append(st_p); pool_s.append(st_s)
        xp = xp + rms_norm(mp, g[1])
        xs = xs + rms_norm(ms, g[1])
        hp = rms_norm(xp, g[2])
        hs = rms_norm(xs, g[2])
        zero_hist = jnp.zeros((B, CONV_WIDTH - 1, 2 * D_FF), hp.dtype)
        fp, cv_p = conv_ffn(hp, zero_hist, ffn_w_up[i], ffn_conv_w[i], ffn_conv_b[i], ffn_w_down[i])
        fs, cv_s = conv_ffn(hs, state_conv[i], ffn_w_up[i], ffn_conv_w[i], ffn_conv_b[i], ffn_w_down[i])
        conv_p.append(cv_p); conv_s.append(cv_s)
        xp = xp + rms_norm(fp, g[3])
        xs = xs + rms_norm(fs, g[3])
    return (xp, xs,
            jnp.stack(ckv_p), jnp.stack(kr_p), jnp.stack(pool_p), jnp.stack(conv_p),
            jnp.stack(ckv_s), jnp.stack(kr_s), jnp.stack(pool_s), jnp.stack(conv_s))
```

```python
import numpy as np
import concourse.bass as bass
import concourse.mybir as mybir
from concourse.bass_utils import run_bass_kernel_spmd

F32 = mybir.dt.float32
BF16 = mybir.dt.bfloat16
AF = mybir.ActivationFunctionType
ALU = mybir.AluOpType

EPS = 1e-6
ROPE_BASE = 10000.0
NCORES = 8
SLOT_COLS = 4096
NSLOT = 6
CAST_COLS = 16384


class Cfg:
    def __init__(self, D=2048, SEQ=4096, NH=16, QL=512, KL=512, DFF=5632, DEPTH=4,
                 NBS=4, SS=16, PAST=2048, T=512):
        self.D, self.SEQ, self.NH, self.QL, self.KL, self.DFF, self.DEPTH = D, SEQ, NH, QL, KL, DFF, DEPTH
        self.NBS, self.SS, self.PAST, self.T = NBS, SS, PAST, T
        self.DC, self.QC, self.KC, self.FP = D // 128, QL // 128, KL // 128, DFF // 128
        self.NT = SEQ // T
        self.NM, self.NP = (DEPTH + 1) // 2, DEPTH // 2
        self.NS = NBS * SS
        self.PKC = PAST // 128
        self.GC = self.DC // 4
        self.scale = float((128 + 64) ** -0.5)
        assert self.DC % 4 == 0 and SEQ % T == 0 and T == 512


def _blocks_piece(W, K, blocks):
    W3 = W.reshape(K, 128, W.shape[1])
    outs = [np.ascontiguousarray(W3[:, :, c0:c0 + M].transpose(1, 0, 2)).reshape(128, K * M) for c0, M in blocks]
    return np.concatenate(outs, axis=1)


def weight_plan(cfg):
    P = []
    D, DC, NH, QC, KC, FP = cfg.D, cfg.DC, cfg.NH, cfg.QC, cfg.KC, cfg.FP

    cur = [0]

    def add(name, ncols, fn):
        assert ncols <= SLOT_COLS, (name, ncols)
        P.append(dict(name=name, ncols=ncols, build=fn, layer=cur[0]))

    for i in range(cfg.DEPTH):
        cur[0] = i
        j = i // 2
        if i % 2 == 0:
            g = max(1, SLOT_COLS // (DC * 128))
            for o0 in range(0, QC, g):
                oc = list(range(o0, min(QC, o0 + g)))
                add(f"dq{i}_{o0}", len(oc) * DC * 128,
                    lambda inp, j=j, oc=oc: _blocks_piece(inp["mla_w_dq"][j], DC, [(o * 128, 128) for o in oc]))
            g = max(1, SLOT_COLS // (QC * 128))
            for h0 in range(0, NH, g):
                hs = list(range(h0, min(NH, h0 + g)))
                add(f"uqn{i}_{h0}", len(hs) * QC * 128,
                    lambda inp, j=j, hs=hs: _blocks_piece(inp["mla_w_uq"][j], QC, [(h * 192, 128) for h in hs]))
            for h0 in range(0, NH, g):
                hs = list(range(h0, min(NH, h0 + g)))

                def f(inp, j=j, hs=hs):
                    W = inp["mla_w_uq"][j]
                    bl = []
                    for h in hs:
                        r = W[:, h * 192 + 128:h * 192 + 192]
                        sw = np.concatenate([r[:, 32:], r[:, :32]], axis=1)
                        bl.append(_blocks_piece(np.concatenate([r, sw], axis=1), QC, [(0, 64), (64, 64)]))
                    return np.concatenate(bl, axis=1)
                add(f"uqr{i}_{h0}", len(hs) * QC * 128, f)
            g = max(1, SLOT_COLS // (DC * 128))
            for o0 in range(0, KC, g):
                oc = list(range(o0, min(KC, o0 + g)))
                add(f"dkv{i}_{o0}", len(oc) * DC * 128,
                    lambda inp, j=j, oc=oc: _blocks_piece(inp["mla_w_dkv"][j], DC, [(o * 128, 128) for o in oc]))

            def fr(inp, j=j):
                r = inp["mla_w_dkv"][j][:, cfg.KL:cfg.KL + 64]
                sw = np.concatenate([r[:, 32:], r[:, :32]], axis=1)
                return _blocks_piece(np.concatenate([r, sw], axis=1), DC, [(0, 64), (64, 64)])
            add(f"dkvr{i}", DC * 128, fr)
            g = max(1, SLOT_COLS // (KC * 128))
            for h0 in range(0, NH, g):
                hs = list(range(h0, min(NH, h0 + g)))
                add(f"uk{i}_{h0}", len(hs) * KC * 128,
                    lambda inp, j=j, hs=hs: _blocks_piece(inp["mla_w_uk"][j].reshape(cfg.KL, NH * 128), KC, [(h * 128, 128) for h in hs]))
            g2 = max(1, SLOT_COLS // cfg.KL)
            for h0 in range(0, NH, g2):
                hs = list(range(h0, min(NH, h0 + g2)))
                add(f"ukT{i}_{h0}", len(hs) * cfg.KL,
                    lambda inp, j=j, hs=hs: np.concatenate([np.ascontiguousarray(inp["mla_w_uk"][j][:, h, :].T) for h in hs], axis=1))
            for h0 in range(0, NH, g):
                hs = list(range(h0, min(NH, h0 + g)))

                def fv(inp, j=j, hs=hs):
                    W = inp["mla_w_uv"][j].reshape(KC, 128, NH, 128)[:, :, hs, :]
                    return np.ascontiguousarray(W.transpose(1, 0, 2, 3)).reshape(128, KC * len(hs) * 128)
                add(f"uv{i}_{h0}", len(hs) * KC * 128, fv)
            g = max(1, SLOT_COLS // (NH * 128))
            for o0 in range(0, DC, g):
                oc = list(range(o0, min(DC, o0 + g)))
                add(f"wo{i}_{o0}", len(oc) * NH * 128,
                    lambda inp, j=j, oc=oc: _blocks_piece(inp["mla_w_o"][j], NH, [(o * 128, 128) for o in oc]))
        else:
            GC = cfg.GC
            g = max(1, SLOT_COLS // (GC * GC * 128))
            for g0 in range(0, 4, g):
                gs = list(range(g0, min(4, g0 + g)))
                add(f"pool{i}_{g0}", len(gs) * GC * GC * 128,
                    lambda inp, j=j, gs=gs: np.concatenate(
                        [_blocks_piece(inp["pool_w"][j][gg], GC, [(o * 128, 128) for o in range(GC)]) for gg in gs], axis=1))
        g = max(1, SLOT_COLS // (DC * 256))
        for p0 in range(0, FP, g):
            ps = list(range(p0, min(FP, p0 + g)))

            def fu(inp, i=i, ps=ps):
                W = inp["ffn_w_up"][i]
                bl = []
                for p in ps:
                    bl += [(p * 128, 128), (cfg.DFF + p * 128, 128)]
                return _blocks_piece(W, DC, bl)
            add(f"up{i}_{p0}", len(ps) * DC * 256, fu)
        nparts = -(-FP * 128 // SLOT_COLS)
        kper = -(-FP // nparts)
        for o in range(DC):
            for part in range(nparts):
                k0, k1 = part * kper, min(FP, (part + 1) * kper)

                def fd(inp, i=i, o=o, k0=k0, k1=k1):
                    W = inp["ffn_w_down"][i][k0 * 128:k1 * 128]
                    return _blocks_piece(W, k1 - k0, [(o * 128, 128)])
                add(f"dn{i}_{o}_{part}", (k1 - k0) * 128, fd)
    tot = [0] * cfg.DEPTH
    for p in P:
        p["off"] = tot[p["layer"]]
        tot[p["layer"]] += p["ncols"]
    return P, tot


def pcol_layout(cfg):
    L = {}
    n = 0
    for nm, cnt in (("g", cfg.DEPTH * 4 * cfg.DC), ("qn", cfg.NM * cfg.QC), ("kn", cfg.NM * cfg.KC),
                    ("psc", max(1, cfg.NP) * cfg.DC), ("cw", cfg.DEPTH * 3 * 2 * cfg.FP), ("cb", cfg.DEPTH * 2 * cfg.FP)):
        L[nm] = n
        n += cnt
    return L, n


class Res:
    __slots__ = ("w", "r")

    def __init__(self):
        self.w = None
        self.r = {}


def RL(n):
    return [Res() for _ in range(n)]


class DSem:
    def __init__(self, h, scoped=True):
        self.h = h
        self.val = 0
        self.scoped = scoped


ENGS = ("pe", "act", "dve", "pool", "sp")


class Planner:
    def __init__(self):
        self.ops = {e: [] for e in ENGS}
        self.waited = {e: {} for e in ENGS}
        self.last_c = {e: -1 for e in ENGS}
        self.dsems = []

    def _need(self, eng, tok, waits):
        if tok is None:
            return
        if tok[0] == "E":
            _, e2, idx = tok
            if e2 == eng and eng == "pe":
                return
            key = ("E", e2)
            if self.waited[eng].get(key, -1) >= idx:
                return
            self.waited[eng][key] = idx
            self.ops[e2][idx][2] = True
            waits.append(tok)
        else:
            _, sem, val = tok
            key = ("D", id(sem))
            if self.waited[eng].get(key, 0) >= val:
                return
            self.waited[eng][key] = val
            waits.append(tok)

    def _deps(self, eng, reads, writes):
        waits = []
        for r in reads:
            self._need(eng, r.w, waits)
        for w in writes:
            self._need(eng, w.w, waits)
            for t in w.r.values():
                self._need(eng, t, waits)
        return waits

    def op(self, eng, ins, reads=(), writes=()):
        waits = self._deps(eng, reads, writes)
        idx = len(self.ops[eng])
        self.ops[eng].append([ins, waits, False, None, 0])
        self.last_c[eng] = idx
        tok = ("E", eng, idx)
        key = ("E", eng)
        for r in reads:
            r.r[key] = tok
        for w in writes:
            w.w = tok
            w.r = {}
        return tok

    def dma(self, q, ins, reads, writes, sem):
        waits = self._deps(q, reads, writes)
        if sem.val > 0:
            self._need(q, ("D", sem, sem.val), waits)
        sem.val += 16
        self.ops[q].append([ins, waits, False, sem, 0])
        tok = ("D", sem, sem.val)
        key = ("D", id(sem))
        for r in reads:
            r.r[key] = tok
        for w in writes:
            w.w = tok
            w.r = {}
        return tok

    def barrier(self, final=False):
        for f in ENGS:
            waits = []
            for e in ENGS:
                if e == f and e == "pe":
                    continue
                if self.last_c[e] >= 0:
                    self._need(f, ("E", e, self.last_c[e]), waits)
            for sem in self.dsems:
                if sem.val and (final or sem.scoped):
                    self._need(f, ("D", sem, sem.val), waits)
            if waits:
                self.ops[f].append([None, waits, False, None, 0])

    def finish(self):
        for e in ENGS:
            c = 0
            for rec in self.ops[e]:
                if rec[2]:
                    c += 1
                rec[4] = c

    def replay(self, name, e, esem):
        ops = self.ops
        for ins, waits, signal, dsem, _ in ops[name]:
            for tok in waits:
                if tok[0] == "E":
                    e.wait_ge(esem[tok[1]], ops[tok[1]][tok[2]][4])
                else:
                    e.wait_ge(tok[1].h, tok[2])
            if ins is not None:
                m, a, k = ins
                r = getattr(e, m)(*a, **k)
                if dsem is not None:
                    r.then_inc(dsem.h, 16)
                elif signal:
                    r.then_inc(esem[name], 1)


def I(m, *a, **k):
    return (m, a, k)


class Arena:
    def __init__(self, ap, words):
        self.ap, self.words, self.top = ap, words, 0
        self.recs = []
        self.last = None

    def res(self, n):
        lo, hi, seed = self.last
        out = [Res() for _ in range(n)]
        for r in out:
            r.r = dict(seed)
        self.recs.append((lo, hi, out))
        return out

    def alloc(self, free_shape, dt):
        n = int(np.prod(free_shape))
        w = n if dt == F32 else (n + 1) // 2
        w = (w + 7) // 8 * 8
        assert self.top + w <= self.words, ("SBUF arena overflow", self.top, w, self.words)
        lo, hi = self.top, self.top + w
        seed, keep = {}, []
        for (l, h, rl) in self.recs:
            if h <= lo or l >= hi:
                keep.append((l, h, rl))
                continue
            for r in rl:
                for t in ([r.w] if r.w is not None else []) + list(r.r.values()):
                    key = ("E", t[1]) if t[0] == "E" else ("D", id(t[1]))
                    o = seed.get(key)
                    if o is None or t[2] > o[2]:
                        seed[key] = t
            if not (lo <= l and h <= hi):
                keep.append((l, h, rl))
        self.recs = keep
        self.last = (lo, hi, seed)
        v = self.ap[:, self.top:self.top + w]
        self.top += w
        if dt != F32:
            v = v.bitcast(dt)
        v = v[:, 0:n]
        if len(free_shape) == 2:
            v = v.rearrange("p (a b) -> p a b", a=free_shape[0])
        elif len(free_shape) == 3:
            v = v.rearrange("p (a b c) -> p a b c", a=free_shape[0], b=free_shape[1])
        elif len(free_shape) == 4:
            v = v.rearrange("p (a b c d) -> p a b c d", a=free_shape[0], b=free_shape[1], c=free_shape[2])
        return v


def build_program(cfg):
    nc = bass.Bass("TRN2", target_bir_lowering=False)
    pieces, TOT = weight_plan(cfg)
    pidx = {p["name"]: k for k, p in enumerate(pieces)}
    PL, NPC = pcol_layout(cfg)
    D, DC, NH, QC, KC, FP, T, NS, NBS, SS = cfg.D, cfg.DC, cfg.NH, cfg.QC, cfg.KC, cfg.FP, cfg.T, cfg.NS, cfg.NBS, cfg.SS
    SEQ, NT, NM, NP, DEPTH = cfg.SEQ, cfg.NT, cfg.NM, cfg.NP, cfg.DEPTH
    NKC = SEQ // 128
    PKC = cfg.PKC

    def din(name, shape, dt=F32):
        return nc.dram_tensor(name, list(shape), dt, kind="ExternalInput").ap()

    def dout(name, shape):
        return nc.dram_tensor(name, list(shape), F32, kind="ExternalOutput").ap()

    def dscr(name, shape, dt=BF16):
        return nc.dram_tensor(name, list(shape), dt, kind="Internal").ap()

    wblob = [din(f"wblob{i}", [128, TOT[i]]) for i in range(DEPTH)]
    xT_in = din("xT", [D, SEQ])
    xsT_in = din("xsT", [D, NS])
    pcol_in = din("pcol", [128, NPC])
    cst_in = din("cst", [128, 256])
    rope_in = din("rope", [64, 2, SEQ + NS])
    cck_in = din("cck", [max(NM, 1), NBS, cfg.PAST, cfg.KL])
    cckT_in = din("cckT", [max(NM, 1), NBS, cfg.KL, cfg.PAST])
    ckrT_in = din("ckrT", [max(NM, 1), NBS, 64, cfg.PAST])
    spool_in = din("spool", [max(NP, 1), 128, DC, NBS, 15])
    sconv_in = din("sconv", [DEPTH, 128, 2 * FP, NBS, 2])

    yT = dout("yT", [D, SEQ])
    ysT = dout("ysT", [D, NS])
    ockvT = dout("ockvT", [max(NM, 1), cfg.KL, SEQ])
    okrT = dout("okrT", [max(NM, 1), 64, SEQ])
    opool = dout("opool", [max(NP, 1), 128, DC, 15])
    oconv = dout("oconv", [DEPTH, 128, 2 * FP, 2])
    ockvTs = dout("ockvTs", [max(NM, 1), cfg.KL, NS])
    okrTs = dout("okrTs", [max(NM, 1), 64, NS])
    opools = dout("opools", [max(NP, 1), 128, DC, NBS, 15])
    oconvs = dout("oconvs", [DEPTH, 128, 2 * FP, NBS, 2])

    wbf = [dscr(f"wbf{i}", [128, TOT[i]]) for i in range(DEPTH)]
    kT_d = dscr("kT_d", [max(NM, 1), NH, 128, SEQ])
    v_d = dscr("v_d", [max(NM, 1), NH, 128, NKC, 128])
    krT_d = dscr("krT_d", [max(NM, 1), 64, SEQ])

    P = Planner()
    ARENA_W = 52480
    es = None

    import contextlib
    with contextlib.ExitStack() as st:
        arena_t = st.enter_context(nc.sbuf_tensor("arena", [128, ARENA_W], F32))
        psb = [st.enter_context(nc.psum_tensor(f"ps{b}", [128, 512], F32)) for b in range(8)]
        esem = {e: st.enter_context(nc.semaphore(f"s_{e}")) for e in ENGS}

        def newsem(name, scoped=True):
            s = DSem(st.enter_context(nc.semaphore(name)), scoped)
            P.dsems.append(s)
            return s

        A = Arena(arena_t, ARENA_W)
        PS = [psb[b][:] for b in range(8)]
        PSR = RL(8)

        xT = A.alloc((DC, T), F32)
        xTR = RL(DC)
        pcol = A.alloc((NPC,), F32)
        pcolR = Res()
        cst = A.alloc((256,), F32)
        cstR = Res()
        ident = A.alloc((128,), BF16)
        onesD = A.alloc((128,), BF16)
        onesQ = A.alloc((128,), BF16)
        onesK = A.alloc((128,), BF16)
        ones1 = A.alloc((128,), BF16)
        constR = Res()
        ring = A.alloc((NSLOT, SLOT_COLS), BF16)
        ringR = RL(NSLOT)
        ringS = [newsem(f"ring{k}", False) for k in range(NSLOT)]
        chist = A.alloc((DEPTH, 2 * FP, 2), F32)
        chistR = [RL(2 * FP) for _ in range(DEPTH)]
        phist = A.alloc((max(NP, 1), DC, 15), F32)
        phistR = [RL(DC) for _ in range(max(NP, 1))]
        rstd = A.alloc((2, T), F32)
        rstdR = RL(2)
        sqb = A.alloc((3, T), BF16)
        sqbR = RL(3)
        pgc = A.alloc((max(NP, 1) * DC,), F32)
        persist_top = A.top
        EW2 = "dve"

        s_in = newsem("s_in")
        s_xy = [newsem(f"xy{c}", False) for c in range(DC)]
        s_cast = [newsem(f"cast{k}", False) for k in range(8)]
        s_kst, s_vst, s_krst = [newsem("kst0"), newsem("kst1")], newsem("vst"), newsem("krst")
        s_kl = [newsem("kl0"), newsem("kl1")]
        s_vl = [newsem("vl0"), newsem("vl1")]
        s_krl = newsem("krl")
        s_o = [newsem(f"o{k}") for k in range(4)]
        s_m = [newsem(f"m{k}") for k in range(4)]
        so_i = [0]
        sm_i = [0]

        def osem():
            so_i[0] += 1
            return s_o[so_i[0] % 4]

        def msem():
            sm_i[0] += 1
            return s_m[sm_i[0] % 4]

        pieceR = {}

        P.dma("sp", I("dma_start", out=pcol, in_=pcol_in[:, :]), [], [pcolR], s_in)
        P.dma("sp", I("dma_start", out=cst, in_=cst_in[:, :]), [], [cstR], s_in)
        P.op("dve", I("tensor_copy", out=ident, in_=cst[:, 0:128]), [cstR], [constR])
        P.op("dve", I("memset", onesD, 1.0 / D), [], [constR])
        P.op("dve", I("memset", onesQ, 1.0 / cfg.QL), [], [constR])
        P.op("dve", I("memset", onesK, 1.0 / cfg.KL), [], [constR])
        P.op("dve", I("memset", ones1, 1.0), [], [constR])
        for jj in range(NP):
            ii = 2 * jj + 1
            g0 = PL["g"] + (ii * 4 + 1) * DC
            P.op("dve", I("tensor_tensor", out=pgc[:, jj * DC:(jj + 1) * DC], in0=pcol[:, PL["psc"] + jj * DC:PL["psc"] + (jj + 1) * DC], in1=pcol[:, g0:g0 + DC], op=ALU.mult),
                 [pcolR], [pcolR])
        P.op("dve", I("memset", chist, 0.0), [], [r for l in chistR for r in l])
        P.op("dve", I("memset", phist, 0.0), [], [r for l in phistR for r in l])

        def gcol(i, n, c):
            k = PL["g"] + (i * 4 + n) * DC + c
            return pcol[:, k:k + 1]

        class WS:
            order = []
            issued = 0
            consumed = 0
            wb = 0
            seen = set()

        def ws_issue():
            k = WS.issued
            p = pieces[pidx[WS.order[k]]]
            s = k % NSLOT
            c0, c1 = p["off"], p["off"] + p["ncols"]
            li = p["layer"]
            nm = p["name"]
            if nm in pieceR:
                P.dma("sp", I("dma_start", out=ring[:, s, 0:p["ncols"]], in_=wbf[li][:, c0:c1]), [pieceR[nm]], [ringR[s]], ringS[s])
            else:
                P.dma("pool", I("dma_start", out=ring[:, s, 0:p["ncols"]], in_=wblob[li][:, c0:c1]), [], [ringR[s]], ringS[s])
                first = nm not in WS.seen
                WS.seen.add(nm)
                if nm in WS.order[k + 1:] and (not first or k % 2 == 0):
                    pieceR[nm] = Res()
                    P.dma("sp", I("dma_start", out=wbf[li][:, c0:c1], in_=ring[:, s, 0:p["ncols"]]), [ringR[s]], [pieceR[nm]], s_cast[WS.wb % 8])
                    WS.wb += 1
            WS.issued += 1

        def ws_next(name):
            k = WS.consumed
            assert WS.order[k] == name, (k, WS.order[k], name)
            while WS.issued < min(len(WS.order), k + NSLOT):
                ws_issue()
            WS.consumed += 1
            s = k % NSLOT
            return ring[:, s, :], ringR[s]

        def tile_order(sample):
            o = []
            for i in range(DEPTH):
                if i % 2 == 0:
                    pre = (f"dq{i}_", f"uqn{i}_", f"uqr{i}_", f"dkv{i}_", f"dkvr{i}")
                    for nm in pre:
                        o += [p["name"] for p in pieces if p["name"].startswith(nm)]
                    if sample:
                        o += [p["name"] for p in pieces if p["name"].startswith(f"ukT{i}_")]
                    else:
                        o += [p["name"] for p in pieces if p["name"].startswith(f"uk{i}_")]
                    o += [p["name"] for p in pieces if p["name"].startswith(f"uv{i}_")]
                    o += [p["name"] for p in pieces if p["name"].startswith(f"wo{i}_")]
                else:
                    o += [p["name"] for p in pieces if p["name"].startswith(f"pool{i}_")]
                o += [p["name"] for p in pieces if p["name"].startswith(f"up{i}_")]
                o += [p["name"] for p in pieces if p["name"].startswith(f"dn{i}_")]
            return o

        for t in range(NT):
            WS.order += tile_order(False)
        WS.order += tile_order(True)

        bank_i = [0]

        def nbank():
            b = bank_i[0] % 6
            bank_i[0] += 1
            return b

        sq_i = [0]

        def stats_begin():
            return dict(n=0, pend=None)

        def stats_add(S, src, srcR, N, ones, total, scale=None):
            k = sq_i[0] % 3
            sq_i[0] += 1
            kw = {}
            if scale is not None:
                kw["scale"] = scale
            P.op("act", I("activation", out=sqb[:, k, 0:N], in_=src, func=AF.Square, **kw), [srcR, pcolR], [sqbR[k]])
            stats_flush(S, N, ones, total)
            S["pend"] = k

        def stats_flush(S, N, ones, total):
            if S["pend"] is None:
                return
            k = S["pend"]
            n = S["n"]
            P.op("pe", I("matmul", PS[7][:, 0:N], ones, sqb[:, k, 0:N], start=(n == 0), stop=(n == total - 1)),
                 [sqbR[k], constR], [PSR[7]])
            S["n"] += 1
            S["pend"] = None

        rs_i = [0]

        def stats_end(S, N, ones, total):
            stats_flush(S, N, ones, total)
            assert S["n"] == total
            k = rs_i[0] % 2
            rs_i[0] += 1
            P.op("act", I("activation", out=rstd[:, k, 0:N], in_=PS[7][:, 0:N], func=AF.Sqrt, bias=EPS, scale=1.0), [PSR[7]], [rstdR[k]])
            P.op("dve", I("reciprocal", out=rstd[:, k, 0:N], in_=rstd[:, k, 0:N]), [rstdR[k]], [rstdR[k]])
            return rstd[:, k, 0:N], rstdR[k]

        def prenorm(i, n, N, out_fn):
            S = stats_begin()
            for c in range(DC):
                stats_add(S, xT[:, c, 0:N], xTR[c], N, onesD, DC)
            r, rR = stats_end(S, N, onesD, DC)
            for c in range(DC):
                o, oR = out_fn(c)
                P.op("dve", I("scalar_tensor_tensor", out=o, in0=xT[:, c, 0:N], scalar=gcol(i, n, c), in1=r, op0=ALU.mult, op1=ALU.mult),
                     [xTR[c], rR, pcolR], [oR])

        def post_residual(i, n, N, m, mR, S):
            r, rR = stats_end(S, N, onesD, DC)
            for c in range(DC):
                P.op(EW2, I("tensor_tensor", out=m[:, c, 0:N], in0=m[:, c, 0:N], in1=r, op=ALU.mult), [mR[c], rR], [mR[c]])
                P.op("dve", I("tensor_tensor", out=xT[:, c, 0:N], in0=xT[:, c, 0:N], in1=m[:, c, 0:N], op=ALU.add), [xTR[c], mR[c]], [xTR[c]])

        def evac_m(b, c, N, m, mR, S, gc, scale=None):
            P.op("act", I("activation", out=m[:, c, 0:N], in_=PS[b][:, 0:N], func=AF.Identity, scale=gc, bias=0.0), [PSR[b], pcolR], [mR[c]])
            stats_add(S, PS[b][:, 0:N], PSR[b], N, onesD, DC, scale=scale)

        def phase_end(mark):
            A.top = mark

        def ffn(i, N, nb, S_, hist, histR, last_out):
            mark0 = A.top
            actT = A.alloc((FP, N), BF16)
            actR = A.res(FP)
            mark1 = A.top
            hT = A.alloc((DC, N), BF16)
            hR = A.res(DC)
            ag = A.alloc((2, N), F32)
            agR = A.res(2)
            av = A.alloc((2, N), F32)
            avR = A.res(2)
            sg = A.alloc((2, N), F32)
            sgR = A.res(2)
            prenorm(i, 2, N, lambda c: (hT[:, c, :], hR[c]))
            cwb = PL["cw"] + i * 3 * 2 * FP
            cbb = PL["cb"] + i * 2 * FP

            def v3(ap):
                return ap.rearrange("p (b s) -> p b s", b=nb)

            g = max(1, SLOT_COLS // (DC * 256))
            pend = None

            def conv_pair(p, bg, bv, q):
                for (ch, b, a, aR) in ((p, bg, ag, agR), (FP + p, bv, av, avR)):
                    w0 = pcol[:, cwb + ch:cwb + ch + 1]
                    w1 = pcol[:, cwb + 2 * FP + ch:cwb + 2 * FP + ch + 1]
                    w2 = pcol[:, cwb + 4 * FP + ch:cwb + 4 * FP + ch + 1]
                    bb = pcol[:, cbb + ch:cbb + ch + 1]
                    P.op("act", I("activation", out=a[:, q, :], in_=PS[b][:, 0:N], func=AF.Identity, scale=w2, bias=bb), [PSR[b], pcolR], [aR[q]])
                for (ch, b, a, aR) in ((p, bg, ag, agR), (FP + p, bv, av, avR)):
                    w1 = pcol[:, cwb + 2 * FP + ch:cwb + 2 * FP + ch + 1]
                    a3, u3 = v3(a[:, q, :]), v3(PS[b][:, 0:N])
                    P.op("dve", I("scalar_tensor_tensor", out=a3[:, :, 1:S_], in0=u3[:, :, 0:S_ - 1], scalar=w1, in1=a3[:, :, 1:S_], op0=ALU.mult, op1=ALU.add),
                         [PSR[b], aR[q], pcolR], [aR[q]])
                for (ch, b, a, aR) in ((p, bg, ag, agR), (FP + p, bv, av, avR)):
                    w0 = pcol[:, cwb + ch:cwb + ch + 1]
                    a3, u3 = v3(a[:, q, :]), v3(PS[b][:, 0:N])
                    P.op("dve", I("scalar_tensor_tensor", out=a3[:, :, 2:S_], in0=u3[:, :, 0:S_ - 2], scalar=w0, in1=a3[:, :, 2:S_], op0=ALU.mult, op1=ALU.add),
                         [PSR[b], aR[q], pcolR], [aR[q]])
                for (ch, b, a, aR) in ((p, bg, ag, agR), (FP + p, bv, av, avR)):
                    w0 = pcol[:, cwb + ch:cwb + ch + 1]
                    a3 = v3(a[:, q, :])
                    P.op("dve", I("scalar_tensor_tensor", out=a3[:, :, 0:2], in0=hist[:, ch, :, 0:2], scalar=w0, in1=a3[:, :, 0:2], op0=ALU.mult, op1=ALU.add),
                         [histR[ch], aR[q], pcolR], [aR[q]])
                for (ch, b, a, aR) in ((p, bg, ag, agR), (FP + p, bv, av, avR)):
                    w1 = pcol[:, cwb + 2 * FP + ch:cwb + 2 * FP + ch + 1]
                    a3 = v3(a[:, q, :])
                    P.op("dve", I("scalar_tensor_tensor", out=a3[:, :, 0:1], in0=hist[:, ch, :, 1:2], scalar=w1, in1=a3[:, :, 0:1], op0=ALU.mult, op1=ALU.add),
                         [histR[ch], aR[q], pcolR], [aR[q]])
                for (ch, b, a, aR) in ((p, bg, ag, agR), (FP + p, bv, av, avR)):
                    u3 = v3(PS[b][:, 0:N])
                    P.op("dve", I("tensor_copy", out=hist[:, ch, :, :], in_=u3[:, :, S_ - 2:S_]), [PSR[b]], [histR[ch]])
                P.op("act", I("activation", out=sg[:, q, :], in_=ag[:, q, :], func=AF.Silu), [agR[q]], [sgR[q]])
                P.op("dve", I("tensor_tensor", out=actT[:, p, :], in0=sg[:, q, :], in1=av[:, q, :], op=ALU.mult), [sgR[q], avR[q]], [actR[p]])

            pi = 0
            for p0 in range(0, FP, g):
                slot, sR = ws_next(f"up{i}_{p0}")
                for k, p in enumerate(range(p0, min(FP, p0 + g))):
                    bg, bv = 2 * (pi % 3), 2 * (pi % 3) + 1
                    base = k * DC * 256
                    for (b, bo) in ((bg, base), (bv, base + DC * 128)):
                        for kc in range(DC):
                            P.op("pe", I("matmul", PS[b][:, 0:N], slot[:, bo + kc * 128:bo + (kc + 1) * 128], hT[:, kc, :], start=(kc == 0), stop=(kc == DC - 1)),
                                 [sR, hR[kc]], [PSR[b]])
                    if pend is not None:
                        conv_pair(*pend)
                    pend = (p, bg, bv, pi % 2)
                    pi += 1
            conv_pair(*pend)
            if last_out is not None:
                P.dma("sp", I("dma_start", out=last_out, in_=hist), list(histR), [Res()], msem())
            A.top = mark1
            m = A.alloc((DC, N), F32)
            mR = A.res(DC)
            S = stats_begin()
            nparts = -(-FP * 128 // SLOT_COLS)
            kper = -(-FP // nparts)
            pend = None
            for o in range(DC):
                b = nbank()
                for part in range(nparts):
                    slot, sR = ws_next(f"dn{i}_{o}_{part}")
                    k0, k1 = part * kper, min(FP, (part + 1) * kper)
                    for kc in range(k0, k1):
                        P.op("pe", I("matmul", PS[b][:, 0:N], slot[:, (kc - k0) * 128:(kc - k0 + 1) * 128], actT[:, kc, :], start=(kc == 0), stop=(kc == FP - 1)),
                             [sR, actR[kc]], [PSR[b]])
                if pend is not None:
                    evac_m(pend[0], pend[1], N, m, mR, S, gcol(i, 3, pend[1]))
                pend = (b, o)
            evac_m(pend[0], pend[1], N, m, mR, S, gcol(i, 3, pend[1]))
            post_residual(i, 3, N, m, mR, S)
            phase_end(mark0)

        def poolmix(i, N, nb, S_, hist, histR, first, last_out):
            j = i // 2
            mark0 = A.top
            L = 15 + S_
            hx = A.alloc((DC, nb, L), F32)
            hxR = A.res(DC)
            dT = A.alloc((DC, N), BF16)
            dR = A.res(DC)
            tA = A.alloc((2, nb, L), F32)
            tAR = A.res(2)
            tB = A.alloc((2, nb, L), F32)
            tBR = A.res(2)
            m = A.alloc((DC, N), F32)
            mR = A.res(DC)
            for c in range(DC):
                P.op("act", I("activation", out=hx[:, c, :, 0:15], in_=hist[:, c, :, :], func=AF.Copy), [histR[c]], [hxR[c]])
            if nb == 1:
                prenorm(i, 0, N, lambda c: (hx[:, c, 0, 15:L], hxR[c]))
            else:
                S = stats_begin()
                for c in range(DC):
                    stats_add(S, xT[:, c, 0:N], xTR[c], N, onesD, DC)
                r, rR = stats_end(S, N, onesD, DC)
                for c in range(DC):
                    P.op("dve", I("scalar_tensor_tensor", out=hx[:, c, :, 15:L], in0=xT[:, c, 0:N].rearrange("p (b s) -> p b s", b=nb), scalar=gcol(i, 0, c),
                                  in1=r.rearrange("p (b s) -> p b s", b=nb), op0=ALU.mult, op1=ALU.mult), [xTR[c], rR, pcolR], [hxR[c]])
            for c in range(DC):
                gi = c // cfg.GC
                w = 2 ** (gi + 1)
                q = c % 2
                src, srcR = hx[:, c], hxR[c]
                bufs = [(tA[:, q], tAR[q]), (tB[:, q], tBR[q])]
                sh = 1
                lv = 0
                while sh < w:
                    dst, dstR = bufs[lv % 2]
                    P.op("dve", I("tensor_tensor", out=dst[:, :, sh:L], in0=src[:, :, sh:L], in1=src[:, :, 0:L - sh], op=ALU.add), [srcR], [dstR])
                    if sh > 1:
                        pass
                    src, srcR = dst, dstR
                    sh *= 2
                    lv += 1
                d3 = dT[:, c, :].rearrange("p (b s) -> p b s", b=nb)
                P.op("dve", I("scalar_tensor_tensor", out=d3, in0=src[:, :, 15:L], scalar=1.0 / w, in1=hx[:, c, :, 15:L], op0=ALU.mult, op1=ALU.subtract),
                     [srcR, hxR[c]], [dR[c]])
                if first:
                    ic = cst[:, 128 + gi * 16:128 + gi * 16 + 15]
                    tmp, tmpR = bufs[lv % 2]
                    P.op("dve", I("tensor_tensor", out=tmp[:, 0, 0:15], in0=src[:, 0, 15:30], in1=ic, op=ALU.mult), [srcR, cstR], [tmpR])
                    P.op("dve", I("tensor_tensor", out=dT[:, c, 0:15], in0=tmp[:, 0, 0:15], in1=hx[:, c, 0, 15:30], op=ALU.subtract), [tmpR, hxR[c]], [dR[c]])
                P.op("act", I("activation", out=hist[:, c, :, :], in_=hx[:, c, :, S_:S_ + 15], func=AF.Copy), [hxR[c]], [histR[c]])
            if last_out is not None:
                P.dma("sp", I("dma_start", out=last_out, in_=hist), list(histR), [Res()], msem())
            GC = cfg.GC
            g = max(1, SLOT_COLS // (GC * GC * 128))
            S = stats_begin()
            pend = None
            for g0 in range(0, 4, g):
                slot, sR = ws_next(f"pool{i}_{g0}")
                for k, gg in enumerate(range(g0, min(4, g0 + g))):
                    for o in range(GC):
                        b = nbank()
                        base = k * GC * GC * 128 + o * GC * 128
                        for kc in range(GC):
                            P.op("pe", I("matmul", PS[b][:, 0:N], slot[:, base + kc * 128:base + (kc + 1) * 128], dT[:, gg * GC + kc, :], start=(kc == 0), stop=(kc == GC - 1)),
                                 [sR, dR[gg * GC + kc]], [PSR[b]])
                        c = gg * GC + o
                        if pend is not None:
                            evac_m(pend[0], pend[1], N, m, mR, S, pgc[:, j * DC + pend[1]:j * DC + pend[1] + 1], scale=pend[2])
                        kk = PL["psc"] + j * DC + c
                        pend = (b, c, pcol[:, kk:kk + 1])
            evac_m(pend[0], pend[1], N, m, mR, S, pgc[:, j * DC + pend[1]:j * DC + pend[1] + 1], scale=pend[2])
            post_residual(i, 1, N, m, mR, S)
            phase_end(mark0)

        def mla_proj(i, N, pos0, hT, hR, Qn, QnR, Qr, QrR, ckvb, ckvbR, krb, krbR, out_ckv, out_kr):
            j = i // 2
            cq = A.alloc((QC, N), F32)
            cqR = A.res(QC)
            cqn = A.alloc((QC, N), BF16)
            cqnR = A.res(QC)
            ckv = A.alloc((KC, N), F32)
            ckvR = A.res(KC)
            krf = A.alloc((N,), F32)
            krfR = A.res(1)[0]
            rtmp = A.alloc((2, N), F32)
            rtmpR = A.res(2)
            cs = A.alloc((2, N), F32)
            csR = A.res(1)[0]
            P.dma("sp", I("dma_start", out=cs[0:64, :, :], in_=rope_in[:, :, pos0:pos0 + N]), [], [csR], s_in)
            prenorm(i, 0, N, lambda c: (hT[:, c, :], hR[c]))

            def lin_group(prefix, nout, K, rhs, rhsR, evac):
                g = max(1, SLOT_COLS // (K * 128))
                pend = None
                for o0 in range(0, nout, g):
                    slot, sR = ws_next(f"{prefix}{i}_{o0}")
                    for k, o in enumerate(range(o0, min(nout, o0 + g))):
                        b = nbank()
                        for kc in range(K):
                            P.op("pe", I("matmul", PS[b][:, 0:N], slot[:, (k * K + kc) * 128:(k * K + kc + 1) * 128], rhs[:, kc, :], start=(kc == 0), stop=(kc == K - 1)),
                                 [sR, rhsR[kc]], [PSR[b]])
                        if pend is not None:
                            evac(*pend)
                        pend = (b, o)
                evac(*pend)

            S = stats_begin()

            def ev_cq(b, o):
                P.op("act", I("activation", out=cq[:, o, :], in_=PS[b][:, 0:N], func=AF.Copy), [PSR[b]], [cqR[o]])
                stats_add(S, PS[b][:, 0:N], PSR[b], N, onesQ, QC)
            lin_group("dq", QC, DC, hT, hR, ev_cq)
            r, rR = stats_end(S, N, onesQ, QC)
            for o in range(QC):
                kk = PL["qn"] + j * QC + o
                P.op("dve", I("scalar_tensor_tensor", out=cqn[:, o, :], in0=cq[:, o, :], scalar=pcol[:, kk:kk + 1], in1=r, op0=ALU.mult, op1=ALU.mult),
                     [cqR[o], rR, pcolR], [cqnR[o]])
            def ev_qn(b, h):
                P.op("act", I("activation", out=Qn[:, h, :], in_=PS[b][:, 0:N], func=AF.Copy), [PSR[b]], [QnR[h]])
            lin_group("uqn", NH, QC, cqn, cqnR, ev_qn)

            def rope_evac(bp, bs, dst, dstR, q):
                P.op("dve", I("tensor_tensor", out=rtmp[0:64, q, :], in0=PS[bp][0:64, 0:N], in1=cs[0:64, 0, :], op=ALU.mult), [PSR[bp], csR], [rtmpR[q]])
                P.op("dve", I("tensor_tensor", out=sgt[0:64, q, :], in0=PS[bs][0:64, 0:N], in1=cs[0:64, 1, :], op=ALU.mult), [PSR[bs], csR], [sgtR[q]])
                P.op("dve", I("tensor_tensor", out=dst, in0=rtmp[0:64, q, :], in1=sgt[0:64, q, :], op=ALU.add), [rtmpR[q], sgtR[q]], [dstR])

            sgt = A.alloc((2, N), F32)
            sgtR = A.res(2)
            g = max(1, SLOT_COLS // (QC * 128))
            pend = None
            hi = 0
            for h0 in range(0, NH, g):
                slot, sR = ws_next(f"uqr{i}_{h0}")
                for k, h in enumerate(range(h0, min(NH, h0 + g))):
                    bp, bs = nbank(), nbank()
                    base = k * QC * 128
                    for (b, bo) in ((bp, base), (bs, base + QC * 64)):
                        for kc in range(QC):
                            P.op("pe", I("matmul", PS[b][0:64, 0:N], slot[:, bo + kc * 64:bo + (kc + 1) * 64], cqn[:, kc, :], start=(kc == 0), stop=(kc == QC - 1)),
                                 [sR, cqnR[kc]], [PSR[b]])
                    if pend is not None:
                        rope_evac(*pend)
                    pend = (bp, bs, Qr[0:64, h, :], QrR[h], hi % 2)
                    hi += 1
            rope_evac(*pend)
            S = stats_begin()

            def ev_kv(b, o):
                P.op("act", I("activation", out=ckv[:, o, :], in_=PS[b][:, 0:N], func=AF.Copy), [PSR[b]], [ckvR[o]])
                stats_add(S, PS[b][:, 0:N], PSR[b], N, onesK, KC)
            lin_group("dkv", KC, DC, hT, hR, ev_kv)
            r, rR = stats_end(S, N, onesK, KC)
            for o in range(KC):
                kk = PL["kn"] + j * KC + o
                P.op("dve", I("scalar_tensor_tensor", out=ckv[:, o, :], in0=ckv[:, o, :], scalar=pcol[:, kk:kk + 1], in1=r, op0=ALU.mult, op1=ALU.mult),
                     [ckvR[o], rR, pcolR], [ckvR[o]])
                P.op("act", I("activation", out=ckvb[:, o, :], in_=ckv[:, o, :], func=AF.Copy), [ckvR[o]], [ckvbR[o]])
            P.dma("sp", I("dma_start", out=out_ckv, in_=ckv), list(ckvR), [Res()], osem())
            slot, sR = ws_next(f"dkvr{i}")
            bp, bs = nbank(), nbank()
            for (b, bo) in ((bp, 0), (bs, DC * 64)):
                for kc in range(DC):
                    P.op("pe", I("matmul", PS[b][0:64, 0:N], slot[:, bo + kc * 64:bo + (kc + 1) * 64], hT[:, kc, :], start=(kc == 0), stop=(kc == DC - 1)),
                         [sR, hR[kc]], [PSR[b]])
            rope_evac(bp, bs, krf[0:64, :], krfR, 0)
            P.op("act", I("activation", out=krb[0:64, :], in_=krf[0:64, :], func=AF.Copy), [krfR], [krbR])
            P.dma("sp", I("dma_start", out=out_kr, in_=krf[0:64, :]), [krfR], [Res()], osem())

        kTdR = [[Res() for _ in range(NH)] for _ in range(max(NM, 1))]
        vdR = [[Res() for _ in range(NH)] for _ in range(max(NM, 1))]
        krdR = [Res() for _ in range(max(NM, 1))]

        def mla_prompt(i, t):
            j = i // 2
            N = T
            t0 = t * T
            mark0 = A.top
            Qn = A.alloc((NH, N), BF16)
            QnR = A.res(NH)
            Qr = A.alloc((NH, N), BF16)
            QrR = A.res(NH)
            markA = A.top
            hT = A.alloc((DC, N), BF16)
            hR = A.res(DC)
            ckvb = A.alloc((KC, N), BF16)
            ckvbR = A.res(KC)
            krb = A.alloc((N,), BF16)
            krbR = A.res(1)[0]
            gk = max(1, SLOT_COLS // (KC * 128))
            Kst = A.alloc((2, N), BF16)
            KstR = A.res(2)
            Vsb = A.alloc((min(NH, gk), 4, 128), BF16)
            VsbR = A.res(1)[0]
            mla_proj(i, N, t0, hT, hR, Qn, QnR, Qr, QrR, ckvb, ckvbR, krb, krbR,
                     ockvT[j].rearrange("(c p) t -> p c t", p=128)[:, :, t0:t0 + N], okrT[j][:, t0:t0 + N])
            P.dma("sp", I("dma_start", out=krT_d[j][:, t0:t0 + N], in_=krb[0:64, :]), [krbR], [krdR[j]], s_krst)
            g = max(1, SLOT_COLS // (KC * 128))
            pend = None

            def ev_k(b, h):
                q = h % 2
                P.op("act", I("activation", out=Kst[:, q, :], in_=PS[b][:, 0:N], func=AF.Copy), [PSR[b]], [KstR[q]])
                P.dma("sp", I("dma_start", out=kT_d[j][h][:, t0:t0 + N], in_=Kst[:, q, :]), [KstR[q]], [kTdR[j][h]], s_kst[q])
            for h0 in range(0, NH, g):
                slot, sR = ws_next(f"uk{i}_{h0}")
                for k, h in enumerate(range(h0, min(NH, h0 + g))):
                    b = nbank()
                    for kc in range(KC):
                        P.op("pe", I("matmul", PS[b][:, 0:N], slot[:, (k * KC + kc) * 128:(k * KC + kc + 1) * 128], ckvb[:, kc, :], start=(kc == 0), stop=(kc == KC - 1)),
                             [sR, ckvbR[kc]], [PSR[b]])
                    if pend is not None:
                        ev_k(*pend)
                    pend = (b, h)
            ev_k(*pend)
            pend = None

            def ev_v(b, tc, hs0, nh):
                P.op("act", I("activation", out=Vsb[:, hs0:hs0 + nh, tc, :], in_=PS[b][:, 0:nh * 128].rearrange("p (h d) -> p h d", h=nh), func=AF.Copy), [PSR[b]], [VsbR])
            for h0 in range(0, NH, g):
                slot, sR = ws_next(f"uv{i}_{h0}")
                nhp = min(NH, h0 + g) - h0
                for hs0 in range(0, nhp, 4):
                    nh = min(4, nhp - hs0)
                    for tc in range(N // 128):
                        b = nbank()
                        for kc in range(KC):
                            P.op("pe", I("matmul", PS[b][:, 0:nh * 128], ckvb[:, kc, tc * 128:(tc + 1) * 128],
                                         slot[:, kc * nhp * 128 + hs0 * 128:kc * nhp * 128 + (hs0 + nh) * 128], start=(kc == 0), stop=(kc == KC - 1)),
                                 [sR, ckvbR[kc]], [PSR[b]])
                        if pend is not None:
                            ev_v(*pend)
                        pend = (b, tc, hs0, nh)
                ev_v(*pend)
                pend = None
                P.dma("sp", I("dma_start", out=v_d[j].rearrange("h p k d -> p h k d")[:, h0:h0 + nhp, 4 * t:4 * t + 4, :], in_=Vsb[:, 0:nhp]), [VsbR],
                      [vdR[j][hh] for hh in range(h0, h0 + nhp)], s_vst)
            A.top = markA
            Osb = A.alloc((NH, N), BF16)
            OsbR = A.res(NH)
            markB = A.top
            nk = 4 * (t + 1)
            Ksl = A.alloc((2, nk * 128), BF16)
            KslR = A.res(2)
            Vsl = A.alloc((2, nk, 128), BF16)
            VslR = A.res(2)
            krT = A.alloc((nk * 128,), BF16)
            krTR = A.res(1)[0]
            LOOK = 3
            SB = [0, 1, 2, 5, 6]
            NPT = LOOK + 2
            Pt = A.alloc((NPT, N), BF16)
            PtR = A.res(NPT)
            rl = A.alloc((1, N), F32)
            rlR = A.res(1)
            acc = A.alloc((2, N), F32)
            accR = A.res(2)
            accb = A.alloc((2, N), BF16)
            accbR = A.res(2)
            P.dma("sp", I("dma_start", out=krT[0:64, :], in_=krT_d[j][:, 0:nk * 128]), [krdR[j]], [krTR], s_krl)

            def load_head(h):
                q = h % 2
                P.dma("sp", I("dma_start", out=Ksl[:, q, :], in_=kT_d[j][h][:, 0:nk * 128]), [kTdR[j][h]], [KslR[q]], s_kl[q])
                P.dma("sp", I("dma_start", out=Vsl[:, q, :, :], in_=v_d[j][h][:, 0:nk, :]), [vdR[j][h]], [VslR[q]], s_vl[q])
            load_head(0)
            pi = 0
            for h in range(NH):
                if h + 1 < NH:
                    load_head(h + 1)
                q = h % 2
                bo, bl = 3 + q, 7

                def cols(kc):
                    jj = kc - 4 * t
                    return (0 if jj < 0 else 128 * jj), jj

                def s_mm(kc, sb):
                    c0, jj = cols(kc)
                    P.op("pe", I("matmul", PS[sb][:, c0:N], Ksl[:, q, kc * 128:(kc + 1) * 128], Qn[:, h, c0:N], start=True, stop=False), [KslR[q], QnR[h]], [PSR[sb]])
                    P.op("pe", I("matmul", PS[sb][:, c0:N], krT[0:64, kc * 128:(kc + 1) * 128], Qr[0:64, h, c0:N], start=False, stop=True), [krTR, QrR[h]], [PSR[sb]])

                def p_ev(kc, sb, pb):
                    c0, jj = cols(kc)
                    P.op("act", I("activation", out=Pt[:, pb, c0:N], in_=PS[sb][:, c0:N], func=AF.Exp, scale=cfg.scale), [PSR[sb]], [PtR[pb]])
                    if jj >= 0:
                        P.op("dve", I("memset", Pt[64:128, pb, c0:c0 + 64], 0.0), [], [PtR[pb]])

                def pv_mm(kc, pb):
                    c0, jj = cols(kc)
                    P.op("pe", I("matmul", PS[bo][:, c0:N], Vsl[:, q, kc, :], Pt[:, pb, c0:N], start=(kc == 0), stop=(kc == nk - 1)), [VslR[q], PtR[pb]], [PSR[bo]])
                    if kc == 0:
                        P.op("dve", I("tensor_copy", out=acc[:, q, :], in_=Pt[:, pb, :]), [PtR[pb]], [accR[q]])
                    else:
                        P.op("dve", I("tensor_tensor", out=acc[:, q, c0:N], in0=acc[:, q, c0:N], in1=Pt[:, pb, c0:N], op=ALU.add), [PtR[pb], accR[q]], [accR[q]])
                for kc in range(min(LOOK, nk)):
                    s_mm(kc, SB[(pi + kc) % len(SB)])
                    p_ev(kc, SB[(pi + kc) % len(SB)], (pi + kc) % NPT)
                for kc in range(0, nk, 2):
                    nxt = [k2 for k2 in (kc + LOOK, kc + LOOK + 1) if k2 < nk]
                    for k2 in nxt:
                        s_mm(k2, SB[(pi + k2) % len(SB)])
                    for k2 in nxt:
                        p_ev(k2, SB[(pi + k2) % len(SB)], (pi + k2) % NPT)
                    for k2 in (kc, kc + 1):
                        if k2 < nk:
                            pv_mm(k2, (pi + k2) % NPT)
                pi += nk
                P.op("act", I("activation", out=accb[:, q, :], in_=acc[:, q, :], func=AF.Copy), [accR[q]], [accbR[q]])
                P.op("pe", I("matmul", PS[bl][:, 0:N], ones1, accb[:, q, :], start=True, stop=True), [constR, accbR[q]], [PSR[bl]])
                P.op("dve", I("reciprocal", out=rl[:, 0, :], in_=PS[bl][:, 0:N]), [PSR[bl]], [rlR[0]])
                P.op("dve", I("tensor_tensor", out=Osb[:, h, :], in0=PS[bo][:, 0:N], in1=rl[:, 0, :], op=ALU.mult), [PSR[bo], rlR[0]], [OsbR[h]])
            A.top = markB
            mla_out(i, N, Osb, OsbR)
            phase_end(mark0)

        def mla_out(i, N, Osb, OsbR):
            m = A.alloc((DC, N), F32)
            mR = A.res(DC)
            S = stats_begin()
            g = max(1, SLOT_COLS // (NH * 128))
            pend = None
            for o0 in range(0, DC, g):
                slot, sR = ws_next(f"wo{i}_{o0}")
                for k, o in enumerate(range(o0, min(DC, o0 + g))):
                    b = nbank()
                    for h in range(NH):
                        P.op("pe", I("matmul", PS[b][:, 0:N], slot[:, (k * NH + h) * 128:(k * NH + h + 1) * 128], Osb[:, h, :], start=(h == 0), stop=(h == NH - 1)),
                             [sR, OsbR[h]], [PSR[b]])
                    if pend is not None:
                        evac_m(pend[0], pend[1], N, m, mR, S, gcol(i, 1, pend[1]))
                    pend = (b, o)
            evac_m(pend[0], pend[1], N, m, mR, S, gcol(i, 1, pend[1]))
            post_residual(i, 1, N, m, mR, S)

        def mla_sample(i):
            j = i // 2
            N = NS
            mark0 = A.top
            Qn = A.alloc((NH, N), BF16)
            QnR = A.res(NH)
            Qr = A.alloc((NH, N), BF16)
            QrR = A.res(NH)
            hT = A.alloc((DC, N), BF16)
            hR = A.res(DC)
            ckvb = A.alloc((KC, N), BF16)
            ckvbR = A.res(KC)
            krb = A.alloc((N,), BF16)
            krbR = A.res(1)[0]
            Osb = A.alloc((NH, N), BF16)
            OsbR = A.res(NH)
            QL = A.alloc((KC, NH, N), BF16)
            QLR = A.res(KC)
            OL = A.alloc((KC, NBS, NH, SS), BF16)
            OLR = A.res(KC)
            cnat = A.alloc((NBS, cfg.KL), BF16)
            cnatR = A.res(NBS)
            CK = A.alloc((PKC, cfg.KL), BF16)
            CKR = A.res(1)[0]
            CKT = A.alloc((KC, cfg.PAST), BF16)
            CKTR = A.res(1)[0]
            KRT = A.alloc((cfg.PAST,), BF16)
            KRTR = A.res(1)[0]
            Pt = A.alloc((4, NH * SS), BF16)
            PtR = A.res(4)
            rl = A.alloc((NH * SS,), F32)
            rlR = A.res(1)[0]
            s_c = [newsem(f"sc{i}_{k}") for k in range(3)]
            mla_proj(i, N, SEQ, hT, hR, Qn, QnR, Qr, QrR, ckvb, ckvbR, krb, krbR,
                     ockvTs[j].rearrange("(c p) t -> p c t", p=128), okrTs[j][:, :])
            g2 = max(1, SLOT_COLS // cfg.KL)
            pend = None

            def ev_ql(b, cc, h):
                P.op("act", I("activation", out=QL[:, cc, h, :], in_=PS[b][:, 0:N], func=AF.Copy), [PSR[b]], [QLR[cc]])
            for h0 in range(0, NH, g2):
                slot, sR = ws_next(f"ukT{i}_{h0}")
                for k, h in enumerate(range(h0, min(NH, h0 + g2))):
                    for cc in range(KC):
                        b = nbank()
                        P.op("pe", I("matmul", PS[b][:, 0:N], slot[:, k * cfg.KL + cc * 128:k * cfg.KL + (cc + 1) * 128], Qn[:, h, :], start=True, stop=True), [sR, QnR[h]], [PSR[b]])
                        if pend is not None:
                            ev_ql(*pend)
                        pend = (b, cc, h)
            ev_ql(*pend)
            for bt in range(NBS):
                b = nbank()
                for cc in range(KC):
                    P.op("pe", I("matmul", PS[b][0:SS, cc * 128:(cc + 1) * 128], ckvb[:, cc, bt * SS:(bt + 1) * SS], ident, start=True, stop=True), [ckvbR[cc], constR], [PSR[b]])
                P.op("act", I("activation", out=cnat[0:SS, bt, :], in_=PS[b][0:SS, 0:cfg.KL], func=AF.Copy), [PSR[b]], [cnatR[bt]])
            NQ = NH * SS
            pi = 0
            for bt in range(NBS):
                P.dma("pool", I("dma_start", out=CK, in_=cck_in[j][bt].rearrange("(k p) c -> p k c", p=128)), [], [CKR], s_c[0])
                P.dma("pool", I("dma_start", out=CKT, in_=cckT_in[j][bt].rearrange("(k p) t -> p k t", p=128)), [], [CKTR], s_c[1])
                P.dma("pool", I("dma_start", out=KRT[0:64, :], in_=ckrT_in[j][bt]), [], [KRTR], s_c[2])
                nk = PKC + 1
                assert KC * NQ <= 1024 and NQ <= 512

                def oacc(cc):
                    fo = cc * NQ
                    return PS[3 + fo // 512][:, fo % 512:fo % 512 + NQ], PSR[3 + fo // 512]

                def s_mm(kc, sb):
                    M = 128 if kc < PKC else SS
                    for cc in range(KC):
                        lhs = CKT[:, cc, kc * 128:(kc + 1) * 128] if kc < PKC else ckvb[:, cc, bt * SS:(bt + 1) * SS]
                        P.op("pe", I("matmul", PS[sb][0:M, 0:NQ].rearrange("p (h s) -> p h s", h=NH), lhs, QL[:, cc, :, bt * SS:(bt + 1) * SS], start=(cc == 0), stop=False),
                             [CKTR, ckvbR[cc], QLR[cc]], [PSR[sb]])
                    lhs = KRT[0:64, kc * 128:(kc + 1) * 128] if kc < PKC else krb[0:64, bt * SS:(bt + 1) * SS]
                    P.op("pe", I("matmul", PS[sb][0:M, 0:NQ].rearrange("p (h s) -> p h s", h=NH), lhs, Qr[0:64, :, bt * SS:(bt + 1) * SS], start=False, stop=True),
                         [KRTR, krbR] + QrR, [PSR[sb]])

                def p_ev(kc, sb, pb):
                    M = 128 if kc < PKC else SS
                    P.op("act", I("activation", out=Pt[0:M, pb, :], in_=PS[sb][0:M, 0:NQ], func=AF.Exp, scale=cfg.scale), [PSR[sb]], [PtR[pb]])

                def pv_mm(kc, pb):
                    M = 128 if kc < PKC else SS
                    for cc in range(KC):
                        lhs = CK[:, kc, cc * 128:(cc + 1) * 128] if kc < PKC else cnat[0:SS, bt, cc * 128:(cc + 1) * 128]
                        oa, oaR = oacc(cc)
                        P.op("pe", I("matmul", oa, lhs, Pt[0:M, pb, :], start=(kc == 0 and (cc * NQ) % 512 == 0), stop=(kc == nk - 1), skip_group_check=True),
                             [CKR, cnatR[bt], PtR[pb]], [oaR])
                    P.op("pe", I("matmul", PS[5][:, 0:NQ], ones1[0:M, :], Pt[0:M, pb, :], start=(kc == 0), stop=(kc == nk - 1)), [constR, PtR[pb]], [PSR[5]])
                LOOK = 2
                for kc in range(min(LOOK, nk)):
                    s_mm(kc, (pi + kc) % 3)
                    p_ev(kc, (pi + kc) % 3, (pi + kc) % 4)
                for kc in range(nk):
                    if kc + LOOK < nk:
                        s_mm(kc + LOOK, (pi + kc + LOOK) % 3)
                        p_ev(kc + LOOK, (pi + kc + LOOK) % 3, (pi + kc + LOOK) % 4)
                    pv_mm(kc, (pi + kc) % 4)
                pi += nk
                P.op("dve", I("reciprocal", out=rl, in_=PS[5][:, 0:NQ]), [PSR[5]], [rlR])
                for cc in range(KC):
                    oa, oaR = oacc(cc)
                    P.op("dve", I("tensor_tensor", out=OL[:, cc, bt, :, :], in0=oa.rearrange("p (h s) -> p h s", h=NH), in1=rl.rearrange("p (h s) -> p h s", h=NH), op=ALU.mult),
                         [oaR, rlR], [OLR[cc]])
            g = max(1, SLOT_COLS // (KC * 128))
            pend = None

            def ev_o(b, h):
                P.op("act", I("activation", out=Osb[:, h, :], in_=PS[b][:, 0:N], func=AF.Copy), [PSR[b]], [OsbR[h]])
            bank_i[0] = 0
            for h0 in range(0, NH, g):
                slot, sR = ws_next(f"uv{i}_{h0}")
                nhp = min(NH, h0 + g) - h0
                for k, h in enumerate(range(h0, h0 + nhp)):
                    b = nbank() % 3
                    for cc in range(KC):
                        P.op("pe", I("matmul", PS[b][:, 0:N].rearrange("p (b s) -> p b s", b=NBS), slot[:, cc * nhp * 128 + k * 128:cc * nhp * 128 + (k + 1) * 128],
                                     OL[:, cc, :, h, :], start=(cc == 0), stop=(cc == KC - 1)), [sR, OLR[cc]], [PSR[b]])
                    if pend is not None:
                        ev_o(*pend)
                    pend = (b, h)
            ev_o(*pend)
            mla_out(i, N, Osb, OsbR)
            phase_end(mark0)

        xv = xT_in.rearrange("(c p) t -> p c t", p=128)
        yv = yT.rearrange("(c p) t -> p c t", p=128)
        for t in range(NT):
            t0 = t * T
            for c in range(DC):
                P.dma("sp", I("dma_start", out=xT[:, c, :], in_=xv[:, c, t0:t0 + T]), [], [xTR[c]], s_xy[c])
            for i in range(DEPTH):
                j = i // 2
                last = (t == NT - 1)
                if i % 2 == 0:
                    mla_prompt(i, t)
                else:
                    hv = phist[:, j].rearrange("p c (b k) -> p c b k", b=1)
                    poolmix(i, T, 1, T, hv, phistR[j], t == 0,
                            opool[j].rearrange("p c (b k) -> p c b k", b=1) if last else None)
                hv = chist[:, i].rearrange("p c (b k) -> p c b k", b=1)
                ffn(i, T, 1, T, hv, chistR[i],
                    oconv[i].rearrange("p c (b k) -> p c b k", b=1) if last else None)
            for c in range(DC):
                P.dma("sp", I("dma_start", out=yv[:, c, t0:t0 + T], in_=xT[:, c, :]), [xTR[c]], [Res()], s_xy[c])
        for c in range(DC):
            P.dma("sp", I("dma_start", out=xT[:, c, 0:NS], in_=xsT_in.rearrange("(c p) t -> p c t", p=128)[:, c, :]), [], [xTR[c]], s_xy[c])
        for i in range(DEPTH):
            j = i // 2
            if i % 2 == 0:
                mla_sample(i)
            else:
                mk = A.top
                sh = A.alloc((DC, NBS, 15), F32)
                shR = A.res(DC)
                P.dma("sp", I("dma_start", out=sh, in_=spool_in[j]), [], list(shR), s_in)
                poolmix(i, NS, NBS, SS, sh, shR, False, opools[j])
                A.top = mk
            mk = A.top
            ch = A.alloc((2 * FP, NBS, 2), F32)
            chR = A.res(2 * FP)
            P.dma("sp", I("dma_start", out=ch, in_=sconv_in[i]), [], list(chR), s_in)
            ffn(i, NS, NBS, SS, ch, chR, oconvs[i])
            A.top = mk
        for c in range(DC):
            P.dma("sp", I("dma_start", out=ysT.rearrange("(c p) t -> p c t", p=128)[:, c, :], in_=xT[:, c, 0:NS]), [xTR[c]], [Res()], s_xy[c])
        assert WS.consumed == len(WS.order), (WS.consumed, len(WS.order))
        P.barrier(final=True)
        P.finish()

        with nc.allow_non_contiguous_dma(reason="tiny state rows"), nc.Block() as block:
            @block.tensor
            def _(e):
                P.replay("pe", e, esem)

            @block.scalar
            def _(e):
                P.replay("act", e, esem)

            @block.vector
            def _(e):
                P.replay("dve", e, esem)

            @block.gpsimd
            def _(e):
                P.replay("pool", e, esem)

            @block.sync
            def _(e):
                P.replay("sp", e, esem)
    return nc


def _cols(v):
    v = np.asarray(v, np.float32).reshape(-1, 128)
    return np.ascontiguousarray(v.T)


def prep_inputs(cfg, inp):
    pieces, TOT = weight_plan(cfg)
    wblob = [np.empty((128, TOT[i]), np.float32) for i in range(cfg.DEPTH)]
    for p in pieces:
        a = p["build"](inp)
        assert a.shape == (128, p["ncols"]), (p["name"], a.shape, p["ncols"])
        wblob[p["layer"]][:, p["off"]:p["off"] + p["ncols"]] = a
    PL, NPC = pcol_layout(cfg)
    pcol = np.zeros((128, NPC), np.float32)
    DC, FP = cfg.DC, cfg.FP
    pcol[:, PL["g"]:PL["g"] + cfg.DEPTH * 4 * DC] = _cols(inp["norm_g"])
    if cfg.NM:
        pcol[:, PL["qn"]:PL["qn"] + cfg.NM * cfg.QC] = _cols(inp["mla_q_norm"])
        pcol[:, PL["kn"]:PL["kn"] + cfg.NM * cfg.KC] = _cols(inp["mla_kv_norm"])
    if cfg.NP:
        pcol[:, PL["psc"]:PL["psc"] + cfg.NP * DC] = _cols(inp["pool_scale"])
    pcol[:, PL["cw"]:PL["cw"] + cfg.DEPTH * 3 * 2 * FP] = _cols(inp["ffn_conv_w"])
    pcol[:, PL["cb"]:PL["cb"] + cfg.DEPTH * 2 * FP] = _cols(inp["ffn_conv_b"])
    cst = np.zeros((128, 256), np.float32)
    cst[:, 0:128] = np.eye(128, dtype=np.float32)
    for gi in range(4):
        w = 2 ** (gi + 1)
        cst[:, 128 + gi * 16:128 + gi * 16 + 16] = (1.0 / np.minimum(np.float32(w), np.arange(16, dtype=np.float32) + 1.0))[None, :]
    half = 32
    inv = (np.float32(ROPE_BASE) ** (-np.arange(half, dtype=np.float32) / np.float32(half))).astype(np.float32)
    pos = np.concatenate([np.arange(cfg.SEQ, dtype=np.float32)] + [cfg.PAST + np.arange(cfg.SS, dtype=np.float32)] * cfg.NBS)
    ang = (pos[:, None] * inv[None, :]).astype(np.float32)
    c, s = np.cos(ang).astype(np.float32).T, np.sin(ang).astype(np.float32).T
    rope = np.stack([np.concatenate([c, c], 0), np.concatenate([-s, s], 0)], axis=1)
    rope = np.ascontiguousarray(rope, np.float32)
    maps = []
    NBS = cfg.NBS
    for core in range(NCORES):
        sb = slice(core * NBS, (core + 1) * NBS)
        m = dict(pcol=pcol, cst=cst, rope=rope)
        for i in range(cfg.DEPTH):
            m[f"wblob{i}"] = wblob[i]
        m["xT"] = np.ascontiguousarray(inp["x_prompt"][core].T)
        m["xsT"] = np.ascontiguousarray(inp["x_sample"][sb].reshape(cfg.NS, cfg.D).T)
        if cfg.NM:
            ck = inp["cache_ckv"][:, sb]
            m["cck"] = np.ascontiguousarray(ck)
            m["cckT"] = np.ascontiguousarray(ck.transpose(0, 1, 3, 2))
            m["ckrT"] = np.ascontiguousarray(inp["cache_krope"][:, sb].transpose(0, 1, 3, 2))
        else:
            m["cck"] = np.zeros((1, NBS, cfg.PAST, cfg.KL), np.float32)
            m["cckT"] = np.zeros((1, NBS, cfg.KL, cfg.PAST), np.float32)
            m["ckrT"] = np.zeros((1, NBS, 64, cfg.PAST), np.float32)
        if cfg.NP:
            m["spool"] = np.ascontiguousarray(inp["state_pool"][:, sb].reshape(cfg.NP, NBS, 15, cfg.DC, 128).transpose(0, 4, 3, 1, 2))
        else:
            m["spool"] = np.zeros((1, 128, cfg.DC, NBS, 15), np.float32)
        m["sconv"] = np.ascontiguousarray(inp["state_conv"][:, sb].reshape(cfg.DEPTH, NBS, 2, 2 * cfg.FP, 128).transpose(0, 4, 3, 1, 2))
        maps.append(m)
    return maps


def gather_outputs(cfg, res):
    R = res
    NM, NP = cfg.NM, cfg.NP
    y = np.stack([r["yT"].T for r in R])
    ys = np.concatenate([r["ysT"].T.reshape(cfg.NBS, cfg.SS, cfg.D) for r in R], 0)
    ckv_p = np.stack([r["ockvT"][:NM].transpose(0, 2, 1) for r in R], 1)
    kr_p = np.stack([r["okrT"][:NM].transpose(0, 2, 1) for r in R], 1)
    pool_p = np.stack([r["opool"][:NP].transpose(0, 3, 2, 1).reshape(NP, 15, cfg.D) for r in R], 1)
    conv_p = np.stack([r["oconv"].transpose(0, 3, 2, 1).reshape(cfg.DEPTH, 2, 2 * cfg.DFF) for r in R], 1)
    ckv_s = np.concatenate([r["ockvTs"][:NM].transpose(0, 2, 1).reshape(NM, cfg.NBS, cfg.SS, cfg.KL) for r in R], 1)
    kr_s = np.concatenate([r["okrTs"][:NM].transpose(0, 2, 1).reshape(NM, cfg.NBS, cfg.SS, 64) for r in R], 1)
    pool_s = np.concatenate([r["opools"][:NP].transpose(0, 3, 4, 2, 1).reshape(NP, cfg.NBS, 15, cfg.D) for r in R], 1)
    conv_s = np.concatenate([r["oconvs"].transpose(0, 3, 4, 2, 1).reshape(cfg.DEPTH, cfg.NBS, 2, 2 * cfg.DFF) for r in R], 1)
    outs = (y, ys, ckv_p, kr_p, pool_p, conv_p, ckv_s, kr_s, pool_s, conv_s)
    return tuple(np.ascontiguousarray(o, dtype=np.float32) for o in outs)


def run_cfg(cfg, inputs):
    inp = {k: np.asarray(v) for k, v in inputs.items()}
    maps = prep_inputs(cfg, inp)
    nc = build_program(cfg)
    res = run_bass_kernel_spmd(nc, maps, core_ids=list(range(NCORES)))
    return gather_outputs(cfg, res.results)


def kernel(**inputs):
    return run_cfg(Cfg(), inputs)
```

```python
import numpy as np
import concourse.bass as bass
import concourse.mybir as mybir
from concourse.bass_utils import run_bass_kernel_spmd

F32 = mybir.dt.float32
BF16 = mybir.dt.bfloat16
AF = mybir.ActivationFunctionType
ALU = mybir.AluOpType

EPS = 1e-6
ROPE_BASE = 10000.0
NCORES = 8
SLOT_COLS = 4096
NSLOT = 6
CAST_COLS = 16384


class Cfg:
    def __init__(self, D=2048, SEQ=4096, NH=16, QL=512, KL=512, DFF=5632, DEPTH=4,
                 NBS=4, SS=16, PAST=2048, T=512):
        self.D, self.SEQ, self.NH, self.QL, self.KL, self.DFF, self.DEPTH = D, SEQ, NH, QL, KL, DFF, DEPTH
        self.NBS, self.SS, self.PAST, self.T = NBS, SS, PAST, T
        self.DC, self.QC, self.KC, self.FP = D // 128, QL // 128, KL // 128, DFF // 128
        self.NT = SEQ // T
        self.NM, self.NP = (DEPTH + 1) // 2, DEPTH // 2
        self.NS = NBS * SS
        self.PKC = PAST // 128
        self.GC = self.DC // 4
        self.scale = float((128 + 64) ** -0.5)
        assert self.DC % 4 == 0 and SEQ % T == 0 and T == 512


def _blocks_piece(W, K, blocks):
    W3 = W.reshape(K, 128, W.shape[1])
    outs = [np.ascontiguousarray(W3[:, :, c0:c0 + M].transpose(1, 0, 2)).reshape(128, K * M) for c0, M in blocks]
    return np.concatenate(outs, axis=1)


def weight_plan(cfg):
    P = []
    D, DC, NH, QC, KC, FP = cfg.D, cfg.DC, cfg.NH, cfg.QC, cfg.KC, cfg.FP

    cur = [0]

    def add(name, ncols, fn):
        assert ncols <= SLOT_COLS, (name, ncols)
        P.append(dict(name=name, ncols=ncols, build=fn, layer=cur[0]))

    for i in range(cfg.DEPTH):
        cur[0] = i
        j = i // 2
        if i % 2 == 0:
            g = max(1, SLOT_COLS // (DC * 128))
            for o0 in range(0, QC, g):
                oc = list(range(o0, min(QC, o0 + g)))
                add(f"dq{i}_{o0}", len(oc) * DC * 128,
                    lambda inp, j=j, oc=oc: _blocks_piece(inp["mla_w_dq"][j], DC, [(o * 128, 128) for o in oc]))
            g = max(1, SLOT_COLS // (QC * 128))
            for h0 in range(0, NH, g):
                hs = list(range(h0, min(NH, h0 + g)))
                add(f"uqn{i}_{h0}", len(hs) * QC * 128,
                    lambda inp, j=j, hs=hs: _blocks_piece(inp["mla_w_uq"][j], QC, [(h * 192, 128) for h in hs]))
            for h0 in range(0, NH, g):
                hs = list(range(h0, min(NH, h0 + g)))

                def f(inp, j=j, hs=hs):
                    W = inp["mla_w_uq"][j]
                    bl = []
                    for h in hs:
                        r = W[:, h * 192 + 128:h * 192 + 192]
                        sw = np.concatenate([r[:, 32:], r[:, :32]], axis=1)
                        bl.append(_blocks_piece(np.concatenate([r, sw], axis=1), QC, [(0, 64), (64, 64)]))
                    return np.concatenate(bl, axis=1)
                add(f"uqr{i}_{h0}", len(hs) * QC * 128, f)
            g = max(1, SLOT_COLS // (DC * 128))
            for o0 in range(0, KC, g):
                oc = list(range(o0, min(KC, o0 + g)))
                add(f"dkv{i}_{o0}", len(oc) * DC * 128,
                    lambda inp, j=j, oc=oc: _blocks_piece(inp["mla_w_dkv"][j], DC, [(o * 128, 128) for o in oc]))

            def fr(inp, j=j):
                r = inp["mla_w_dkv"][j][:, cfg.KL:cfg.KL + 64]
                sw = np.concatenate([r[:, 32:], r[:, :32]], axis=1)
                return _blocks_piece(np.concatenate([r, sw], axis=1), DC, [(0, 64), (64, 64)])
            add(f"dkvr{i}", DC * 128, fr)
            g = max(1, SLOT_COLS // (KC * 128))
            for h0 in range(0, NH, g):
                hs = list(range(h0, min(NH, h0 + g)))
                add(f"uk{i}_{h0}", len(hs) * KC * 128,
                    lambda inp, j=j, hs=hs: _blocks_piece(inp["mla_w_uk"][j].reshape(cfg.KL, NH * 128), KC, [(h * 128, 128) for h in hs]))
            g2 = max(1, SLOT_COLS // cfg.KL)
            for h0 in range(0, NH, g2):
                hs = list(range(h0, min(NH, h0 + g2)))
                add(f"ukT{i}_{h0}", len(hs) * cfg.KL,
                    lambda inp, j=j, hs=hs: np.concatenate([np.ascontiguousarray(inp["mla_w_uk"][j][:, h, :].T) for h in hs], axis=1))
            for h0 in range(0, NH, g):
                hs = list(range(h0, min(NH, h0 + g)))

                def fv(inp, j=j, hs=hs):
                    W = inp["mla_w_uv"][j].reshape(KC, 128, NH, 128)[:, :, hs, :]
                    return np.ascontiguousarray(W.transpose(1, 0, 2, 3)).reshape(128, KC * len(hs) * 128)
                add(f"uv{i}_{h0}", len(hs) * KC * 128, fv)
            g = max(1, SLOT_COLS // (NH * 128))
            for o0 in range(0, DC, g):
                oc = list(range(o0, min(DC, o0 + g)))
                add(f"wo{i}_{o0}", len(oc) * NH * 128,
                    lambda inp, j=j, oc=oc: _blocks_piece(inp["mla_w_o"][j], NH, [(o * 128, 128) for o in oc]))
        else:
            GC = cfg.GC
            g = max(1, SLOT_COLS // (GC * GC * 128))
            for g0 in range(0, 4, g):
                gs = list(range(g0, min(4, g0 + g)))
                add(f"pool{i}_{g0}", len(gs) * GC * GC * 128,
                    lambda inp, j=j, gs=gs: np.concatenate(
                        [_blocks_piece(inp["pool_w"][j][gg], GC, [(o * 128, 128) for o in range(GC)]) for gg in gs], axis=1))
        g = max(1, SLOT_COLS // (DC * 256))
        for p0 in range(0, FP, g):
            ps = list(range(p0, min(FP, p0 + g)))

            def fu(inp, i=i, ps=ps):
                W = inp["ffn_w_up"][i]
                bl = []
                for p in ps:
                    bl += [(p * 128, 128), (cfg.DFF + p * 128, 128)]
                return _blocks_piece(W, DC, bl)
            add(f"up{i}_{p0}", len(ps) * DC * 256, fu)
        nparts = -(-FP * 128 // SLOT_COLS)
        kper = -(-FP // nparts)
        for o in range(DC):
            for part in range(nparts):
                k0, k1 = part * kper, min(FP, (part + 1) * kper)

                def fd(inp, i=i, o=o, k0=k0, k1=k1):
                    W = inp["ffn_w_down"][i][k0 * 128:k1 * 128]
                    return _blocks_piece(W, k1 - k0, [(o * 128, 128)])
                add(f"dn{i}_{o}_{part}", (k1 - k0) * 128, fd)
    tot = [0] * cfg.DEPTH
    for p in P:
        p["off"] = tot[p["layer"]]
        tot[p["layer"]] += p["ncols"]
    return P, tot


def pcol_layout(cfg):
    L = {}
    n = 0
    for nm, cnt in (("g", cfg.DEPTH * 4 * cfg.DC), ("qn", cfg.NM * cfg.QC), ("kn", cfg.NM * cfg.KC),
                    ("psc", max(1, cfg.NP) * cfg.DC), ("cw", cfg.DEPTH * 3 * 2 * cfg.FP), ("cb", cfg.DEPTH * 2 * cfg.FP)):
        L[nm] = n
        n += cnt
    return L, n


class Res:
    __slots__ = ("w", "r")

    def __init__(self):
        self.w = None
        self.r = {}


def RL(n):
    return [Res() for _ in range(n)]


class DSem:
    def __init__(self, h, scoped=True):
        self.h = h
        self.val = 0
        self.scoped = scoped


ENGS = ("pe", "act", "dve", "pool", "sp")


class Planner:
    def __init__(self):
        self.ops = {e: [] for e in ENGS}
        self.waited = {e: {} for e in ENGS}
        self.last_c = {e: -1 for e in ENGS}
        self.dsems = []

    def _need(self, eng, tok, waits):
        if tok is None:
            return
        if tok[0] == "E":
            _, e2, idx = tok
            if e2 == eng and eng == "pe":
                return
            key = ("E", e2)
            if self.waited[eng].get(key, -1) >= idx:
                return
            self.waited[eng][key] = idx
            self.ops[e2][idx][2] = True
            waits.append(tok)
        else:
            _, sem, val = tok
            key = ("D", id(sem))
            if self.waited[eng].get(key, 0) >= val:
                return
            self.waited[eng][key] = val
            waits.append(tok)

    def _deps(self, eng, reads, writes):
        waits = []
        for r in reads:
            self._need(eng, r.w, waits)
        for w in writes:
            self._need(eng, w.w, waits)
            for t in w.r.values():
                self._need(eng, t, waits)
        return waits

    def op(self, eng, ins, reads=(), writes=()):
        waits = self._deps(eng, reads, writes)
        idx = len(self.ops[eng])
        self.ops[eng].append([ins, waits, False, None, 0])
        self.last_c[eng] = idx
        tok = ("E", eng, idx)
        key = ("E", eng)
        for r in reads:
            r.r[key] = tok
        for w in writes:
            w.w = tok
            w.r = {}
        return tok

    def dma(self, q, ins, reads, writes, sem):
        waits = self._deps(q, reads, writes)
        if sem.val > 0:
            self._need(q, ("D", sem, sem.val), waits)
        sem.val += 16
        self.ops[q].append([ins, waits, False, sem, 0])
        tok = ("D", sem, sem.val)
        key = ("D", id(sem))
        for r in reads:
            r.r[key] = tok
        for w in writes:
            w.w = tok
            w.r = {}
        return tok

    def barrier(self, final=False):
        for f in ENGS:
            waits = []
            for e in ENGS:
                if e == f and e == "pe":
                    continue
                if self.last_c[e] >= 0:
                    self._need(f, ("E", e, self.last_c[e]), waits)
            for sem in self.dsems:
                if sem.val and (final or sem.scoped):
                    self._need(f, ("D", sem, sem.val), waits)
            if waits:
                self.ops[f].append([None, waits, False, None, 0])

    def finish(self):
        for e in ENGS:
            c = 0
            for rec in self.ops[e]:
                if rec[2]:
                    c += 1
                rec[4] = c

    def replay(self, name, e, esem):
        ops = self.ops
        for ins, waits, signal, dsem, _ in ops[name]:
            for tok in waits:
                if tok[0] == "E":
                    e.wait_ge(esem[tok[1]], ops[tok[1]][tok[2]][4])
                else:
                    e.wait_ge(tok[1].h, tok[2])
            if ins is not None:
                m, a, k = ins
                r = getattr(e, m)(*a, **k)
                if dsem is not None:
                    r.then_inc(dsem.h, 16)
                elif signal:
                    r.then_inc(esem[name], 1)


def I(m, *a, **k):
    return (m, a, k)


class Arena:
    def __init__(self, ap, words):
        self.ap, self.words, self.top = ap, words, 0
        self.recs = []
        self.last = None

    def res(self, n):
        lo, hi, seed = self.last
        out = [Res() for _ in range(n)]
        for r in out:
            r.r = dict(seed)
        self.recs.append((lo, hi, out))
        return out

    def alloc(self, free_shape, dt):
        n = int(np.prod(free_shape))
        w = n if dt == F32 else (n + 1) // 2
        w = (w + 7) // 8 * 8
        assert self.top + w <= self.words, ("SBUF arena overflow", self.top, w, self.words)
        lo, hi = self.top, self.top + w
        seed, keep = {}, []
        for (l, h, rl) in self.recs:
            if h <= lo or l >= hi:
                keep.append((l, h, rl))
                continue
            for r in rl:
                for t in ([r.w] if r.w is not None else []) + list(r.r.values()):
                    key = ("E", t[1]) if t[0] == "E" else ("D", id(t[1]))
                    o = seed.get(key)
                    if o is None or t[2] > o[2]:
                        seed[key] = t
            if not (lo <= l and h <= hi):
                keep.append((l, h, rl))
        self.recs = keep
        self.last = (lo, hi, seed)
        v = self.ap[:, self.top:self.top + w]
        self.top += w
        if dt != F32:
            v = v.bitcast(dt)
        v = v[:, 0:n]
        if len(free_shape) == 2:
            v = v.rearrange("p (a b) -> p a b", a=free_shape[0])
        elif len(free_shape) == 3:
            v = v.rearrange("p (a b c) -> p a b c", a=free_shape[0], b=free_shape[1])
        elif len(free_shape) == 4:
            v = v.rearrange("p (a b c d) -> p a b c d", a=free_shape[0], b=free_shape[1], c=free_shape[2])
        return v


def build_program(cfg):
    nc = bass.Bass("TRN2", target_bir_lowering=False)
    pieces, TOT = weight_plan(cfg)
    pidx = {p["name"]: k for k, p in enumerate(pieces)}
    PL, NPC = pcol_layout(cfg)
    D, DC, NH, QC, KC, FP, T, NS, NBS, SS = cfg.D, cfg.DC, cfg.NH, cfg.QC, cfg.KC, cfg.FP, cfg.T, cfg.NS, cfg.NBS, cfg.SS
    SEQ, NT, NM, NP, DEPTH = cfg.SEQ, cfg.NT, cfg.NM, cfg.NP, cfg.DEPTH
    NKC = SEQ // 128
    PKC = cfg.PKC

    def din(name, shape, dt=F32):
        return nc.dram_tensor(name, list(shape), dt, kind="ExternalInput").ap()

    def dout(name, shape):
        return nc.dram_tensor(name, list(shape), F32, kind="ExternalOutput").ap()

    def dscr(name, shape, dt=BF16):
        return nc.dram_tensor(name, list(shape), dt, kind="Internal").ap()

    wblob = [din(f"wblob{i}", [128, TOT[i]]) for i in range(DEPTH)]
    xT_in = din("xT", [D, SEQ])
    xsT_in = din("xsT", [D, NS])
    pcol_in = din("pcol", [128, NPC])
    cst_in = din("cst", [128, 256])
    rope_in = din("rope", [64, 2, SEQ + NS])
    cck_in = din("cck", [max(NM, 1), NBS, cfg.PAST, cfg.KL])
    cckT_in = din("cckT", [max(NM, 1), NBS, cfg.KL, cfg.PAST])
    ckrT_in = din("ckrT", [max(NM, 1), NBS, 64, cfg.PAST])
    spool_in = din("spool", [max(NP, 1), 128, DC, NBS, 15])
    sconv_in = din("sconv", [DEPTH, 128, 2 * FP, NBS, 2])

    yT = dout("yT", [D, SEQ])
    ysT = dout("ysT", [D, NS])
    ockvT = dout("ockvT", [max(NM, 1), cfg.KL, SEQ])
    okrT = dout("okrT", [max(NM, 1), 64, SEQ])
    opool = dout("opool", [max(NP, 1), 128, DC, 15])
    oconv = dout("oconv", [DEPTH, 128, 2 * FP, 2])
    ockvTs = dout("ockvTs", [max(NM, 1), cfg.KL, NS])
    okrTs = dout("okrTs", [max(NM, 1), 64, NS])
    opools = dout("opools", [max(NP, 1), 128, DC, NBS, 15])
    oconvs = dout("oconvs", [DEPTH, 128, 2 * FP, NBS, 2])

    wbf = [dscr(f"wbf{i}", [128, TOT[i]]) for i in range(DEPTH)]
    kT_d = dscr("kT_d", [max(NM, 1), NH, 128, SEQ])
    v_d = dscr("v_d", [max(NM, 1), NH, 128, NKC, 128])
    krT_d = dscr("krT_d", [max(NM, 1), 64, SEQ])

    P = Planner()
    ARENA_W = 52480
    es = None

    import contextlib
    with contextlib.ExitStack() as st:
        arena_t = st.enter_context(nc.sbuf_tensor("arena", [128, ARENA_W], F32))
        psb = [st.enter_context(nc.psum_tensor(f"ps{b}", [128, 512], F32)) for b in range(8)]
        esem = {e: st.enter_context(nc.semaphore(f"s_{e}")) for e in ENGS}

        def newsem(name, scoped=True):
            s = DSem(st.enter_context(nc.semaphore(name)), scoped)
            P.dsems.append(s)
            return s

        A = Arena(arena_t, ARENA_W)
        PS = [psb[b][:] for b in range(8)]
        PSR = RL(8)

        xT = A.alloc((DC, T), F32)
        xTR = RL(DC)
        pcol = A.alloc((NPC,), F32)
        pcolR = Res()
        cst = A.alloc((256,), F32)
        cstR = Res()
        ident = A.alloc((128,), BF16)
        onesD = A.alloc((128,), BF16)
        onesQ = A.alloc((128,), BF16)
        onesK = A.alloc((128,), BF16)
        ones1 = A.alloc((128,), BF16)
        constR = Res()
        ring = A.alloc((NSLOT, SLOT_COLS), BF16)
        ringR = RL(NSLOT)
        ringS = [newsem(f"ring{k}", False) for k in range(NSLOT)]
        chist = A.alloc((DEPTH, 2 * FP, 2), F32)
        chistR = [RL(2 * FP) for _ in range(DEPTH)]
        phist = A.alloc((max(NP, 1), DC, 15), F32)
        phistR = [RL(DC) for _ in range(max(NP, 1))]
        rstd = A.alloc((2, T), F32)
        rstdR = RL(2)
        sqb = A.alloc((3, T), BF16)
        sqbR = RL(3)
        pgc = A.alloc((max(NP, 1) * DC,), F32)
        persist_top = A.top
        EW2 = "dve"

        s_in = newsem("s_in")
        s_xy = [newsem(f"xy{c}", False) for c in range(DC)]
        s_cast = [newsem(f"cast{k}", False) for k in range(8)]
        s_kst, s_vst, s_krst = [newsem("kst0"), newsem("kst1")], newsem("vst"), newsem("krst")
        s_kl = [newsem("kl0"), newsem("kl1")]
        s_vl = [newsem("vl0"), newsem("vl1")]
        s_krl = newsem("krl")
        s_o = [newsem(f"o{k}") for k in range(4)]
        s_m = [newsem(f"m{k}") for k in range(4)]
        so_i = [0]
        sm_i = [0]

        def osem():
            so_i[0] += 1
            return s_o[so_i[0] % 4]

        def msem():
            sm_i[0] += 1
            return s_m[sm_i[0] % 4]

        pieceR = {}

        P.dma("sp", I("dma_start", out=pcol, in_=pcol_in[:, :]), [], [pcolR], s_in)
        P.dma("sp", I("dma_start", out=cst, in_=cst_in[:, :]), [], [cstR], s_in)
        P.op("dve", I("tensor_copy", out=ident, in_=cst[:, 0:128]), [cstR], [constR])
        P.op("dve", I("memset", onesD, 1.0 / D), [], [constR])
        P.op("dve", I("memset", onesQ, 1.0 / cfg.QL), [], [constR])
        P.op("dve", I("memset", onesK, 1.0 / cfg.KL), [], [constR])
        P.op("dve", I("memset", ones1, 1.0), [], [constR])
        for jj in range(NP):
            ii = 2 * jj + 1
            g0 = PL["g"] + (ii * 4 + 1) * DC
            P.op("dve", I("tensor_tensor", out=pgc[:, jj * DC:(jj + 1) * DC], in0=pcol[:, PL["psc"] + jj * DC:PL["psc"] + (jj + 1) * DC], in1=pcol[:, g0:g0 + DC], op=ALU.mult),
                 [pcolR], [pcolR])
        P.op("dve", I("memset", chist, 0.0), [], [r for l in chistR for r in l])
        P.op("dve", I("memset", phist, 0.0), [], [r for l in phistR for r in l])

        def gcol(i, n, c):
            k = PL["g"] + (i * 4 + n) * DC + c
            return pcol[:, k:k + 1]

        class WS:
            order = []
            issued = 0
            consumed = 0
            wb = 0
            seen = set()

        def ws_issue():
            k = WS.issued
            p = pieces[pidx[WS.order[k]]]
            s = k % NSLOT
            c0, c1 = p["off"], p["off"] + p["ncols"]
            li = p["layer"]
            nm = p["name"]
            if nm in pieceR:
                P.dma("sp", I("dma_start", out=ring[:, s, 0:p["ncols"]], in_=wbf[li][:, c0:c1]), [pieceR[nm]], [ringR[s]], ringS[s])
            else:
                P.dma("pool", I("dma_start", out=ring[:, s, 0:p["ncols"]], in_=wblob[li][:, c0:c1]), [], [ringR[s]], ringS[s])
                first = nm not in WS.seen
                WS.seen.add(nm)
                if nm in WS.order[k + 1:] and (not first or k % 2 == 0):
                    pieceR[nm] = Res()
                    P.dma("sp", I("dma_start", out=wbf[li][:, c0:c1], in_=ring[:, s, 0:p["ncols"]]), [ringR[s]], [pieceR[nm]], s_cast[WS.wb % 8])
                    WS.wb += 1
            WS.issued += 1

        def ws_next(name):
            k = WS.consumed
            assert WS.order[k] == name, (k, WS.order[k], name)
            while WS.issued < min(len(WS.order), k + NSLOT):
                ws_issue()
            WS.consumed += 1
            s = k % NSLOT
            return ring[:, s, :], ringR[s]

        def tile_order(sample):
            o = []
            for i in range(DEPTH):
                if i % 2 == 0:
                    pre = (f"dq{i}_", f"uqn{i}_", f"uqr{i}_", f"dkv{i}_", f"dkvr{i}")
                    for nm in pre:
                        o += [p["name"] for p in pieces if p["name"].startswith(nm)]
                    if sample:
                        o += [p["name"] for p in pieces if p["name"].startswith(f"ukT{i}_")]
                    else:
                        o += [p["name"] for p in pieces if p["name"].startswith(f"uk{i}_")]
                    o += [p["name"] for p in pieces if p["name"].startswith(f"uv{i}_")]
                    o += [p["name"] for p in pieces if p["name"].startswith(f"wo{i}_")]
                else:
                    o += [p["name"] for p in pieces if p["name"].startswith(f"pool{i}_")]
                o += [p["name"] for p in pieces if p["name"].startswith(f"up{i}_")]
                o += [p["name"] for p in pieces if p["name"].startswith(f"dn{i}_")]
            return o

        for t in range(NT):
            WS.order += tile_order(False)
        WS.order += tile_order(True)

        bank_i = [0]

        def nbank():
            b = bank_i[0] % 6
            bank_i[0] += 1
            return b

        sq_i = [0]

        def stats_begin():
            return dict(n=0, pend=None)

        def stats_add(S, src, srcR, N, ones, total, scale=None):
            k = sq_i[0] % 3
            sq_i[0] += 1
            kw = {}
            if scale is not None:
                kw["scale"] = scale
            P.op("act", I("activation", out=sqb[:, k, 0:N], in_=src, func=AF.Square, **kw), [srcR, pcolR], [sqbR[k]])
            stats_flush(S, N, ones, total)
            S["pend"] = k

        def stats_flush(S, N, ones, total):
            if S["pend"] is None:
                return
            k = S["pend"]
            n = S["n"]
            P.op("pe", I("matmul", PS[7][:, 0:N], ones, sqb[:, k, 0:N], start=(n == 0), stop=(n == total - 1)),
                 [sqbR[k], constR], [PSR[7]])
            S["n"] += 1
            S["pend"] = None

        rs_i = [0]

        def stats_end(S, N, ones, total):
            stats_flush(S, N, ones, total)
            assert S["n"] == total
            k = rs_i[0] % 2
            rs_i[0] += 1
            P.op("act", I("activation", out=rstd[:, k, 0:N], in_=PS[7][:, 0:N], func=AF.Sqrt, bias=EPS, scale=1.0), [PSR[7]], [rstdR[k]])
            P.op("dve", I("reciprocal", out=rstd[:, k, 0:N], in_=rstd[:, k, 0:N]), [rstdR[k]], [rstdR[k]])
            return rstd[:, k, 0:N], rstdR[k]

        def prenorm(i, n, N, out_fn):
            S = stats_begin()
            for c in range(DC):
                stats_add(S, xT[:, c, 0:N], xTR[c], N, onesD, DC)
            r, rR = stats_end(S, N, onesD, DC)
            for c in range(DC):
                o, oR = out_fn(c)
                P.op("dve", I("scalar_tensor_tensor", out=o, in0=xT[:, c, 0:N], scalar=gcol(i, n, c), in1=r, op0=ALU.mult, op1=ALU.mult),
                     [xTR[c], rR, pcolR], [oR])

        def post_residual(i, n, N, m, mR, S):
            r, rR = stats_end(S, N, onesD, DC)
            for c in range(DC):
                P.op(EW2, I("tensor_tensor", out=m[:, c, 0:N], in0=m[:, c, 0:N], in1=r, op=ALU.mult), [mR[c], rR], [mR[c]])
                P.op("dve", I("tensor_tensor", out=xT[:, c, 0:N], in0=xT[:, c, 0:N], in1=m[:, c, 0:N], op=ALU.add), [xTR[c], mR[c]], [xTR[c]])

        def evac_m(b, c, N, m, mR, S, gc, scale=None):
            P.op("act", I("activation", out=m[:, c, 0:N], in_=PS[b][:, 0:N], func=AF.Identity, scale=gc, bias=0.0), [PSR[b], pcolR], [mR[c]])
            stats_add(S, PS[b][:, 0:N], PSR[b], N, onesD, DC, scale=scale)

        def phase_end(mark):
            A.top = mark

        def ffn(i, N, nb, S_, hist, histR, last_out):
            mark0 = A.top
            actT = A.alloc((FP, N), BF16)
            actR = A.res(FP)
            mark1 = A.top
            hT = A.alloc((DC, N), BF16)
            hR = A.res(DC)
            ag = A.alloc((2, N), F32)
            agR = A.res(2)
            av = A.alloc((2, N), F32)
            avR = A.res(2)
            sg = A.alloc((2, N), F32)
            sgR = A.res(2)
            prenorm(i, 2, N, lambda c: (hT[:, c, :], hR[c]))
            cwb = PL["cw"] + i * 3 * 2 * FP
            cbb = PL["cb"] + i * 2 * FP

            def v3(ap):
                return ap.rearrange("p (b s) -> p b s", b=nb)

            g = max(1, SLOT_COLS // (DC * 256))
            pend = None

            def conv_pair(p, bg, bv, q):
                for (ch, b, a, aR) in ((p, bg, ag, agR), (FP + p, bv, av, avR)):
                    w0 = pcol[:, cwb + ch:cwb + ch + 1]
                    w1 = pcol[:, cwb + 2 * FP + ch:cwb + 2 * FP + ch + 1]
                    w2 = pcol[:, cwb + 4 * FP + ch:cwb + 4 * FP + ch + 1]
                    bb = pcol[:, cbb + ch:cbb + ch + 1]
                    P.op("act", I("activation", out=a[:, q, :], in_=PS[b][:, 0:N], func=AF.Identity, scale=w2, bias=bb), [PSR[b], pcolR], [aR[q]])
                for (ch, b, a, aR) in ((p, bg, ag, agR), (FP + p, bv, av, avR)):
                    w1 = pcol[:, cwb + 2 * FP + ch:cwb + 2 * FP + ch + 1]
                    a3, u3 = v3(a[:, q, :]), v3(PS[b][:, 0:N])
                    P.op("dve", I("scalar_tensor_tensor", out=a3[:, :, 1:S_], in0=u3[:, :, 0:S_ - 1], scalar=w1, in1=a3[:, :, 1:S_], op0=ALU.mult, op1=ALU.add),
                         [PSR[b], aR[q], pcolR], [aR[q]])
                for (ch, b, a, aR) in ((p, bg, ag, agR), (FP + p, bv, av, avR)):
                    w0 = pcol[:, cwb + ch:cwb + ch + 1]
                    a3, u3 = v3(a[:, q, :]), v3(PS[b][:, 0:N])
                    P.op("dve", I("scalar_tensor_tensor", out=a3[:, :, 2:S_], in0=u3[:, :, 0:S_ - 2], scalar=w0, in1=a3[:, :, 2:S_], op0=ALU.mult, op1=ALU.add),
                         [PSR[b], aR[q], pcolR], [aR[q]])
                for (ch, b, a, aR) in ((p, bg, ag, agR), (FP + p, bv, av, avR)):
                    w0 = pcol[:, cwb + ch:cwb + ch + 1]
                    a3 = v3(a[:, q, :])
                    P.op("dve", I("scalar_tensor_tensor", out=a3[:, :, 0:2], in0=hist[:, ch, :, 0:2], scalar=w0, in1=a3[:, :, 0:2], op0=ALU.mult, op1=ALU.add),
                         [histR[ch], aR[q], pcolR], [aR[q]])
                for (ch, b, a, aR) in ((p, bg, ag, agR), (FP + p, bv, av, avR)):
                    w1 = pcol[:, cwb + 2 * FP + ch:cwb + 2 * FP + ch + 1]
                    a3 = v3(a[:, q, :])
                    P.op("dve", I("scalar_tensor_tensor", out=a3[:, :, 0:1], in0=hist[:, ch, :, 1:2], scalar=w1, in1=a3[:, :, 0:1], op0=ALU.mult, op1=ALU.add),
                         [histR[ch], aR[q], pcolR], [aR[q]])
                for (ch, b, a, aR) in ((p, bg, ag, agR), (FP + p, bv, av, avR)):
                    u3 = v3(PS[b][:, 0:N])
                    P.op("dve", I("tensor_copy", out=hist[:, ch, :, :], in_=u3[:, :, S_ - 2:S_]), [PSR[b]], [histR[ch]])
                P.op("act", I("activation", out=sg[:, q, :], in_=ag[:, q, :], func=AF.Silu), [agR[q]], [sgR[q]])
                P.op("dve", I("tensor_tensor", out=actT[:, p, :], in0=sg[:, q, :], in1=av[:, q, :], op=ALU.mult), [sgR[q], avR[q]], [actR[p]])

            pi = 0
            for p0 in range(0, FP, g):
                slot, sR = ws_next(f"up{i}_{p0}")
                for k, p in enumerate(range(p0, min(FP, p0 + g))):
                    bg, bv = 2 * (pi % 3), 2 * (pi % 3) + 1
                    base = k * DC * 256
                    for (b, bo) in ((bg, base), (bv, base + DC * 128)):
                        for kc in range(DC):
                            P.op("pe", I("matmul", PS[b][:, 0:N], slot[:, bo + kc * 128:bo + (kc + 1) * 128], hT[:, kc, :], start=(kc == 0), stop=(kc == DC - 1)),
                                 [sR, hR[kc]], [PSR[b]])
                    if pend is not None:
                        conv_pair(*pend)
                    pend = (p, bg, bv, pi % 2)
                    pi += 1
            conv_pair(*pend)
            if last_out is not None:
                P.dma("sp", I("dma_start", out=last_out, in_=hist), list(histR), [Res()], msem())
            A.top = mark1
            m = A.alloc((DC, N), F32)
            mR = A.res(DC)
            S = stats_begin()
            nparts = -(-FP * 128 // SLOT_COLS)
            kper = -(-FP // nparts)
            pend = None
            for o in range(DC):
                b = nbank()
                for part in range(nparts):
                    slot, sR = ws_next(f"dn{i}_{o}_{part}")
                    k0, k1 = part * kper, min(FP, (part + 1) * kper)
                    for kc in range(k0, k1):
                        P.op("pe", I("matmul", PS[b][:, 0:N], slot[:, (kc - k0) * 128:(kc - k0 + 1) * 128], actT[:, kc, :], start=(kc == 0), stop=(kc == FP - 1)),
                             [sR, actR[kc]], [PSR[b]])
                if pend is not None:
                    evac_m(pend[0], pend[1], N, m, mR, S, gcol(i, 3, pend[1]))
                pend = (b, o)
            evac_m(pend[0], pend[1], N, m, mR, S, gcol(i, 3, pend[1]))
            post_residual(i, 3, N, m, mR, S)
            phase_end(mark0)

        def poolmix(i, N, nb, S_, hist, histR, first, last_out):
            j = i // 2
            mark0 = A.top
            L = 15 + S_
            hx = A.alloc((DC, nb, L), F32)
            hxR = A.res(DC)
            dT = A.alloc((DC, N), BF16)
            dR = A.res(DC)
            tA = A.alloc((2, nb, L), F32)
            tAR = A.res(2)
            tB = A.alloc((2, nb, L), F32)
            tBR = A.res(2)
            m = A.alloc((DC, N), F32)
            mR = A.res(DC)
            for c in range(DC):
                P.op("act", I("activation", out=hx[:, c, :, 0:15], in_=hist[:, c, :, :], func=AF.Copy), [histR[c]], [hxR[c]])
            if nb == 1:
                prenorm(i, 0, N, lambda c: (hx[:, c, 0, 15:L], hxR[c]))
            else:
                S = stats_begin()
                for c in range(DC):
                    stats_add(S, xT[:, c, 0:N], xTR[c], N, onesD, DC)
                r, rR = stats_end(S, N, onesD, DC)
                for c in range(DC):
                    P.op("dve", I("scalar_tensor_tensor", out=hx[:, c, :, 15:L], in0=xT[:, c, 0:N].rearrange("p (b s) -> p b s", b=nb), scalar=gcol(i, 0, c),
                                  in1=r.rearrange("p (b s) -> p b s", b=nb), op0=ALU.mult, op1=ALU.mult), [xTR[c], rR, pcolR], [hxR[c]])
            for c in range(DC):
                gi = c // cfg.GC
                w = 2 ** (gi + 1)
                q = c % 2
                src, srcR = hx[:, c], hxR[c]
                bufs = [(tA[:, q], tAR[q]), (tB[:, q], tBR[q])]
                sh = 1
                lv = 0
                while sh < w:
                    dst, dstR = bufs[lv % 2]
                    P.op("dve", I("tensor_tensor", out=dst[:, :, sh:L], in0=src[:, :, sh:L], in1=src[:, :, 0:L - sh], op=ALU.add), [srcR], [dstR])
                    if sh > 1:
                        pass
                    src, srcR = dst, dstR
                    sh *= 2
                    lv += 1
                d3 = dT[:, c, :].rearrange("p (b s) -> p b s", b=nb)
                P.op("dve", I("scalar_tensor_tensor", out=d3, in0=src[:, :, 15:L], scalar=1.0 / w, in1=hx[:, c, :, 15:L], op0=ALU.mult, op1=ALU.subtract),
                     [srcR, hxR[c]], [dR[c]])
                if first:
                    ic = cst[:, 128 + gi * 16:128 + gi * 16 + 15]
                    tmp, tmpR = bufs[lv % 2]
                    P.op("dve", I("tensor_tensor", out=tmp[:, 0, 0:15], in0=src[:, 0, 15:30], in1=ic, op=ALU.mult), [srcR, cstR], [tmpR])
                    P.op("dve", I("tensor_tensor", out=dT[:, c, 0:15], in0=tmp[:, 0, 0:15], in1=hx[:, c, 0, 15:30], op=ALU.subtract), [tmpR, hxR[c]], [dR[c]])
                P.op("act", I("activation", out=hist[:, c, :, :], in_=hx[:, c, :, S_:S_ + 15], func=AF.Copy), [hxR[c]], [histR[c]])
            if last_out is not None:
                P.dma("sp", I("dma_start", out=last_out, in_=hist), list(histR), [Res()], msem())
            GC = cfg.GC
            g = max(1, SLOT_COLS // (GC * GC * 128))
            S = stats_begin()
            pend = None
            for g0 in range(0, 4, g):
                slot, sR = ws_next(f"pool{i}_{g0}")
                for k, gg in enumerate(range(g0, min(4, g0 + g))):
                    for o in range(GC):
                        b = nbank()
                        base = k * GC * GC * 128 + o * GC * 128
                        for kc in range(GC):
                            P.op("pe", I("matmul", PS[b][:, 0:N], slot[:, base + kc * 128:base + (kc + 1) * 128], dT[:, gg * GC + kc, :], start=(kc == 0), stop=(kc == GC - 1)),
                                 [sR, dR[gg * GC + kc]], [PSR[b]])
                        c = gg * GC + o
                        if pend is not None:
                            evac_m(pend[0], pend[1], N, m, mR, S, pgc[:, j * DC + pend[1]:j * DC + pend[1] + 1], scale=pend[2])
                        kk = PL["psc"] + j * DC + c
                        pend = (b, c, pcol[:, kk:kk + 1])
            evac_m(pend[0], pend[1], N, m, mR, S, pgc[:, j * DC + pend[1]:j * DC + pend[1] + 1], scale=pend[2])
            post_residual(i, 1, N, m, mR, S)
            phase_end(mark0)

        def mla_proj(i, N, pos0, hT, hR, Qn, QnR, Qr, QrR, ckvb, ckvbR, krb, krbR, out_ckv, out_kr):
            j = i // 2
            cq = A.alloc((QC, N), F32)
            cqR = A.res(QC)
            cqn = A.alloc((QC, N), BF16)
            cqnR = A.res(QC)
            ckv = A.alloc((KC, N), F32)
            ckvR = A.res(KC)
            krf = A.alloc((N,), F32)
            krfR = A.res(1)[0]
            rtmp = A.alloc((2, N), F32)
            rtmpR = A.res(2)
            cs = A.alloc((2, N), F32)
            csR = A.res(1)[0]
            P.dma("sp", I("dma_start", out=cs[0:64, :, :], in_=rope_in[:, :, pos0:pos0 + N]), [], [csR], s_in)
            prenorm(i, 0, N, lambda c: (hT[:, c, :], hR[c]))

            def lin_group(prefix, nout, K, rhs, rhsR, evac):
                g = max(1, SLOT_COLS // (K * 128))
                pend = None
                for o0 in range(0, nout, g):
                    slot, sR = ws_next(f"{prefix}{i}_{o0}")
                    for k, o in enumerate(range(o0, min(nout, o0 + g))):
                        b = nbank()
                        for kc in range(K):
                            P.op("pe", I("matmul", PS[b][:, 0:N], slot[:, (k * K + kc) * 128:(k * K + kc + 1) * 128], rhs[:, kc, :], start=(kc == 0), stop=(kc == K - 1)),
                                 [sR, rhsR[kc]], [PSR[b]])
                        if pend is not None:
                            evac(*pend)
                        pend = (b, o)
                evac(*pend)

            S = stats_begin()

            def ev_cq(b, o):
                P.op("act", I("activation", out=cq[:, o, :], in_=PS[b][:, 0:N], func=AF.Copy), [PSR[b]], [cqR[o]])
                stats_add(S, PS[b][:, 0:N], PSR[b], N, onesQ, QC)
            lin_group("dq", QC, DC, hT, hR, ev_cq)
            r, rR = stats_end(S, N, onesQ, QC)
            for o in range(QC):
                kk = PL["qn"] + j * QC + o
                P.op("dve", I("scalar_tensor_tensor", out=cqn[:, o, :], in0=cq[:, o, :], scalar=pcol[:, kk:kk + 1], in1=r, op0=ALU.mult, op1=ALU.mult),
                     [cqR[o], rR, pcolR], [cqnR[o]])
            def ev_qn(b, h):
                P.op("act", I("activation", out=Qn[:, h, :], in_=PS[b][:, 0:N], func=AF.Copy), [PSR[b]], [QnR[h]])
            lin_group("uqn", NH, QC, cqn, cqnR, ev_qn)

            def rope_evac(bp, bs, dst, dstR, q):
                P.op("dve", I("tensor_tensor", out=rtmp[0:64, q, :], in0=PS[bp][0:64, 0:N], in1=cs[0:64, 0, :], op=ALU.mult), [PSR[bp], csR], [rtmpR[q]])
                P.op("dve", I("tensor_tensor", out=sgt[0:64, q, :], in0=PS[bs][0:64, 0:N], in1=cs[0:64, 1, :], op=ALU.mult), [PSR[bs], csR], [sgtR[q]])
                P.op("dve", I("tensor_tensor", out=dst, in0=rtmp[0:64, q, :], in1=sgt[0:64, q, :], op=ALU.add), [rtmpR[q], sgtR[q]], [dstR])

            sgt = A.alloc((2, N), F32)
            sgtR = A.res(2)
            g = max(1, SLOT_COLS // (QC * 128))
            pend = None
            hi = 0
            for h0 in range(0, NH, g):
                slot, sR = ws_next(f"uqr{i}_{h0}")
                for k, h in enumerate(range(h0, min(NH, h0 + g))):
                    bp, bs = nbank(), nbank()
                    base = k * QC * 128
                    for (b, bo) in ((bp, base), (bs, base + QC * 64)):
                        for kc in range(QC):
                            P.op("pe", I("matmul", PS[b][0:64, 0:N], slot[:, bo + kc * 64:bo + (kc + 1) * 64], cqn[:, kc, :], start=(kc == 0), stop=(kc == QC - 1)),
                                 [sR, cqnR[kc]], [PSR[b]])
                    if pend is not None:
                        rope_evac(*pend)
                    pend = (bp, bs, Qr[0:64, h, :], QrR[h], hi % 2)
                    hi += 1
            rope_evac(*pend)
            S = stats_begin()

            def ev_kv(b, o):
                P.op("act", I("activation", out=ckv[:, o, :], in_=PS[b][:, 0:N], func=AF.Copy), [PSR[b]], [ckvR[o]])
                stats_add(S, PS[b][:, 0:N], PSR[b], N, onesK, KC)
            lin_group("dkv", KC, DC, hT, hR, ev_kv)
            r, rR = stats_end(S, N, onesK, KC)
            for o in range(KC):
                kk = PL["kn"] + j * KC + o
                P.op("dve", I("scalar_tensor_tensor", out=ckv[:, o, :], in0=ckv[:, o, :], scalar=pcol[:, kk:kk + 1], in1=r, op0=ALU.mult, op1=ALU.mult),
                     [ckvR[o], rR, pcolR], [ckvR[o]])
                P.op("act", I("activation", out=ckvb[:, o, :], in_=ckv[:, o, :], func=AF.Copy), [ckvR[o]], [ckvbR[o]])
            P.dma("sp", I("dma_start", out=out_ckv, in_=ckv), list(ckvR), [Res()], osem())
            slot, sR = ws_next(f"dkvr{i}")
            bp, bs = nbank(), nbank()
            for (b, bo) in ((bp, 0), (bs, DC * 64)):
                for kc in range(DC):
                    P.op("pe", I("matmul", PS[b][0:64, 0:N], slot[:, bo + kc * 64:bo + (kc + 1) * 64], hT[:, kc, :], start=(kc == 0), stop=(kc == DC - 1)),
                         [sR, hR[kc]], [PSR[b]])
            rope_evac(bp, bs, krf[0:64, :], krfR, 0)
            P.op("act", I("activation", out=krb[0:64, :], in_=krf[0:64, :], func=AF.Copy), [krfR], [krbR])
            P.dma("sp", I("dma_start", out=out_kr, in_=krf[0:64, :]), [krfR], [Res()], osem())

        kTdR = [[Res() for _ in range(NH)] for _ in range(max(NM, 1))]
        vdR = [[Res() for _ in range(NH)] for _ in range(max(NM, 1))]
        krdR = [Res() for _ in range(max(NM, 1))]

        def mla_prompt(i, t):
            j = i // 2
            N = T
            t0 = t * T
            mark0 = A.top
            Qn = A.alloc((NH, N), BF16)
            QnR = A.res(NH)
            Qr = A.alloc((NH, N), BF16)
            QrR = A.res(NH)
            for hh in range(NH):
                P.op("dve", I("memset", Qr[64:128, hh, :], 0.0), [], [QrR[hh]])
            markA = A.top
            hT = A.alloc((DC, N), BF16)
            hR = A.res(DC)
            ckvb = A.alloc((KC, N), BF16)
            ckvbR = A.res(KC)
            krb = A.alloc((N,), BF16)
            krbR = A.res(1)[0]
            gk = max(1, SLOT_COLS // (KC * 128))
            Kst = A.alloc((2, N), BF16)
            KstR = A.res(2)
            Vsb = A.alloc((min(NH, gk), 4, 128), BF16)
            VsbR = A.res(1)[0]
            mla_proj(i, N, t0, hT, hR, Qn, QnR, Qr, QrR, ckvb, ckvbR, krb, krbR,
                     ockvT[j].rearrange("(c p) t -> p c t", p=128)[:, :, t0:t0 + N], okrT[j][:, t0:t0 + N])
            P.dma("sp", I("dma_start", out=krT_d[j][:, t0:t0 + N], in_=krb[0:64, :]), [krbR], [krdR[j]], s_krst)
            g = max(1, SLOT_COLS // (KC * 128))
            pend = None

            def ev_k(b, h):
                q = h % 2
                P.op("act", I("activation", out=Kst[:, q, :], in_=PS[b][:, 0:N], func=AF.Copy), [PSR[b]], [KstR[q]])
                P.dma("sp", I("dma_start", out=kT_d[j][h][:, t0:t0 + N], in_=Kst[:, q, :]), [KstR[q]], [kTdR[j][h]], s_kst[q])
            for h0 in range(0, NH, g):
                slot, sR = ws_next(f"uk{i}_{h0}")
                for k, h in enumerate(range(h0, min(NH, h0 + g))):
                    b = nbank()
                    for kc in range(KC):
                        P.op("pe", I("matmul", PS[b][:, 0:N], slot[:, (k * KC + kc) * 128:(k * KC + kc + 1) * 128], ckvb[:, kc, :], start=(kc == 0), stop=(kc == KC - 1)),
                             [sR, ckvbR[kc]], [PSR[b]])
                    if pend is not None:
                        ev_k(*pend)
                    pend = (b, h)
            ev_k(*pend)
            pend = None

            def ev_v(b, tc, hs0, nh):
                P.op("act", I("activation", out=Vsb[:, hs0:hs0 + nh, tc, :], in_=PS[b][:, 0:nh * 128].rearrange("p (h d) -> p h d", h=nh), func=AF.Copy), [PSR[b]], [VsbR])
            for h0 in range(0, NH, g):
                slot, sR = ws_next(f"uv{i}_{h0}")
                nhp = min(NH, h0 + g) - h0
                for hs0 in range(0, nhp, 4):
                    nh = min(4, nhp - hs0)
                    for tc in range(N // 128):
                        b = nbank()
                        for kc in range(KC):
                            P.op("pe", I("matmul", PS[b][:, 0:nh * 128], ckvb[:, kc, tc * 128:(tc + 1) * 128],
                                         slot[:, kc * nhp * 128 + hs0 * 128:kc * nhp * 128 + (hs0 + nh) * 128], start=(kc == 0), stop=(kc == KC - 1)),
                                 [sR, ckvbR[kc]], [PSR[b]])
                        if pend is not None:
                            ev_v(*pend)
                        pend = (b, tc, hs0, nh)
                ev_v(*pend)
                pend = None
                P.dma("sp", I("dma_start", out=v_d[j].rearrange("h p k d -> p h k d")[:, h0:h0 + nhp, 4 * t:4 * t + 4, :], in_=Vsb[:, 0:nhp]), [VsbR],
                      [vdR[j][hh] for hh in range(h0, h0 + nhp)], s_vst)
            A.top = markA
            Osb = A.alloc((NH, N), BF16)
            OsbR = A.res(NH)
            markB = A.top
            nk = 4 * (t + 1)
            Ksl = A.alloc((2, nk * 128), BF16)
            KslR = A.res(2)
            Vsl = A.alloc((2, nk, 128), BF16)
            VslR = A.res(2)
            krT = A.alloc((nk * 128,), BF16)
            krTR = A.res(1)[0]
            LOOK = 3
            SB = [0, 1, 2, 5, 6]
            NPT = LOOK + 2
            Pt = A.alloc((NPT, N), BF16)
            PtR = A.res(NPT)
            rl = A.alloc((1, N), F32)
            rlR = A.res(1)
            acc = A.alloc((2, N), F32)
            accR = A.res(2)
            accb = A.alloc((2, N), BF16)
            accbR = A.res(2)
            P.op("dve", I("memset", krT[64:128, :], 0.0), [], [krTR])
            P.dma("sp", I("dma_start", out=krT[0:64, :], in_=krT_d[j][:, 0:nk * 128]), [krdR[j]], [krTR], s_krl)

            def load_head(h):
                q = h % 2
                P.dma("sp", I("dma_start", out=Ksl[:, q, :], in_=kT_d[j][h][:, 0:nk * 128]), [kTdR[j][h]], [KslR[q]], s_kl[q])
                P.dma("sp", I("dma_start", out=Vsl[:, q, :, :], in_=v_d[j][h][:, 0:nk, :]), [vdR[j][h]], [VslR[q]], s_vl[q])
            load_head(0)
            pi = 0
            for h in range(NH):
                if h + 1 < NH:
                    load_head(h + 1)
                q = h % 2
                bo, bl = 3 + q, 7

                def cols(kc):
                    jj = kc - 4 * t
                    return (0 if jj < 0 else 128 * jj), jj

                def s_mm(kc, sb):
                    c0, jj = cols(kc)
                    P.op("pe", I("matmul", PS[sb][:, c0:N], Ksl[:, q, kc * 128:(kc + 1) * 128], Qn[:, h, c0:N], start=True, stop=False), [KslR[q], QnR[h]], [PSR[sb]])
                    P.op("pe", I("matmul", PS[sb][:, c0:N], krT[:, kc * 128:(kc + 1) * 128], Qr[:, h, c0:N], start=False, stop=True), [krTR, QrR[h]], [PSR[sb]])

                def p_ev(kc, sb, pb):
                    c0, jj = cols(kc)
                    P.op("act", I("activation", out=Pt[:, pb, c0:N], in_=PS[sb][:, c0:N], func=AF.Exp, scale=cfg.scale), [PSR[sb]], [PtR[pb]])
                    if jj >= 0:
                        P.op("dve", I("memset", Pt[64:128, pb, c0:c0 + 64], 0.0), [], [PtR[pb]])

                def pv_mm(kc, pb):
                    c0, jj = cols(kc)
                    P.op("pe", I("matmul", PS[bo][:, c0:N], Vsl[:, q, kc, :], Pt[:, pb, c0:N], start=(kc == 0), stop=(kc == nk - 1)), [VslR[q], PtR[pb]], [PSR[bo]])
                    if kc == 0:
                        P.op("dve", I("tensor_copy", out=acc[:, q, :], in_=Pt[:, pb, :]), [PtR[pb]], [accR[q]])
                    else:
                        P.op("dve", I("tensor_tensor", out=acc[:, q, c0:N], in0=acc[:, q, c0:N], in1=Pt[:, pb, c0:N], op=ALU.add), [PtR[pb], accR[q]], [accR[q]])
                for kc in range(min(LOOK, nk)):
                    s_mm(kc, SB[(pi + kc) % len(SB)])
                    p_ev(kc, SB[(pi + kc) % len(SB)], (pi + kc) % NPT)
                for kc in range(0, nk, 2):
                    nxt = [k2 for k2 in (kc + LOOK, kc + LOOK + 1) if k2 < nk]
                    for k2 in nxt:
                        s_mm(k2, SB[(pi + k2) % len(SB)])
                    for k2 in nxt:
                        p_ev(k2, SB[(pi + k2) % len(SB)], (pi + k2) % NPT)
                    for k2 in (kc, kc + 1):
                        if k2 < nk:
                            pv_mm(k2, (pi + k2) % NPT)
                pi += nk
                P.op("act", I("activation", out=accb[:, q, :], in_=acc[:, q, :], func=AF.Copy), [accR[q]], [accbR[q]])
                P.op("pe", I("matmul", PS[bl][:, 0:N], ones1, accb[:, q, :], start=True, stop=True), [constR, accbR[q]], [PSR[bl]])
                P.op("dve", I("reciprocal", out=rl[:, 0, :], in_=PS[bl][:, 0:N]), [PSR[bl]], [rlR[0]])
                P.op("dve", I("tensor_tensor", out=Osb[:, h, :], in0=PS[bo][:, 0:N], in1=rl[:, 0, :], op=ALU.mult), [PSR[bo], rlR[0]], [OsbR[h]])
            A.top = markB
            mla_out(i, N, Osb, OsbR)
            phase_end(mark0)

        def mla_out(i, N, Osb, OsbR):
            m = A.alloc((DC, N), F32)
            mR = A.res(DC)
            S = stats_begin()
            g = max(1, SLOT_COLS // (NH * 128))
            pend = None
            for o0 in range(0, DC, g):
                slot, sR = ws_next(f"wo{i}_{o0}")
                for k, o in enumerate(range(o0, min(DC, o0 + g))):
                    b = nbank()
                    for h in range(NH):
                        P.op("pe", I("matmul", PS[b][:, 0:N], slot[:, (k * NH + h) * 128:(k * NH + h + 1) * 128], Osb[:, h, :], start=(h == 0), stop=(h == NH - 1)),
                             [sR, OsbR[h]], [PSR[b]])
                    if pend is not None:
                        evac_m(pend[0], pend[1], N, m, mR, S, gcol(i, 1, pend[1]))
                    pend = (b, o)
            evac_m(pend[0], pend[1], N, m, mR, S, gcol(i, 1, pend[1]))
            post_residual(i, 1, N, m, mR, S)

        def mla_sample(i):
            j = i // 2
            N = NS
            mark0 = A.top
            Qn = A.alloc((NH, N), BF16)
            QnR = A.res(NH)
            Qr = A.alloc((NH, N), BF16)
            QrR = A.res(NH)
            hT = A.alloc((DC, N), BF16)
            hR = A.res(DC)
            ckvb = A.alloc((KC, N), BF16)
            ckvbR = A.res(KC)
            krb = A.alloc((N,), BF16)
            krbR = A.res(1)[0]
            Osb = A.alloc((NH, N), BF16)
            OsbR = A.res(NH)
            QL = A.alloc((KC, NH, N), BF16)
            QLR = A.res(KC)
            OL = A.alloc((KC, NBS, NH, SS), BF16)
            OLR = A.res(KC)
            cnat = A.alloc((NBS, cfg.KL), BF16)
            cnatR = A.res(NBS)
            CK = A.alloc((PKC, cfg.KL), BF16)
            CKR = A.res(1)[0]
            CKT = A.alloc((KC, cfg.PAST), BF16)
            CKTR = A.res(1)[0]
            KRT = A.alloc((cfg.PAST,), BF16)
            KRTR = A.res(1)[0]
            Pt = A.alloc((4, NH * SS), BF16)
            PtR = A.res(4)
            rl = A.alloc((NH * SS,), F32)
            rlR = A.res(1)[0]
            s_c = [newsem(f"sc{i}_{k}") for k in range(3)]
            mla_proj(i, N, SEQ, hT, hR, Qn, QnR, Qr, QrR, ckvb, ckvbR, krb, krbR,
                     ockvTs[j].rearrange("(c p) t -> p c t", p=128), okrTs[j][:, :])
            g2 = max(1, SLOT_COLS // cfg.KL)
            pend = None

            def ev_ql(b, cc, h):
                P.op("act", I("activation", out=QL[:, cc, h, :], in_=PS[b][:, 0:N], func=AF.Copy), [PSR[b]], [QLR[cc]])
            for h0 in range(0, NH, g2):
                slot, sR = ws_next(f"ukT{i}_{h0}")
                for k, h in enumerate(range(h0, min(NH, h0 + g2))):
                    for cc in range(KC):
                        b = nbank()
                        P.op("pe", I("matmul", PS[b][:, 0:N], slot[:, k * cfg.KL + cc * 128:k * cfg.KL + (cc + 1) * 128], Qn[:, h, :], start=True, stop=True), [sR, QnR[h]], [PSR[b]])
                        if pend is not None:
                            ev_ql(*pend)
                        pend = (b, cc, h)
            ev_ql(*pend)
            for bt in range(NBS):
                b = nbank()
                for cc in range(KC):
                    P.op("pe", I("matmul", PS[b][0:SS, cc * 128:(cc + 1) * 128], ckvb[:, cc, bt * SS:(bt + 1) * SS], ident, start=True, stop=True), [ckvbR[cc], constR], [PSR[b]])
                P.op("act", I("activation", out=cnat[0:SS, bt, :], in_=PS[b][0:SS, 0:cfg.KL], func=AF.Copy), [PSR[b]], [cnatR[bt]])
            NQ = NH * SS
            pi = 0
            for bt in range(NBS):
                P.dma("pool", I("dma_start", out=CK, in_=cck_in[j][bt].rearrange("(k p) c -> p k c", p=128)), [], [CKR], s_c[0])
                P.dma("pool", I("dma_start", out=CKT, in_=cckT_in[j][bt].rearrange("(k p) t -> p k t", p=128)), [], [CKTR], s_c[1])
                P.dma("pool", I("dma_start", out=KRT[0:64, :], in_=ckrT_in[j][bt]), [], [KRTR], s_c[2])
                nk = PKC + 1
                assert KC * NQ <= 1024 and NQ <= 512

                def oacc(cc):
                    fo = cc * NQ
                    return PS[3 + fo // 512][:, fo % 512:fo % 512 + NQ], PSR[3 + fo // 512]

                def s_mm(kc, sb):
                    M = 128 if kc < PKC else SS
                    for cc in range(KC):
                        lhs = CKT[:, cc, kc * 128:(kc + 1) * 128] if kc < PKC else ckvb[:, cc, bt * SS:(bt + 1) * SS]
                        P.op("pe", I("matmul", PS[sb][0:M, 0:NQ].rearrange("p (h s) -> p h s", h=NH), lhs, QL[:, cc, :, bt * SS:(bt + 1) * SS], start=(cc == 0), stop=False),
                             [CKTR, ckvbR[cc], QLR[cc]], [PSR[sb]])
                    lhs = KRT[0:64, kc * 128:(kc + 1) * 128] if kc < PKC else krb[0:64, bt * SS:(bt + 1) * SS]
                    P.op("pe", I("matmul", PS[sb][0:M, 0:NQ].rearrange("p (h s) -> p h s", h=NH), lhs, Qr[0:64, :, bt * SS:(bt + 1) * SS], start=False, stop=True),
                         [KRTR, krbR] + QrR, [PSR[sb]])

                def p_ev(kc, sb, pb):
                    M = 128 if kc < PKC else SS
                    P.op("act", I("activation", out=Pt[0:M, pb, :], in_=PS[sb][0:M, 0:NQ], func=AF.Exp, scale=cfg.scale), [PSR[sb]], [PtR[pb]])

                def pv_mm(kc, pb):
                    M = 128 if kc < PKC else SS
                    for cc in range(KC):
                        lhs = CK[:, kc, cc * 128:(cc + 1) * 128] if kc < PKC else cnat[0:SS, bt, cc * 128:(cc + 1) * 128]
                        oa, oaR = oacc(cc)
                        P.op("pe", I("matmul", oa, lhs, Pt[0:M, pb, :], start=(kc == 0 and (cc * NQ) % 512 == 0), stop=(kc == nk - 1), skip_group_check=True),
                             [CKR, cnatR[bt], PtR[pb]], [oaR])
                    P.op("pe", I("matmul", PS[5][:, 0:NQ], ones1[0:M, :], Pt[0:M, pb, :], start=(kc == 0), stop=(kc == nk - 1)), [constR, PtR[pb]], [PSR[5]])
                LOOK = 2
                for kc in range(min(LOOK, nk)):
                    s_mm(kc, (pi + kc) % 3)
                    p_ev(kc, (pi + kc) % 3, (pi + kc) % 4)
                for kc in range(nk):
                    if kc + LOOK < nk:
                        s_mm(kc + LOOK, (pi + kc + LOOK) % 3)
                        p_ev(kc + LOOK, (pi + kc + LOOK) % 3, (pi + kc + LOOK) % 4)
                    pv_mm(kc, (pi + kc) % 4)
                pi += nk
                P.op("dve", I("reciprocal", out=rl, in_=PS[5][:, 0:NQ]), [PSR[5]], [rlR])
                for cc in range(KC):
                    oa, oaR = oacc(cc)
                    P.op("dve", I("tensor_tensor", out=OL[:, cc, bt, :, :], in0=oa.rearrange("p (h s) -> p h s", h=NH), in1=rl.rearrange("p (h s) -> p h s", h=NH), op=ALU.mult),
                         [oaR, rlR], [OLR[cc]])
            g = max(1, SLOT_COLS // (KC * 128))
            pend = None

            def ev_o(b, h):
                P.op("act", I("activation", out=Osb[:, h, :], in_=PS[b][:, 0:N], func=AF.Copy), [PSR[b]], [OsbR[h]])
            bank_i[0] = 0
            for h0 in range(0, NH, g):
                slot, sR = ws_next(f"uv{i}_{h0}")
                nhp = min(NH, h0 + g) - h0
                for k, h in enumerate(range(h0, h0 + nhp)):
                    b = nbank() % 3
                    for cc in range(KC):
                        P.op("pe", I("matmul", PS[b][:, 0:N].rearrange("p (b s) -> p b s", b=NBS), slot[:, cc * nhp * 128 + k * 128:cc * nhp * 128 + (k + 1) * 128],
                                     OL[:, cc, :, h, :], start=(cc == 0), stop=(cc == KC - 1)), [sR, OLR[cc]], [PSR[b]])
                    if pend is not None:
                        ev_o(*pend)
                    pend = (b, h)
            ev_o(*pend)
            mla_out(i, N, Osb, OsbR)
            phase_end(mark0)

        xv = xT_in.rearrange("(c p) t -> p c t", p=128)
        yv = yT.rearrange("(c p) t -> p c t", p=128)
        for t in range(NT):
            t0 = t * T
            for c in range(DC):
                P.dma("sp", I("dma_start", out=xT[:, c, :], in_=xv[:, c, t0:t0 + T]), [], [xTR[c]], s_xy[c])
            for i in range(DEPTH):
                j = i // 2
                last = (t == NT - 1)
                if i % 2 == 0:
                    mla_prompt(i, t)
                else:
                    hv = phist[:, j].rearrange("p c (b k) -> p c b k", b=1)
                    poolmix(i, T, 1, T, hv, phistR[j], t == 0,
                            opool[j].rearrange("p c (b k) -> p c b k", b=1) if last else None)
                hv = chist[:, i].rearrange("p c (b k) -> p c b k", b=1)
                ffn(i, T, 1, T, hv, chistR[i],
                    oconv[i].rearrange("p c (b k) -> p c b k", b=1) if last else None)
            for c in range(DC):
                P.dma("sp", I("dma_start", out=yv[:, c, t0:t0 + T], in_=xT[:, c, :]), [xTR[c]], [Res()], s_xy[c])
        for c in range(DC):
            P.dma("sp", I("dma_start", out=xT[:, c, 0:NS], in_=xsT_in.rearrange("(c p) t -> p c t", p=128)[:, c, :]), [], [xTR[c]], s_xy[c])
        for i in range(DEPTH):
            j = i // 2
            if i % 2 == 0:
                mla_sample(i)
            else:
                mk = A.top
                sh = A.alloc((DC, NBS, 15), F32)
                shR = A.res(DC)
                P.dma("sp", I("dma_start", out=sh, in_=spool_in[j]), [], list(shR), s_in)
                poolmix(i, NS, NBS, SS, sh, shR, False, opools[j])
                A.top = mk
            mk = A.top
            ch = A.alloc((2 * FP, NBS, 2), F32)
            chR = A.res(2 * FP)
            P.dma("sp", I("dma_start", out=ch, in_=sconv_in[i]), [], list(chR), s_in)
            ffn(i, NS, NBS, SS, ch, chR, oconvs[i])
            A.top = mk
        for c in range(DC):
            P.dma("sp", I("dma_start", out=ysT.rearrange("(c p) t -> p c t", p=128)[:, c, :], in_=xT[:, c, 0:NS]), [xTR[c]], [Res()], s_xy[c])
        assert WS.consumed == len(WS.order), (WS.consumed, len(WS.order))
        P.barrier(final=True)
        P.finish()

        with nc.allow_non_contiguous_dma(reason="tiny state rows"), nc.Block() as block:
            @block.tensor
            def _(e):
                P.replay("pe", e, esem)

            @block.scalar
            def _(e):
                P.replay("act", e, esem)

            @block.vector
            def _(e):
                P.replay("dve", e, esem)

            @block.gpsimd
            def _(e):
                P.replay("pool", e, esem)

            @block.sync
            def _(e):
                P.replay("sp", e, esem)
    return nc


def _cols(v):
    v = np.asarray(v, np.float32).reshape(-1, 128)
    return np.ascontiguousarray(v.T)


def prep_inputs(cfg, inp):
    pieces, TOT = weight_plan(cfg)
    wblob = [np.empty((128, TOT[i]), np.float32) for i in range(cfg.DEPTH)]
    for p in pieces:
        a = p["build"](inp)
        assert a.shape == (128, p["ncols"]), (p["name"], a.shape, p["ncols"])
        wblob[p["layer"]][:, p["off"]:p["off"] + p["ncols"]] = a
    PL, NPC = pcol_layout(cfg)
    pcol = np.zeros((128, NPC), np.float32)
    DC, FP = cfg.DC, cfg.FP
    pcol[:, PL["g"]:PL["g"] + cfg.DEPTH * 4 * DC] = _cols(inp["norm_g"])
    if cfg.NM:
        pcol[:, PL["qn"]:PL["qn"] + cfg.NM * cfg.QC] = _cols(inp["mla_q_norm"])
        pcol[:, PL["kn"]:PL["kn"] + cfg.NM * cfg.KC] = _cols(inp["mla_kv_norm"])
    if cfg.NP:
        pcol[:, PL["psc"]:PL["psc"] + cfg.NP * DC] = _cols(inp["pool_scale"])
    pcol[:, PL["cw"]:PL["cw"] + cfg.DEPTH * 3 * 2 * FP] = _cols(inp["ffn_conv_w"])
    pcol[:, PL["cb"]:PL["cb"] + cfg.DEPTH * 2 * FP] = _cols(inp["ffn_conv_b"])
    cst = np.zeros((128, 256), np.float32)
    cst[:, 0:128] = np.eye(128, dtype=np.float32)
    for gi in range(4):
        w = 2 ** (gi + 1)
        cst[:, 128 + gi * 16:128 + gi * 16 + 16] = (1.0 / np.minimum(np.float32(w), np.arange(16, dtype=np.float32) + 1.0))[None, :]
    half = 32
    inv = (np.float32(ROPE_BASE) ** (-np.arange(half, dtype=np.float32) / np.float32(half))).astype(np.float32)
    pos = np.concatenate([np.arange(cfg.SEQ, dtype=np.float32)] + [cfg.PAST + np.arange(cfg.SS, dtype=np.float32)] * cfg.NBS)
    ang = (pos[:, None] * inv[None, :]).astype(np.float32)
    c, s = np.cos(ang).astype(np.float32).T, np.sin(ang).astype(np.float32).T
    rope = np.stack([np.concatenate([c, c], 0), np.concatenate([-s, s], 0)], axis=1)
    rope = np.ascontiguousarray(rope, np.float32)
    maps = []
    NBS = cfg.NBS
    for core in range(NCORES):
        sb = slice(core * NBS, (core + 1) * NBS)
        m = dict(pcol=pcol, cst=cst, rope=rope)
        for i in range(cfg.DEPTH):
            m[f"wblob{i}"] = wblob[i]
        m["xT"] = np.ascontiguousarray(inp["x_prompt"][core].T)
        m["xsT"] = np.ascontiguousarray(inp["x_sample"][sb].reshape(cfg.NS, cfg.D).T)
        if cfg.NM:
            ck = inp["cache_ckv"][:, sb]
            m["cck"] = np.ascontiguousarray(ck)
            m["cckT"] = np.ascontiguousarray(ck.transpose(0, 1, 3, 2))
            m["ckrT"] = np.ascontiguousarray(inp["cache_krope"][:, sb].transpose(0, 1, 3, 2))
        else:
            m["cck"] = np.zeros((1, NBS, cfg.PAST, cfg.KL), np.float32)
            m["cckT"] = np.zeros((1, NBS, cfg.KL, cfg.PAST), np.float32)
            m["ckrT"] = np.zeros((1, NBS, 64, cfg.PAST), np.float32)
        if cfg.NP:
            m["spool"] = np.ascontiguousarray(inp["state_pool"][:, sb].reshape(cfg.NP, NBS, 15, cfg.DC, 128).transpose(0, 4, 3, 1, 2))
        else:
            m["spool"] = np.zeros((1, 128, cfg.DC, NBS, 15), np.float32)
        m["sconv"] = np.ascontiguousarray(inp["state_conv"][:, sb].reshape(cfg.DEPTH, NBS, 2, 2 * cfg.FP, 128).transpose(0, 4, 3, 1, 2))
        maps.append(m)
    return maps


def gather_outputs(cfg, res):
    R = res
    NM, NP = cfg.NM, cfg.NP
    y = np.stack([r["yT"].T for r in R])
    ys = np.concatenate([r["ysT"].T.reshape(cfg.NBS, cfg.SS, cfg.D) for r in R], 0)
    ckv_p = np.stack([r["ockvT"][:NM].transpose(0, 2, 1) for r in R], 1)
    kr_p = np.stack([r["okrT"][:NM].transpose(0, 2, 1) for r in R], 1)
    pool_p = np.stack([r["opool"][:NP].transpose(0, 3, 2, 1).reshape(NP, 15, cfg.D) for r in R], 1)
    conv_p = np.stack([r["oconv"].transpose(0, 3, 2, 1).reshape(cfg.DEPTH, 2, 2 * cfg.DFF) for r in R], 1)
    ckv_s = np.concatenate([r["ockvTs"][:NM].transpose(0, 2, 1).reshape(NM, cfg.NBS, cfg.SS, cfg.KL) for r in R], 1)
    kr_s = np.concatenate([r["okrTs"][:NM].transpose(0, 2, 1).reshape(NM, cfg.NBS, cfg.SS, 64) for r in R], 1)
    pool_s = np.concatenate([r["opools"][:NP].transpose(0, 3, 4, 2, 1).reshape(NP, cfg.NBS, 15, cfg.D) for r in R], 1)
    conv_s = np.concatenate([r["oconvs"].transpose(0, 3, 4, 2, 1).reshape(cfg.DEPTH, cfg.NBS, 2, 2 * cfg.DFF) for r in R], 1)
    outs = (y, ys, ckv_p, kr_p, pool_p, conv_p, ckv_s, kr_s, pool_s, conv_s)
    return tuple(np.ascontiguousarray(o, dtype=np.float32) for o in outs)


def run_cfg(cfg, inputs):
    inp = {k: np.asarray(v) for k, v in inputs.items()}
    maps = prep_inputs(cfg, inp)
    nc = build_program(cfg)
    res = run_bass_kernel_spmd(nc, maps, core_ids=list(range(NCORES)))
    return gather_outputs(cfg, res.results)


def kernel(**inputs):
    return run_cfg(Cfg(), inputs)
```

```python
import numpy as np
import concourse.bass as bass
import concourse.mybir as mybir
from concourse.bass_utils import run_bass_kernel_spmd

F32 = mybir.dt.float32
BF16 = mybir.dt.bfloat16
AF = mybir.ActivationFunctionType
ALU = mybir.AluOpType

EPS = 1e-6
ROPE_BASE = 10000.0
NCORES = 8
SLOT_COLS = 4096
NSLOT = 6
CAST_COLS = 16384


class Cfg:
    def __init__(self, D=2048, SEQ=4096, NH=16, QL=512, KL=512, DFF=5632, DEPTH=4,
                 NBS=4, SS=16, PAST=2048, T=512):
        self.D, self.SEQ, self.NH, self.QL, self.KL, self.DFF, self.DEPTH = D, SEQ, NH, QL, KL, DFF, DEPTH
        self.NBS, self.SS, self.PAST, self.T = NBS, SS, PAST, T
        self.DC, self.QC, self.KC, self.FP = D // 128, QL // 128, KL // 128, DFF // 128
        self.NT = SEQ // T
        self.NM, self.NP = (DEPTH + 1) // 2, DEPTH // 2
        self.NS = NBS * SS
        self.PKC = PAST // 128
        self.GC = self.DC // 4
        self.scale = float((128 + 64) ** -0.5)
        assert self.DC % 4 == 0 and SEQ % T == 0 and T == 512


def _blocks_piece(W, K, blocks):
    W3 = W.reshape(K, 128, W.shape[1])
    outs = [np.ascontiguousarray(W3[:, :, c0:c0 + M].transpose(1, 0, 2)).reshape(128, K * M) for c0, M in blocks]
    return np.concatenate(outs, axis=1)


def weight_plan(cfg):
    P = []
    D, DC, NH, QC, KC, FP = cfg.D, cfg.DC, cfg.NH, cfg.QC, cfg.KC, cfg.FP

    cur = [0]

    def add(name, ncols, fn):
        assert ncols <= SLOT_COLS, (name, ncols)
        P.append(dict(name=name, ncols=ncols, build=fn, layer=cur[0]))

    for i in range(cfg.DEPTH):
        cur[0] = i
        j = i // 2
        if i % 2 == 0:
            g = max(1, SLOT_COLS // (DC * 128))
            for o0 in range(0, QC, g):
                oc = list(range(o0, min(QC, o0 + g)))
                add(f"dq{i}_{o0}", len(oc) * DC * 128,
                    lambda inp, j=j, oc=oc: _blocks_piece(inp["mla_w_dq"][j], DC, [(o * 128, 128) for o in oc]))
            g = max(1, SLOT_COLS // (QC * 128))
            for h0 in range(0, NH, g):
                hs = list(range(h0, min(NH, h0 + g)))
                add(f"uqn{i}_{h0}", len(hs) * QC * 128,
                    lambda inp, j=j, hs=hs: _blocks_piece(inp["mla_w_uq"][j], QC, [(h * 192, 128) for h in hs]))
            for h0 in range(0, NH, g):
                hs = list(range(h0, min(NH, h0 + g)))

                def f(inp, j=j, hs=hs):
                    W = inp["mla_w_uq"][j]
                    bl = []
                    for h in hs:
                        r = W[:, h * 192 + 128:h * 192 + 192]
                        sw = np.concatenate([r[:, 32:], r[:, :32]], axis=1)
                        bl.append(_blocks_piece(np.concatenate([r, sw], axis=1), QC, [(0, 64), (64, 64)]))
                    return np.concatenate(bl, axis=1)
                add(f"uqr{i}_{h0}", len(hs) * QC * 128, f)
            g = max(1, SLOT_COLS // (DC * 128))
            for o0 in range(0, KC, g):
                oc = list(range(o0, min(KC, o0 + g)))
                add(f"dkv{i}_{o0}", len(oc) * DC * 128,
                    lambda inp, j=j, oc=oc: _blocks_piece(inp["mla_w_dkv"][j], DC, [(o * 128, 128) for o in oc]))

            def fr(inp, j=j):
                r = inp["mla_w_dkv"][j][:, cfg.KL:cfg.KL + 64]
                sw = np.concatenate([r[:, 32:], r[:, :32]], axis=1)
                return _blocks_piece(np.concatenate([r, sw], axis=1), DC, [(0, 64), (64, 64)])
            add(f"dkvr{i}", DC * 128, fr)
            g = max(1, SLOT_COLS // (KC * 128))
            for h0 in range(0, NH, g):
                hs = list(range(h0, min(NH, h0 + g)))
                add(f"uk{i}_{h0}", len(hs) * KC * 128,
                    lambda inp, j=j, hs=hs: _blocks_piece(inp["mla_w_uk"][j].reshape(cfg.KL, NH * 128), KC, [(h * 128, 128) for h in hs]))
            g2 = max(1, SLOT_COLS // cfg.KL)
            for h0 in range(0, NH, g2):
                hs = list(range(h0, min(NH, h0 + g2)))
                add(f"ukT{i}_{h0}", len(hs) * cfg.KL,
                    lambda inp, j=j, hs=hs: np.concatenate([np.ascontiguousarray(inp["mla_w_uk"][j][:, h, :].T) for h in hs], axis=1))
            for h0 in range(0, NH, g):
                hs = list(range(h0, min(NH, h0 + g)))

                def fv(inp, j=j, hs=hs):
                    W = inp["mla_w_uv"][j].reshape(KC, 128, NH, 128)[:, :, hs, :]
                    return np.ascontiguousarray(W.transpose(1, 0, 2, 3)).reshape(128, KC * len(hs) * 128)
                add(f"uv{i}_{h0}", len(hs) * KC * 128, fv)
            g = max(1, SLOT_COLS // (NH * 128))
            for o0 in range(0, DC, g):
                oc = list(range(o0, min(DC, o0 + g)))
                add(f"wo{i}_{o0}", len(oc) * NH * 128,
                    lambda inp, j=j, oc=oc: _blocks_piece(inp["mla_w_o"][j], NH, [(o * 128, 128) for o in oc]))
        else:
            GC = cfg.GC
            g = max(1, SLOT_COLS // (GC * GC * 128))
            for g0 in range(0, 4, g):
                gs = list(range(g0, min(4, g0 + g)))
                add(f"pool{i}_{g0}", len(gs) * GC * GC * 128,
                    lambda inp, j=j, gs=gs: np.concatenate(
                        [_blocks_piece(inp["pool_w"][j][gg], GC, [(o * 128, 128) for o in range(GC)]) for gg in gs], axis=1))
        g = max(1, SLOT_COLS // (DC * 256))
        for p0 in range(0, FP, g):
            ps = list(range(p0, min(FP, p0 + g)))

            def fu(inp, i=i, ps=ps):
                W = inp["ffn_w_up"][i]
                bl = []
                for p in ps:
                    bl += [(p * 128, 128), (cfg.DFF + p * 128, 128)]
                return _blocks_piece(W, DC, bl)
            add(f"up{i}_{p0}", len(ps) * DC * 256, fu)
        nparts = -(-FP * 128 // SLOT_COLS)
        kper = -(-FP // nparts)
        for o in range(DC):
            for part in range(nparts):
                k0, k1 = part * kper, min(FP, (part + 1) * kper)

                def fd(inp, i=i, o=o, k0=k0, k1=k1):
                    W = inp["ffn_w_down"][i][k0 * 128:k1 * 128]
                    return _blocks_piece(W, k1 - k0, [(o * 128, 128)])
                add(f"dn{i}_{o}_{part}", (k1 - k0) * 128, fd)
    tot = [0] * cfg.DEPTH
    for p in P:
        p["off"] = tot[p["layer"]]
        tot[p["layer"]] += p["ncols"]
    return P, tot


def pcol_layout(cfg):
    L = {}
    n = 0
    for nm, cnt in (("g", cfg.DEPTH * 4 * cfg.DC), ("qn", cfg.NM * cfg.QC), ("kn", cfg.NM * cfg.KC),
                    ("psc", max(1, cfg.NP) * cfg.DC), ("cw", cfg.DEPTH * 3 * 2 * cfg.FP), ("cb", cfg.DEPTH * 2 * cfg.FP)):
        L[nm] = n
        n += cnt
    return L, n


class Res:
    __slots__ = ("w", "r")

    def __init__(self):
        self.w = None
        self.r = {}


def RL(n):
    return [Res() for _ in range(n)]


class DSem:
    def __init__(self, h, scoped=True):
        self.h = h
        self.val = 0
        self.scoped = scoped


ENGS = ("pe", "act", "dve", "pool", "sp")


class Planner:
    def __init__(self):
        self.ops = {e: [] for e in ENGS}
        self.waited = {e: {} for e in ENGS}
        self.last_c = {e: -1 for e in ENGS}
        self.dsems = []

    def _need(self, eng, tok, waits):
        if tok is None:
            return
        if tok[0] == "E":
            _, e2, idx = tok
            if e2 == eng and eng == "pe":
                return
            key = ("E", e2)
            if self.waited[eng].get(key, -1) >= idx:
                return
            self.waited[eng][key] = idx
            self.ops[e2][idx][2] = True
            waits.append(tok)
        else:
            _, sem, val = tok
            key = ("D", id(sem))
            if self.waited[eng].get(key, 0) >= val:
                return
            self.waited[eng][key] = val
            waits.append(tok)

    def _deps(self, eng, reads, writes):
        waits = []
        for r in reads:
            self._need(eng, r.w, waits)
        for w in writes:
            self._need(eng, w.w, waits)
            for t in w.r.values():
                self._need(eng, t, waits)
        return waits

    def op(self, eng, ins, reads=(), writes=()):
        waits = self._deps(eng, reads, writes)
        idx = len(self.ops[eng])
        self.ops[eng].append([ins, waits, False, None, 0])
        self.last_c[eng] = idx
        tok = ("E", eng, idx)
        key = ("E", eng)
        for r in reads:
            r.r[key] = tok
        for w in writes:
            w.w = tok
            w.r = {}
        return tok

    def dma(self, q, ins, reads, writes, sem):
        waits = self._deps(q, reads, writes)
        if sem.val > 0:
            self._need(q, ("D", sem, sem.val), waits)
        sem.val += 16
        self.ops[q].append([ins, waits, False, sem, 0])
        tok = ("D", sem, sem.val)
        key = ("D", id(sem))
        for r in reads:
            r.r[key] = tok
        for w in writes:
            w.w = tok
            w.r = {}
        return tok

    def barrier(self, final=False):
        for f in ENGS:
            waits = []
            for e in ENGS:
                if e == f and e == "pe":
                    continue
                if self.last_c[e] >= 0:
                    self._need(f, ("E", e, self.last_c[e]), waits)
            for sem in self.dsems:
                if sem.val and (final or sem.scoped):
                    self._need(f, ("D", sem, sem.val), waits)
            if waits:
                self.ops[f].append([None, waits, False, None, 0])

    def finish(self):
        for e in ENGS:
            c = 0
            for rec in self.ops[e]:
                if rec[2]:
                    c += 1
                rec[4] = c

    def replay(self, name, e, esem):
        ops = self.ops
        for ins, waits, signal, dsem, _ in ops[name]:
            for tok in waits:
                if tok[0] == "E":
                    e.wait_ge(esem[tok[1]], ops[tok[1]][tok[2]][4])
                else:
                    e.wait_ge(tok[1].h, tok[2])
            if ins is not None:
                m, a, k = ins
                r = getattr(e, m)(*a, **k)
                if dsem is not None:
                    r.then_inc(dsem.h, 16)
                elif signal:
                    r.then_inc(esem[name], 1)


def I(m, *a, **k):
    return (m, a, k)


class Arena:
    def __init__(self, ap, words):
        self.ap, self.words, self.top = ap, words, 0
        self.recs = []
        self.last = None

    def res(self, n):
        lo, hi, seed = self.last
        out = [Res() for _ in range(n)]
        for r in out:
            r.r = dict(seed)
        self.recs.append((lo, hi, out))
        return out

    def alloc(self, free_shape, dt):
        n = int(np.prod(free_shape))
        w = n if dt == F32 else (n + 1) // 2
        w = (w + 7) // 8 * 8
        assert self.top + w <= self.words, ("SBUF arena overflow", self.top, w, self.words)
        lo, hi = self.top, self.top + w
        seed, keep = {}, []
        for (l, h, rl) in self.recs:
            if h <= lo or l >= hi:
                keep.append((l, h, rl))
                continue
            for r in rl:
                for t in ([r.w] if r.w is not None else []) + list(r.r.values()):
                    key = ("E", t[1]) if t[0] == "E" else ("D", id(t[1]))
                    o = seed.get(key)
                    if o is None or t[2] > o[2]:
                        seed[key] = t
            if not (lo <= l and h <= hi):
                keep.append((l, h, rl))
        self.recs = keep
        self.last = (lo, hi, seed)
        v = self.ap[:, self.top:self.top + w]
        self.top += w
        if dt != F32:
            v = v.bitcast(dt)
        v = v[:, 0:n]
        if len(free_shape) == 2:
            v = v.rearrange("p (a b) -> p a b", a=free_shape[0])
        elif len(free_shape) == 3:
            v = v.rearrange("p (a b c) -> p a b c", a=free_shape[0], b=free_shape[1])
        elif len(free_shape) == 4:
            v = v.rearrange("p (a b c d) -> p a b c d", a=free_shape[0], b=free_shape[1], c=free_shape[2])
        return v


def build_program(cfg):
    nc = bass.Bass("TRN2", target_bir_lowering=False)
    pieces, TOT = weight_plan(cfg)
    pidx = {p["name"]: k for k, p in enumerate(pieces)}
    PL, NPC = pcol_layout(cfg)
    D, DC, NH, QC, KC, FP, T, NS, NBS, SS = cfg.D, cfg.DC, cfg.NH, cfg.QC, cfg.KC, cfg.FP, cfg.T, cfg.NS, cfg.NBS, cfg.SS
    SEQ, NT, NM, NP, DEPTH = cfg.SEQ, cfg.NT, cfg.NM, cfg.NP, cfg.DEPTH
    NKC = SEQ // 128
    PKC = cfg.PKC

    def din(name, shape, dt=F32):
        return nc.dram_tensor(name, list(shape), dt, kind="ExternalInput").ap()

    def dout(name, shape):
        return nc.dram_tensor(name, list(shape), F32, kind="ExternalOutput").ap()

    def dscr(name, shape, dt=BF16):
        return nc.dram_tensor(name, list(shape), dt, kind="Internal").ap()

    wblob = [din(f"wblob{i}", [128, TOT[i]]) for i in range(DEPTH)]
    xT_in = din("xT", [D, SEQ])
    xsT_in = din("xsT", [D, NS])
    pcol_in = din("pcol", [128, NPC])
    cst_in = din("cst", [128, 256])
    rope_in = din("rope", [64, 2, SEQ + NS])
    cck_in = din("cck", [max(NM, 1), NBS, cfg.PAST, cfg.KL])
    cckT_in = din("cckT", [max(NM, 1), NBS, cfg.KL, cfg.PAST])
    ckrT_in = din("ckrT", [max(NM, 1), NBS, 64, cfg.PAST])
    spool_in = din("spool", [max(NP, 1), 128, DC, NBS, 15])
    sconv_in = din("sconv", [DEPTH, 128, 2 * FP, NBS, 2])

    yT = dout("yT", [D, SEQ])
    ysT = dout("ysT", [D, NS])
    ockvT = dout("ockvT", [max(NM, 1), cfg.KL, SEQ])
    okrT = dout("okrT", [max(NM, 1), 64, SEQ])
    opool = dout("opool", [max(NP, 1), 128, DC, 15])
    oconv = dout("oconv", [DEPTH, 128, 2 * FP, 2])
    ockvTs = dout("ockvTs", [max(NM, 1), cfg.KL, NS])
    okrTs = dout("okrTs", [max(NM, 1), 64, NS])
    opools = dout("opools", [max(NP, 1), 128, DC, NBS, 15])
    oconvs = dout("oconvs", [DEPTH, 128, 2 * FP, NBS, 2])

    wbf = [dscr(f"wbf{i}", [128, TOT[i]]) for i in range(DEPTH)]
    kT_d = dscr("kT_d", [max(NM, 1), NH, 128, SEQ])
    v_d = dscr("v_d", [max(NM, 1), NH, 128, NKC, 128])
    krT_d = dscr("krT_d", [max(NM, 1), 64, SEQ])

    P = Planner()
    ARENA_W = 52480
    es = None

    import contextlib
    with contextlib.ExitStack() as st:
        arena_t = st.enter_context(nc.sbuf_tensor("arena", [128, ARENA_W], F32))
        psb = [st.enter_context(nc.psum_tensor(f"ps{b}", [128, 512], F32)) for b in range(8)]
        esem = {e: st.enter_context(nc.semaphore(f"s_{e}")) for e in ENGS}

        def newsem(name, scoped=True):
            s = DSem(st.enter_context(nc.semaphore(name)), scoped)
            P.dsems.append(s)
            return s

        A = Arena(arena_t, ARENA_W)
        PS = [psb[b][:] for b in range(8)]
        PSR = RL(8)

        xT = A.alloc((DC, T), F32)
        xTR = RL(DC)
        pcol = A.alloc((NPC,), F32)
        pcolR = Res()
        cst = A.alloc((256,), F32)
        cstR = Res()
        ident = A.alloc((128,), BF16)
        onesD = A.alloc((128,), BF16)
        onesQ = A.alloc((128,), BF16)
        onesK = A.alloc((128,), BF16)
        ones1 = A.alloc((128,), BF16)
        constR = Res()
        ring = A.alloc((NSLOT, SLOT_COLS), BF16)
        ringR = RL(NSLOT)
        ringS = [newsem(f"ring{k}", False) for k in range(NSLOT)]
        chist = A.alloc((DEPTH, 2 * FP, 2), F32)
        chistR = [RL(2 * FP) for _ in range(DEPTH)]
        phist = A.alloc((max(NP, 1), DC, 15), F32)
        phistR = [RL(DC) for _ in range(max(NP, 1))]
        rstd = A.alloc((2, T), F32)
        rstdR = RL(2)
        sqb = A.alloc((3, T), BF16)
        sqbR = RL(3)
        pgc = A.alloc((max(NP, 1) * DC,), F32)
        persist_top = A.top
        EW2 = "dve"

        s_in = newsem("s_in")
        s_xy = [newsem(f"xy{c}", False) for c in range(DC)]
        s_cast = [newsem(f"cast{k}", False) for k in range(8)]
        s_kst, s_vst, s_krst = [newsem("kst0"), newsem("kst1")], newsem("vst"), newsem("krst")
        s_kl = [newsem("kl0"), newsem("kl1")]
        s_vl = [newsem("vl0"), newsem("vl1")]
        s_krl = newsem("krl")
        s_o = [newsem(f"o{k}") for k in range(4)]
        s_m = [newsem(f"m{k}") for k in range(4)]
        so_i = [0]
        sm_i = [0]

        def osem():
            so_i[0] += 1
            return s_o[so_i[0] % 4]

        def msem():
            sm_i[0] += 1
            return s_m[sm_i[0] % 4]

        pieceR = {}

        P.dma("sp", I("dma_start", out=pcol, in_=pcol_in[:, :]), [], [pcolR], s_in)
        P.dma("sp", I("dma_start", out=cst, in_=cst_in[:, :]), [], [cstR], s_in)
        P.op("dve", I("tensor_copy", out=ident, in_=cst[:, 0:128]), [cstR], [constR])
        P.op("dve", I("memset", onesD, 1.0 / D), [], [constR])
        P.op("dve", I("memset", onesQ, 1.0 / cfg.QL), [], [constR])
        P.op("dve", I("memset", onesK, 1.0 / cfg.KL), [], [constR])
        P.op("dve", I("memset", ones1, 1.0), [], [constR])
        for jj in range(NP):
            ii = 2 * jj + 1
            g0 = PL["g"] + (ii * 4 + 1) * DC
            P.op("dve", I("tensor_tensor", out=pgc[:, jj * DC:(jj + 1) * DC], in0=pcol[:, PL["psc"] + jj * DC:PL["psc"] + (jj + 1) * DC], in1=pcol[:, g0:g0 + DC], op=ALU.mult),
                 [pcolR], [pcolR])
        P.op("dve", I("memset", chist, 0.0), [], [r for l in chistR for r in l])
        P.op("dve", I("memset", phist, 0.0), [], [r for l in phistR for r in l])

        def gcol(i, n, c):
            k = PL["g"] + (i * 4 + n) * DC + c
            return pcol[:, k:k + 1]

        class WS:
            order = []
            issued = 0
            consumed = 0
            wb = 0
            seen = set()

        def ws_issue():
            k = WS.issued
            p = pieces[pidx[WS.order[k]]]
            s = k % NSLOT
            c0, c1 = p["off"], p["off"] + p["ncols"]
            li = p["layer"]
            nm = p["name"]
            if nm in pieceR:
                P.dma("sp", I("dma_start", out=ring[:, s, 0:p["ncols"]], in_=wbf[li][:, c0:c1]), [pieceR[nm]], [ringR[s]], ringS[s])
            else:
                P.dma("pool", I("dma_start", out=ring[:, s, 0:p["ncols"]], in_=wblob[li][:, c0:c1]), [], [ringR[s]], ringS[s])
                first = nm not in WS.seen
                WS.seen.add(nm)
                if nm in WS.order[k + 1:] and (not first or k % 2 == 0):
                    pieceR[nm] = Res()
                    P.dma("sp", I("dma_start", out=wbf[li][:, c0:c1], in_=ring[:, s, 0:p["ncols"]]), [ringR[s]], [pieceR[nm]], s_cast[WS.wb % 8])
                    WS.wb += 1
            WS.issued += 1

        def ws_next(name):
            k = WS.consumed
            assert WS.order[k] == name, (k, WS.order[k], name)
            while WS.issued < min(len(WS.order), k + NSLOT):
                ws_issue()
            WS.consumed += 1
            s = k % NSLOT
            return ring[:, s, :], ringR[s]

        def tile_order(sample):
            o = []
            for i in range(DEPTH):
                if i % 2 == 0:
                    pre = (f"dq{i}_", f"uqn{i}_", f"uqr{i}_", f"dkv{i}_", f"dkvr{i}")
                    for nm in pre:
                        o += [p["name"] for p in pieces if p["name"].startswith(nm)]
                    if sample:
                        o += [p["name"] for p in pieces if p["name"].startswith(f"ukT{i}_")]
                    else:
                        o += [p["name"] for p in pieces if p["name"].startswith(f"uk{i}_")]
                    o += [p["name"] for p in pieces if p["name"].startswith(f"uv{i}_")]
                    o += [p["name"] for p in pieces if p["name"].startswith(f"wo{i}_")]
                else:
                    o += [p["name"] for p in pieces if p["name"].startswith(f"pool{i}_")]
                o += [p["name"] for p in pieces if p["name"].startswith(f"up{i}_")]
                o += [p["name"] for p in pieces if p["name"].startswith(f"dn{i}_")]
            return o

        for t in range(NT):
            WS.order += tile_order(False)
        WS.order += tile_order(True)

        bank_i = [0]

        def nbank():
            b = bank_i[0] % 6
            bank_i[0] += 1
            return b

        sq_i = [0]

        def stats_begin():
            return dict(n=0, pend=None)

        def stats_add(S, src, srcR, N, ones, total, scale=None):
            k = sq_i[0] % 3
            sq_i[0] += 1
            kw = {}
            if scale is not None:
                kw["scale"] = scale
            P.op("act", I("activation", out=sqb[:, k, 0:N], in_=src, func=AF.Square, **kw), [srcR, pcolR], [sqbR[k]])
            stats_flush(S, N, ones, total)
            S["pend"] = k

        def stats_flush(S, N, ones, total):
            if S["pend"] is None:
                return
            k = S["pend"]
            n = S["n"]
            P.op("pe", I("matmul", PS[7][:, 0:N], ones, sqb[:, k, 0:N], start=(n == 0), stop=(n == total - 1)),
                 [sqbR[k], constR], [PSR[7]])
            S["n"] += 1
            S["pend"] = None

        rs_i = [0]

        def stats_end(S, N, ones, total):
            stats_flush(S, N, ones, total)
            assert S["n"] == total
            k = rs_i[0] % 2
            rs_i[0] += 1
            P.op("act", I("activation", out=rstd[:, k, 0:N], in_=PS[7][:, 0:N], func=AF.Sqrt, bias=EPS, scale=1.0), [PSR[7]], [rstdR[k]])
            P.op("dve", I("reciprocal", out=rstd[:, k, 0:N], in_=rstd[:, k, 0:N]), [rstdR[k]], [rstdR[k]])
            return rstd[:, k, 0:N], rstdR[k]

        def prenorm(i, n, N, out_fn):
            S = stats_begin()
            for c in range(DC):
                stats_add(S, xT[:, c, 0:N], xTR[c], N, onesD, DC)
            r, rR = stats_end(S, N, onesD, DC)
            for c in range(DC):
                o, oR = out_fn(c)
                P.op("dve", I("scalar_tensor_tensor", out=o, in0=xT[:, c, 0:N], scalar=gcol(i, n, c), in1=r, op0=ALU.mult, op1=ALU.mult),
                     [xTR[c], rR, pcolR], [oR])

        def post_residual(i, n, N, m, mR, S):
            r, rR = stats_end(S, N, onesD, DC)
            def mul(c):
                P.op(EW2, I("tensor_tensor", out=m[:, c, 0:N], in0=m[:, c, 0:N], in1=r, op=ALU.mult), [mR[c], rR], [mR[c]])
            mul(0)
            for c in range(DC):
                if c + 1 < DC:
                    mul(c + 1)
                P.op("dve", I("tensor_tensor", out=xT[:, c, 0:N], in0=xT[:, c, 0:N], in1=m[:, c, 0:N], op=ALU.add), [xTR[c], mR[c]], [xTR[c]])

        def evac_m(b, c, N, m, mR, S, gc, scale=None):
            P.op("act", I("activation", out=m[:, c, 0:N], in_=PS[b][:, 0:N], func=AF.Identity, scale=gc, bias=0.0), [PSR[b], pcolR], [mR[c]])
            stats_add(S, PS[b][:, 0:N], PSR[b], N, onesD, DC, scale=scale)

        def phase_end(mark):
            A.top = mark

        def ffn(i, N, nb, S_, hist, histR, last_out):
            mark0 = A.top
            actT = A.alloc((FP, N), BF16)
            actR = A.res(FP)
            mark1 = A.top
            hT = A.alloc((DC, N), BF16)
            hR = A.res(DC)
            ag = A.alloc((2, N), F32)
            agR = A.res(2)
            av = A.alloc((2, N), F32)
            avR = A.res(2)
            sg = A.alloc((2, N), F32)
            sgR = A.res(2)
            prenorm(i, 2, N, lambda c: (hT[:, c, :], hR[c]))
            cwb = PL["cw"] + i * 3 * 2 * FP
            cbb = PL["cb"] + i * 2 * FP

            def v3(ap):
                return ap.rearrange("p (b s) -> p b s", b=nb)

            g = max(1, SLOT_COLS // (DC * 256))
            pend = None

            def conv_pair(p, bg, bv, q):
                for (ch, b, a, aR) in ((p, bg, ag, agR), (FP + p, bv, av, avR)):
                    w0 = pcol[:, cwb + ch:cwb + ch + 1]
                    w1 = pcol[:, cwb + 2 * FP + ch:cwb + 2 * FP + ch + 1]
                    w2 = pcol[:, cwb + 4 * FP + ch:cwb + 4 * FP + ch + 1]
                    bb = pcol[:, cbb + ch:cbb + ch + 1]
                    P.op("act", I("activation", out=a[:, q, :], in_=PS[b][:, 0:N], func=AF.Identity, scale=w2, bias=bb), [PSR[b], pcolR], [aR[q]])
                for (ch, b, a, aR) in ((p, bg, ag, agR), (FP + p, bv, av, avR)):
                    w1 = pcol[:, cwb + 2 * FP + ch:cwb + 2 * FP + ch + 1]
                    a3, u3 = v3(a[:, q, :]), v3(PS[b][:, 0:N])
                    P.op("dve", I("scalar_tensor_tensor", out=a3[:, :, 1:S_], in0=u3[:, :, 0:S_ - 1], scalar=w1, in1=a3[:, :, 1:S_], op0=ALU.mult, op1=ALU.add),
                         [PSR[b], aR[q], pcolR], [aR[q]])
                for (ch, b, a, aR) in ((p, bg, ag, agR), (FP + p, bv, av, avR)):
                    w0 = pcol[:, cwb + ch:cwb + ch + 1]
                    a3, u3 = v3(a[:, q, :]), v3(PS[b][:, 0:N])
                    P.op("dve", I("scalar_tensor_tensor", out=a3[:, :, 2:S_], in0=u3[:, :, 0:S_ - 2], scalar=w0, in1=a3[:, :, 2:S_], op0=ALU.mult, op1=ALU.add),
                         [PSR[b], aR[q], pcolR], [aR[q]])
                for (ch, b, a, aR) in ((p, bg, ag, agR), (FP + p, bv, av, avR)):
                    w0 = pcol[:, cwb + ch:cwb + ch + 1]
                    a3 = v3(a[:, q, :])
                    P.op("dve", I("scalar_tensor_tensor", out=a3[:, :, 0:2], in0=hist[:, ch, :, 0:2], scalar=w0, in1=a3[:, :, 0:2], op0=ALU.mult, op1=ALU.add),
                         [histR[ch], aR[q], pcolR], [aR[q]])
                for (ch, b, a, aR) in ((p, bg, ag, agR), (FP + p, bv, av, avR)):
                    w1 = pcol[:, cwb + 2 * FP + ch:cwb + 2 * FP + ch + 1]
                    a3 = v3(a[:, q, :])
                    P.op("dve", I("scalar_tensor_tensor", out=a3[:, :, 0:1], in0=hist[:, ch, :, 1:2], scalar=w1, in1=a3[:, :, 0:1], op0=ALU.mult, op1=ALU.add),
                         [histR[ch], aR[q], pcolR], [aR[q]])
                for (ch, b, a, aR) in ((p, bg, ag, agR), (FP + p, bv, av, avR)):
                    u3 = v3(PS[b][:, 0:N])
                    P.op("dve", I("tensor_copy", out=hist[:, ch, :, :], in_=u3[:, :, S_ - 2:S_]), [PSR[b]], [histR[ch]])
                P.op("act", I("activation", out=sg[:, q, :], in_=ag[:, q, :], func=AF.Silu), [agR[q]], [sgR[q]])
                P.op("dve", I("tensor_tensor", out=actT[:, p, :], in0=sg[:, q, :], in1=av[:, q, :], op=ALU.mult), [sgR[q], avR[q]], [actR[p]])

            pi = 0
            for p0 in range(0, FP, g):
                slot, sR = ws_next(f"up{i}_{p0}")
                for k, p in enumerate(range(p0, min(FP, p0 + g))):
                    bg, bv = 2 * (pi % 3), 2 * (pi % 3) + 1
                    base = k * DC * 256
                    for (b, bo) in ((bg, base), (bv, base + DC * 128)):
                        for kc in range(DC):
                            P.op("pe", I("matmul", PS[b][:, 0:N], slot[:, bo + kc * 128:bo + (kc + 1) * 128], hT[:, kc, :], start=(kc == 0), stop=(kc == DC - 1)),
                                 [sR, hR[kc]], [PSR[b]])
                    if pend is not None:
                        conv_pair(*pend)
                    pend = (p, bg, bv, pi % 2)
                    pi += 1
            conv_pair(*pend)
            if last_out is not None:
                P.dma("sp", I("dma_start", out=last_out, in_=hist), list(histR), [Res()], msem())
            A.top = mark1
            m = A.alloc((DC, N), F32)
            mR = A.res(DC)
            S = stats_begin()
            nparts = -(-FP * 128 // SLOT_COLS)
            kper = -(-FP // nparts)
            pend = None
            for o in range(DC):
                b = nbank()
                for part in range(nparts):
                    slot, sR = ws_next(f"dn{i}_{o}_{part}")
                    k0, k1 = part * kper, min(FP, (part + 1) * kper)
                    for kc in range(k0, k1):
                        P.op("pe", I("matmul", PS[b][:, 0:N], slot[:, (kc - k0) * 128:(kc - k0 + 1) * 128], actT[:, kc, :], start=(kc == 0), stop=(kc == FP - 1)),
                             [sR, actR[kc]], [PSR[b]])
                if pend is not None:
                    evac_m(pend[0], pend[1], N, m, mR, S, gcol(i, 3, pend[1]))
                pend = (b, o)
            evac_m(pend[0], pend[1], N, m, mR, S, gcol(i, 3, pend[1]))
            post_residual(i, 3, N, m, mR, S)
            phase_end(mark0)

        def poolmix(i, N, nb, S_, hist, histR, first, last_out):
            j = i // 2
            mark0 = A.top
            L = 15 + S_
            hx = A.alloc((DC, nb, L), F32)
            hxR = A.res(DC)
            dT = A.alloc((DC, N), BF16)
            dR = A.res(DC)
            tA = A.alloc((2, nb, L), F32)
            tAR = A.res(2)
            tB = A.alloc((2, nb, L), F32)
            tBR = A.res(2)
            m = A.alloc((DC, N), F32)
            mR = A.res(DC)
            for c in range(DC):
                P.op("act", I("activation", out=hx[:, c, :, 0:15], in_=hist[:, c, :, :], func=AF.Copy), [histR[c]], [hxR[c]])
            if nb == 1:
                prenorm(i, 0, N, lambda c: (hx[:, c, 0, 15:L], hxR[c]))
            else:
                S = stats_begin()
                for c in range(DC):
                    stats_add(S, xT[:, c, 0:N], xTR[c], N, onesD, DC)
                r, rR = stats_end(S, N, onesD, DC)
                for c in range(DC):
                    P.op("dve", I("scalar_tensor_tensor", out=hx[:, c, :, 15:L], in0=xT[:, c, 0:N].rearrange("p (b s) -> p b s", b=nb), scalar=gcol(i, 0, c),
                                  in1=r.rearrange("p (b s) -> p b s", b=nb), op0=ALU.mult, op1=ALU.mult), [xTR[c], rR, pcolR], [hxR[c]])
            for c in range(DC):
                gi = c // cfg.GC
                w = 2 ** (gi + 1)
                q = c % 2
                src, srcR = hx[:, c], hxR[c]
                bufs = [(tA[:, q], tAR[q]), (tB[:, q], tBR[q])]
                sh = 1
                lv = 0
                while sh < w:
                    dst, dstR = bufs[lv % 2]
                    P.op("dve", I("tensor_tensor", out=dst[:, :, sh:L], in0=src[:, :, sh:L], in1=src[:, :, 0:L - sh], op=ALU.add), [srcR], [dstR])
                    if sh > 1:
                        pass
                    src, srcR = dst, dstR
                    sh *= 2
                    lv += 1
                d3 = dT[:, c, :].rearrange("p (b s) -> p b s", b=nb)
                P.op("dve", I("scalar_tensor_tensor", out=d3, in0=src[:, :, 15:L], scalar=1.0 / w, in1=hx[:, c, :, 15:L], op0=ALU.mult, op1=ALU.subtract),
                     [srcR, hxR[c]], [dR[c]])
                if first:
                    ic = cst[:, 128 + gi * 16:128 + gi * 16 + 15]
                    tmp, tmpR = bufs[lv % 2]
                    P.op("dve", I("tensor_tensor", out=tmp[:, 0, 0:15], in0=src[:, 0, 15:30], in1=ic, op=ALU.mult), [srcR, cstR], [tmpR])
                    P.op("dve", I("tensor_tensor", out=dT[:, c, 0:15], in0=tmp[:, 0, 0:15], in1=hx[:, c, 0, 15:30], op=ALU.subtract), [tmpR, hxR[c]], [dR[c]])
                P.op("act", I("activation", out=hist[:, c, :, :], in_=hx[:, c, :, S_:S_ + 15], func=AF.Copy), [hxR[c]], [histR[c]])
            if last_out is not None:
                P.dma("sp", I("dma_start", out=last_out, in_=hist), list(histR), [Res()], msem())
            GC = cfg.GC
            g = max(1, SLOT_COLS // (GC * GC * 128))
            S = stats_begin()
            pend = None
            for g0 in range(0, 4, g):
                slot, sR = ws_next(f"pool{i}_{g0}")
                for k, gg in enumerate(range(g0, min(4, g0 + g))):
                    for o in range(GC):
                        b = nbank()
                        base = k * GC * GC * 128 + o * GC * 128
                        for kc in range(GC):
                            P.op("pe", I("matmul", PS[b][:, 0:N], slot[:, base + kc * 128:base + (kc + 1) * 128], dT[:, gg * GC + kc, :], start=(kc == 0), stop=(kc == GC - 1)),
                                 [sR, dR[gg * GC + kc]], [PSR[b]])
                        c = gg * GC + o
                        if pend is not None:
                            evac_m(pend[0], pend[1], N, m, mR, S, pgc[:, j * DC + pend[1]:j * DC + pend[1] + 1], scale=pend[2])
                        kk = PL["psc"] + j * DC + c
                        pend = (b, c, pcol[:, kk:kk + 1])
            evac_m(pend[0], pend[1], N, m, mR, S, pgc[:, j * DC + pend[1]:j * DC + pend[1] + 1], scale=pend[2])
            post_residual(i, 1, N, m, mR, S)
            phase_end(mark0)

        def mla_proj(i, N, pos0, hT, hR, Qn, QnR, Qr, QrR, ckvb, ckvbR, krb, krbR, out_ckv, out_kr):
            j = i // 2
            cq = A.alloc((QC, N), F32)
            cqR = A.res(QC)
            cqn = A.alloc((QC, N), BF16)
            cqnR = A.res(QC)
            ckv = A.alloc((KC, N), F32)
            ckvR = A.res(KC)
            krf = A.alloc((N,), F32)
            krfR = A.res(1)[0]
            rtmp = A.alloc((2, N), F32)
            rtmpR = A.res(2)
            cs = A.alloc((2, N), F32)
            csR = A.res(1)[0]
            P.dma("sp", I("dma_start", out=cs[0:64, :, :], in_=rope_in[:, :, pos0:pos0 + N]), [], [csR], s_in)
            prenorm(i, 0, N, lambda c: (hT[:, c, :], hR[c]))

            def lin_group(prefix, nout, K, rhs, rhsR, evac):
                g = max(1, SLOT_COLS // (K * 128))
                pend = None
                for o0 in range(0, nout, g):
                    slot, sR = ws_next(f"{prefix}{i}_{o0}")
                    for k, o in enumerate(range(o0, min(nout, o0 + g))):
                        b = nbank()
                        for kc in range(K):
                            P.op("pe", I("matmul", PS[b][:, 0:N], slot[:, (k * K + kc) * 128:(k * K + kc + 1) * 128], rhs[:, kc, :], start=(kc == 0), stop=(kc == K - 1)),
                                 [sR, rhsR[kc]], [PSR[b]])
                        if pend is not None:
                            evac(*pend)
                        pend = (b, o)
                evac(*pend)

            S = stats_begin()

            def ev_cq(b, o):
                P.op("act", I("activation", out=cq[:, o, :], in_=PS[b][:, 0:N], func=AF.Copy), [PSR[b]], [cqR[o]])
                stats_add(S, PS[b][:, 0:N], PSR[b], N, onesQ, QC)
            lin_group("dq", QC, DC, hT, hR, ev_cq)
            r, rR = stats_end(S, N, onesQ, QC)
            for o in range(QC):
                kk = PL["qn"] + j * QC + o
                P.op("dve", I("scalar_tensor_tensor", out=cqn[:, o, :], in0=cq[:, o, :], scalar=pcol[:, kk:kk + 1], in1=r, op0=ALU.mult, op1=ALU.mult),
                     [cqR[o], rR, pcolR], [cqnR[o]])
            def ev_qn(b, h):
                P.op("act", I("activation", out=Qn[:, h, :], in_=PS[b][:, 0:N], func=AF.Copy), [PSR[b]], [QnR[h]])
            lin_group("uqn", NH, QC, cqn, cqnR, ev_qn)

            def rope_evac(bp, bs, dst, dstR, q):
                P.op("dve", I("tensor_tensor", out=rtmp[0:64, q, :], in0=PS[bp][0:64, 0:N], in1=cs[0:64, 0, :], op=ALU.mult), [PSR[bp], csR], [rtmpR[q]])
                P.op("dve", I("tensor_tensor", out=sgt[0:64, q, :], in0=PS[bs][0:64, 0:N], in1=cs[0:64, 1, :], op=ALU.mult), [PSR[bs], csR], [sgtR[q]])
                P.op("dve", I("tensor_tensor", out=dst, in0=rtmp[0:64, q, :], in1=sgt[0:64, q, :], op=ALU.add), [rtmpR[q], sgtR[q]], [dstR])

            sgt = A.alloc((2, N), F32)
            sgtR = A.res(2)
            g = max(1, SLOT_COLS // (QC * 128))
            pend = None
            hi = 0
            for h0 in range(0, NH, g):
                slot, sR = ws_next(f"uqr{i}_{h0}")
                for k, h in enumerate(range(h0, min(NH, h0 + g))):
                    bp, bs = nbank(), nbank()
                    base = k * QC * 128
                    for (b, bo) in ((bp, base), (bs, base + QC * 64)):
                        for kc in range(QC):
                            P.op("pe", I("matmul", PS[b][0:64, 0:N], slot[:, bo + kc * 64:bo + (kc + 1) * 64], cqn[:, kc, :], start=(kc == 0), stop=(kc == QC - 1)),
                                 [sR, cqnR[kc]], [PSR[b]])
                    if pend is not None:
                        rope_evac(*pend)
                    pend = (bp, bs, Qr[0:64, h, :], QrR[h], hi % 2)
                    hi += 1
            rope_evac(*pend)
            S = stats_begin()

            def ev_kv(b, o):
                P.op("act", I("activation", out=ckv[:, o, :], in_=PS[b][:, 0:N], func=AF.Copy), [PSR[b]], [ckvR[o]])
                stats_add(S, PS[b][:, 0:N], PSR[b], N, onesK, KC)
            lin_group("dkv", KC, DC, hT, hR, ev_kv)
            r, rR = stats_end(S, N, onesK, KC)
            for o in range(KC):
                kk = PL["kn"] + j * KC + o
                P.op("dve", I("scalar_tensor_tensor", out=ckv[:, o, :], in0=ckv[:, o, :], scalar=pcol[:, kk:kk + 1], in1=r, op0=ALU.mult, op1=ALU.mult),
                     [ckvR[o], rR, pcolR], [ckvR[o]])
                P.op("act", I("activation", out=ckvb[:, o, :], in_=ckv[:, o, :], func=AF.Copy), [ckvR[o]], [ckvbR[o]])
            P.dma("sp", I("dma_start", out=out_ckv, in_=ckv), list(ckvR), [Res()], osem())
            slot, sR = ws_next(f"dkvr{i}")
            bp, bs = nbank(), nbank()
            for (b, bo) in ((bp, 0), (bs, DC * 64)):
                for kc in range(DC):
                    P.op("pe", I("matmul", PS[b][0:64, 0:N], slot[:, bo + kc * 64:bo + (kc + 1) * 64], hT[:, kc, :], start=(kc == 0), stop=(kc == DC - 1)),
                         [sR, hR[kc]], [PSR[b]])
            rope_evac(bp, bs, krf[0:64, :], krfR, 0)
            P.op("act", I("activation", out=krb[0:64, :], in_=krf[0:64, :], func=AF.Copy), [krfR], [krbR])
            P.dma("sp", I("dma_start", out=out_kr, in_=krf[0:64, :]), [krfR], [Res()], osem())

        kTdR = [[Res() for _ in range(NH)] for _ in range(max(NM, 1))]
        vdR = [[Res() for _ in range(NH)] for _ in range(max(NM, 1))]
        krdR = [Res() for _ in range(max(NM, 1))]

        def mla_prompt(i, t):
            j = i // 2
            N = T
            t0 = t * T
            mark0 = A.top
            Qn = A.alloc((NH, N), BF16)
            QnR = A.res(NH)
            Qr = A.alloc((NH, N), BF16)
            QrR = A.res(NH)
            for hh in range(NH):
                P.op("dve", I("memset", Qr[64:128, hh, :], 0.0), [], [QrR[hh]])
            markA = A.top
            hT = A.alloc((DC, N), BF16)
            hR = A.res(DC)
            ckvb = A.alloc((KC, N), BF16)
            ckvbR = A.res(KC)
            krb = A.alloc((N,), BF16)
            krbR = A.res(1)[0]
            gk = max(1, SLOT_COLS // (KC * 128))
            Kst = A.alloc((2, N), BF16)
            KstR = A.res(2)
            Vsb = A.alloc((min(NH, gk), 4, 128), BF16)
            VsbR = A.res(1)[0]
            mla_proj(i, N, t0, hT, hR, Qn, QnR, Qr, QrR, ckvb, ckvbR, krb, krbR,
                     ockvT[j].rearrange("(c p) t -> p c t", p=128)[:, :, t0:t0 + N], okrT[j][:, t0:t0 + N])
            P.dma("sp", I("dma_start", out=krT_d[j][:, t0:t0 + N], in_=krb[0:64, :]), [krbR], [krdR[j]], s_krst)
            g = max(1, SLOT_COLS // (KC * 128))
            pend = None

            def ev_k(b, h):
                q = h % 2
                P.op("act", I("activation", out=Kst[:, q, :], in_=PS[b][:, 0:N], func=AF.Copy), [PSR[b]], [KstR[q]])
                P.dma("sp", I("dma_start", out=kT_d[j][h][:, t0:t0 + N], in_=Kst[:, q, :]), [KstR[q]], [kTdR[j][h]], s_kst[q])
            for h0 in range(0, NH, g):
                slot, sR = ws_next(f"uk{i}_{h0}")
                for k, h in enumerate(range(h0, min(NH, h0 + g))):
                    b = nbank()
                    for kc in range(KC):
                        P.op("pe", I("matmul", PS[b][:, 0:N], slot[:, (k * KC + kc) * 128:(k * KC + kc + 1) * 128], ckvb[:, kc, :], start=(kc == 0), stop=(kc == KC - 1)),
                             [sR, ckvbR[kc]], [PSR[b]])
                    if pend is not None:
                        ev_k(*pend)
                    pend = (b, h)
            ev_k(*pend)
            pend = None

            def ev_v(b, tc, hs0, nh):
                P.op("act", I("activation", out=Vsb[:, hs0:hs0 + nh, tc, :], in_=PS[b][:, 0:nh * 128].rearrange("p (h d) -> p h d", h=nh), func=AF.Copy), [PSR[b]], [VsbR])
            for h0 in range(0, NH, g):
                slot, sR = ws_next(f"uv{i}_{h0}")
                nhp = min(NH, h0 + g) - h0
                for hs0 in range(0, nhp, 4):
                    nh = min(4, nhp - hs0)
                    for tc in range(N // 128):
                        b = nbank()
                        for kc in range(KC):
                            P.op("pe", I("matmul", PS[b][:, 0:nh * 128], ckvb[:, kc, tc * 128:(tc + 1) * 128],
                                         slot[:, kc * nhp * 128 + hs0 * 128:kc * nhp * 128 + (hs0 + nh) * 128], start=(kc == 0), stop=(kc == KC - 1)),
                                 [sR, ckvbR[kc]], [PSR[b]])
                        if pend is not None:
                            ev_v(*pend)
                        pend = (b, tc, hs0, nh)
                ev_v(*pend)
                pend = None
                P.dma("sp", I("dma_start", out=v_d[j].rearrange("h p k d -> p h k d")[:, h0:h0 + nhp, 4 * t:4 * t + 4, :], in_=Vsb[:, 0:nhp]), [VsbR],
                      [vdR[j][hh] for hh in range(h0, h0 + nhp)], s_vst)
            A.top = markA
            Osb = A.alloc((NH, N), BF16)
            OsbR = A.res(NH)
            markB = A.top
            nk = 4 * (t + 1)
            Ksl = A.alloc((2, nk * 128), BF16)
            KslR = A.res(2)
            Vsl = A.alloc((2, nk, 128), BF16)
            VslR = A.res(2)
            krT = A.alloc((nk * 128,), BF16)
            krTR = A.res(1)[0]
            LOOK = 3
            SB = [0, 1, 2, 5, 6]
            NPT = LOOK + 2
            Pt = A.alloc((NPT, N), BF16)
            PtR = A.res(NPT)
            rl = A.alloc((1, N), F32)
            rlR = A.res(1)
            acc = A.alloc((2, N), F32)
            accR = A.res(2)
            accb = A.alloc((2, N), BF16)
            accbR = A.res(2)
            P.op("dve", I("memset", krT[64:128, :], 0.0), [], [krTR])
            P.dma("sp", I("dma_start", out=krT[0:64, :], in_=krT_d[j][:, 0:nk * 128]), [krdR[j]], [krTR], s_krl)

            def load_head(h):
                q = h % 2
                P.dma("sp", I("dma_start", out=Ksl[:, q, :], in_=kT_d[j][h][:, 0:nk * 128]), [kTdR[j][h]], [KslR[q]], s_kl[q])
                P.dma("sp", I("dma_start", out=Vsl[:, q, :, :], in_=v_d[j][h][:, 0:nk, :]), [vdR[j][h]], [VslR[q]], s_vl[q])
            load_head(0)
            pi = 0
            for h in range(NH):
                if h + 1 < NH:
                    load_head(h + 1)
                q = h % 2
                bo, bl = 3 + q, 7

                def cols(kc):
                    jj = kc - 4 * t
                    return (0 if jj < 0 else 128 * jj), jj

                def s_mm(kc, sb):
                    c0, jj = cols(kc)
                    P.op("pe", I("matmul", PS[sb][:, c0:N], Ksl[:, q, kc * 128:(kc + 1) * 128], Qn[:, h, c0:N], start=True, stop=False), [KslR[q], QnR[h]], [PSR[sb]])
                    P.op("pe", I("matmul", PS[sb][:, c0:N], krT[:, kc * 128:(kc + 1) * 128], Qr[:, h, c0:N], start=False, stop=True), [krTR, QrR[h]], [PSR[sb]])

                def p_ev(kc, sb, pb):
                    c0, jj = cols(kc)
                    P.op("act", I("activation", out=Pt[:, pb, c0:N], in_=PS[sb][:, c0:N], func=AF.Exp, scale=cfg.scale), [PSR[sb]], [PtR[pb]])
                    if jj >= 0:
                        P.op("dve", I("memset", Pt[64:128, pb, c0:c0 + 64], 0.0), [], [PtR[pb]])

                def pv_mm(kc, pb):
                    c0, jj = cols(kc)
                    P.op("pe", I("matmul", PS[bo][:, c0:N], Vsl[:, q, kc, :], Pt[:, pb, c0:N], start=(kc == 0), stop=(kc == nk - 1)), [VslR[q], PtR[pb]], [PSR[bo]])
                    e = kc % 2
                    if kc < 2:
                        if c0 > 0:
                            P.op("dve", I("memset", acc[:, e, 0:c0], 0.0), [], [accR[e]])
                        P.op("dve", I("tensor_copy", out=acc[:, e, c0:N], in_=Pt[:, pb, c0:N]), [PtR[pb]], [accR[e]])
                    else:
                        P.op("dve", I("tensor_tensor", out=acc[:, e, c0:N], in0=acc[:, e, c0:N], in1=Pt[:, pb, c0:N], op=ALU.add), [PtR[pb], accR[e]], [accR[e]])
                for kc in range(min(LOOK, nk)):
                    s_mm(kc, SB[(pi + kc) % len(SB)])
                    p_ev(kc, SB[(pi + kc) % len(SB)], (pi + kc) % NPT)
                for kc in range(0, nk, 2):
                    nxt = [k2 for k2 in (kc + LOOK, kc + LOOK + 1) if k2 < nk]
                    for k2 in nxt:
                        s_mm(k2, SB[(pi + k2) % len(SB)])
                    for k2 in nxt:
                        p_ev(k2, SB[(pi + k2) % len(SB)], (pi + k2) % NPT)
                    for k2 in (kc, kc + 1):
                        if k2 < nk:
                            pv_mm(k2, (pi + k2) % NPT)
                pi += nk
                for e in range(2):
                    P.op("act", I("activation", out=accb[:, e, :], in_=acc[:, e, :], func=AF.Copy), [accR[e]], [accbR[e]])
                for e in range(2):
                    P.op("pe", I("matmul", PS[bl][:, 0:N], ones1, accb[:, e, :], start=(e == 0), stop=(e == 1)), [constR, accbR[e]], [PSR[bl]])
                P.op("dve", I("reciprocal", out=rl[:, 0, :], in_=PS[bl][:, 0:N]), [PSR[bl]], [rlR[0]])
                P.op("dve", I("tensor_tensor", out=Osb[:, h, :], in0=PS[bo][:, 0:N], in1=rl[:, 0, :], op=ALU.mult), [PSR[bo], rlR[0]], [OsbR[h]])
            A.top = markB
            mla_out(i, N, Osb, OsbR)
            phase_end(mark0)

        def mla_out(i, N, Osb, OsbR):
            m = A.alloc((DC, N), F32)
            mR = A.res(DC)
            S = stats_begin()
            g = max(1, SLOT_COLS // (NH * 128))
            pend = None
            for o0 in range(0, DC, g):
                slot, sR = ws_next(f"wo{i}_{o0}")
                for k, o in enumerate(range(o0, min(DC, o0 + g))):
                    b = nbank()
                    for h in range(NH):
                        P.op("pe", I("matmul", PS[b][:, 0:N], slot[:, (k * NH + h) * 128:(k * NH + h + 1) * 128], Osb[:, h, :], start=(h == 0), stop=(h == NH - 1)),
                             [sR, OsbR[h]], [PSR[b]])
                    if pend is not None:
                        evac_m(pend[0], pend[1], N, m, mR, S, gcol(i, 1, pend[1]))
                    pend = (b, o)
            evac_m(pend[0], pend[1], N, m, mR, S, gcol(i, 1, pend[1]))
            post_residual(i, 1, N, m, mR, S)

        def mla_sample(i):
            j = i // 2
            N = NS
            mark0 = A.top
            Qn = A.alloc((NH, N), BF16)
            QnR = A.res(NH)
            Qr = A.alloc((NH, N), BF16)
            QrR = A.res(NH)
            hT = A.alloc((DC, N), BF16)
            hR = A.res(DC)
            ckvb = A.alloc((KC, N), BF16)
            ckvbR = A.res(KC)
            krb = A.alloc((N,), BF16)
            krbR = A.res(1)[0]
            Osb = A.alloc((NH, N), BF16)
            OsbR = A.res(NH)
            QL = A.alloc((KC, NH, N), BF16)
            QLR = A.res(KC)
            OL = A.alloc((KC, NBS, NH, SS), BF16)
            OLR = A.res(KC)
            cnat = A.alloc((NBS, cfg.KL), BF16)
            cnatR = A.res(NBS)
            CK = A.alloc((PKC, cfg.KL), BF16)
            CKR = A.res(1)[0]
            CKT = A.alloc((KC, cfg.PAST), BF16)
            CKTR = A.res(1)[0]
            KRT = A.alloc((cfg.PAST,), BF16)
            KRTR = A.res(1)[0]
            Pt = A.alloc((4, NH * SS), BF16)
            PtR = A.res(4)
            rl = A.alloc((NH * SS,), F32)
            rlR = A.res(1)[0]
            s_c = [newsem(f"sc{i}_{k}") for k in range(3)]
            mla_proj(i, N, SEQ, hT, hR, Qn, QnR, Qr, QrR, ckvb, ckvbR, krb, krbR,
                     ockvTs[j].rearrange("(c p) t -> p c t", p=128), okrTs[j][:, :])
            g2 = max(1, SLOT_COLS // cfg.KL)
            pend = None

            def ev_ql(b, cc, h):
                P.op("act", I("activation", out=QL[:, cc, h, :], in_=PS[b][:, 0:N], func=AF.Copy), [PSR[b]], [QLR[cc]])
            for h0 in range(0, NH, g2):
                slot, sR = ws_next(f"ukT{i}_{h0}")
                for k, h in enumerate(range(h0, min(NH, h0 + g2))):
                    for cc in range(KC):
                        b = nbank()
                        P.op("pe", I("matmul", PS[b][:, 0:N], slot[:, k * cfg.KL + cc * 128:k * cfg.KL + (cc + 1) * 128], Qn[:, h, :], start=True, stop=True), [sR, QnR[h]], [PSR[b]])
                        if pend is not None:
                            ev_ql(*pend)
                        pend = (b, cc, h)
            ev_ql(*pend)
            for bt in range(NBS):
                b = nbank()
                for cc in range(KC):
                    P.op("pe", I("matmul", PS[b][0:SS, cc * 128:(cc + 1) * 128], ckvb[:, cc, bt * SS:(bt + 1) * SS], ident, start=True, stop=True), [ckvbR[cc], constR], [PSR[b]])
                P.op("act", I("activation", out=cnat[0:SS, bt, :], in_=PS[b][0:SS, 0:cfg.KL], func=AF.Copy), [PSR[b]], [cnatR[bt]])
            NQ = NH * SS
            pi = 0
            for bt in range(NBS):
                P.dma("pool", I("dma_start", out=CK, in_=cck_in[j][bt].rearrange("(k p) c -> p k c", p=128)), [], [CKR], s_c[0])
                P.dma("pool", I("dma_start", out=CKT, in_=cckT_in[j][bt].rearrange("(k p) t -> p k t", p=128)), [], [CKTR], s_c[1])
                P.dma("pool", I("dma_start", out=KRT[0:64, :], in_=ckrT_in[j][bt]), [], [KRTR], s_c[2])
                nk = PKC + 1
                assert KC * NQ <= 1024 and NQ <= 512

                def oacc(cc):
                    fo = cc * NQ
                    return PS[3 + fo // 512][:, fo % 512:fo % 512 + NQ], PSR[3 + fo // 512]

                def s_mm(kc, sb):
                    M = 128 if kc < PKC else SS
                    for cc in range(KC):
                        lhs = CKT[:, cc, kc * 128:(kc + 1) * 128] if kc < PKC else ckvb[:, cc, bt * SS:(bt + 1) * SS]
                        P.op("pe", I("matmul", PS[sb][0:M, 0:NQ].rearrange("p (h s) -> p h s", h=NH), lhs, QL[:, cc, :, bt * SS:(bt + 1) * SS], start=(cc == 0), stop=False),
                             [CKTR, ckvbR[cc], QLR[cc]], [PSR[sb]])
                    lhs = KRT[0:64, kc * 128:(kc + 1) * 128] if kc < PKC else krb[0:64, bt * SS:(bt + 1) * SS]
                    P.op("pe", I("matmul", PS[sb][0:M, 0:NQ].rearrange("p (h s) -> p h s", h=NH), lhs, Qr[0:64, :, bt * SS:(bt + 1) * SS], start=False, stop=True),
                         [KRTR, krbR] + QrR, [PSR[sb]])

                def p_ev(kc, sb, pb):
                    M = 128 if kc < PKC else SS
                    P.op("act", I("activation", out=Pt[0:M, pb, :], in_=PS[sb][0:M, 0:NQ], func=AF.Exp, scale=cfg.scale), [PSR[sb]], [PtR[pb]])

                def pv_mm(kc, pb):
                    M = 128 if kc < PKC else SS
                    for cc in range(KC):
                        lhs = CK[:, kc, cc * 128:(cc + 1) * 128] if kc < PKC else cnat[0:SS, bt, cc * 128:(cc + 1) * 128]
                        oa, oaR = oacc(cc)
                        P.op("pe", I("matmul", oa, lhs, Pt[0:M, pb, :], start=(kc == 0 and (cc * NQ) % 512 == 0), stop=(kc == nk - 1), skip_group_check=True),
                             [CKR, cnatR[bt], PtR[pb]], [oaR])
                    P.op("pe", I("matmul", PS[5][:, 0:NQ], ones1[0:M, :], Pt[0:M, pb, :], start=(kc == 0), stop=(kc == nk - 1)), [constR, PtR[pb]], [PSR[5]])
                LOOK = 2
                for kc in range(min(LOOK, nk)):
                    s_mm(kc, (pi + kc) % 3)
                    p_ev(kc, (pi + kc) % 3, (pi + kc) % 4)
                for kc in range(nk):
                    if kc + LOOK < nk:
                        s_mm(kc + LOOK, (pi + kc + LOOK) % 3)
                        p_ev(kc + LOOK, (pi + kc + LOOK) % 3, (pi + kc + LOOK) % 4)
                    pv_mm(kc, (pi + kc) % 4)
                pi += nk
                P.op("dve", I("reciprocal", out=rl, in_=PS[5][:, 0:NQ]), [PSR[5]], [rlR])
                for cc in range(KC):
                    oa, oaR = oacc(cc)
                    P.op("dve", I("tensor_tensor", out=OL[:, cc, bt, :, :], in0=oa.rearrange("p (h s) -> p h s", h=NH), in1=rl.rearrange("p (h s) -> p h s", h=NH), op=ALU.mult),
                         [oaR, rlR], [OLR[cc]])
            g = max(1, SLOT_COLS // (KC * 128))
            pend = None

            def ev_o(b, h):
                P.op("act", I("activation", out=Osb[:, h, :], in_=PS[b][:, 0:N], func=AF.Copy), [PSR[b]], [OsbR[h]])
            bank_i[0] = 0
            for h0 in range(0, NH, g):
                slot, sR = ws_next(f"uv{i}_{h0}")
                nhp = min(NH, h0 + g) - h0
                for k, h in enumerate(range(h0, h0 + nhp)):
                    b = nbank() % 3
                    for cc in range(KC):
                        P.op("pe", I("matmul", PS[b][:, 0:N].rearrange("p (b s) -> p b s", b=NBS), slot[:, cc * nhp * 128 + k * 128:cc * nhp * 128 + (k + 1) * 128],
                                     OL[:, cc, :, h, :], start=(cc == 0), stop=(cc == KC - 1)), [sR, OLR[cc]], [PSR[b]])
                    if pend is not None:
                        ev_o(*pend)
                    pend = (b, h)
            ev_o(*pend)
            mla_out(i, N, Osb, OsbR)
            phase_end(mark0)

        xv = xT_in.rearrange("(c p) t -> p c t", p=128)
        yv = yT.rearrange("(c p) t -> p c t", p=128)
        for t in range(NT):
            t0 = t * T
            for c in range(DC):
                P.dma("sp", I("dma_start", out=xT[:, c, :], in_=xv[:, c, t0:t0 + T]), [], [xTR[c]], s_xy[c])
            for i in range(DEPTH):
                j = i // 2
                last = (t == NT - 1)
                if i % 2 == 0:
                    mla_prompt(i, t)
                else:
                    hv = phist[:, j].rearrange("p c (b k) -> p c b k", b=1)
                    poolmix(i, T, 1, T, hv, phistR[j], t == 0,
                            opool[j].rearrange("p c (b k) -> p c b k", b=1) if last else None)
                hv = chist[:, i].rearrange("p c (b k) -> p c b k", b=1)
                ffn(i, T, 1, T, hv, chistR[i],
                    oconv[i].rearrange("p c (b k) -> p c b k", b=1) if last else None)
            for c in range(DC):
                P.dma("sp", I("dma_start", out=yv[:, c, t0:t0 + T], in_=xT[:, c, :]), [xTR[c]], [Res()], s_xy[c])
        for c in range(DC):
            P.dma("sp", I("dma_start", out=xT[:, c, 0:NS], in_=xsT_in.rearrange("(c p) t -> p c t", p=128)[:, c, :]), [], [xTR[c]], s_xy[c])
        for i in range(DEPTH):
            j = i // 2
            if i % 2 == 0:
                mla_sample(i)
            else:
                mk = A.top
                sh = A.alloc((DC, NBS, 15), F32)
                shR = A.res(DC)
                P.dma("sp", I("dma_start", out=sh, in_=spool_in[j]), [], list(shR), s_in)
                poolmix(i, NS, NBS, SS, sh, shR, False, opools[j])
                A.top = mk
            mk = A.top
            ch = A.alloc((2 * FP, NBS, 2), F32)
            chR = A.res(2 * FP)
            P.dma("sp", I("dma_start", out=ch, in_=sconv_in[i]), [], list(chR), s_in)
            ffn(i, NS, NBS, SS, ch, chR, oconvs[i])
            A.top = mk
        for c in range(DC):
            P.dma("sp", I("dma_start", out=ysT.rearrange("(c p) t -> p c t", p=128)[:, c, :], in_=xT[:, c, 0:NS]), [xTR[c]], [Res()], s_xy[c])
        assert WS.consumed == len(WS.order), (WS.consumed, len(WS.order))
        P.barrier(final=True)
        P.finish()

        with nc.allow_non_contiguous_dma(reason="tiny state rows"), nc.Block() as block:
            @block.tensor
            def _(e):
                P.replay("pe", e, esem)

            @block.scalar
            def _(e):
                P.replay("act", e, esem)

            @block.vector
            def _(e):
                P.replay("dve", e, esem)

            @block.gpsimd
            def _(e):
                P.replay("pool", e, esem)

            @block.sync
            def _(e):
                P.replay("sp", e, esem)
    return nc


def _cols(v):
    v = np.asarray(v, np.float32).reshape(-1, 128)
    return np.ascontiguousarray(v.T)


def prep_inputs(cfg, inp):
    pieces, TOT = weight_plan(cfg)
    wblob = [np.empty((128, TOT[i]), np.float32) for i in range(cfg.DEPTH)]
    for p in pieces:
        a = p["build"](inp)
        assert a.shape == (128, p["ncols"]), (p["name"], a.shape, p["ncols"])
        wblob[p["layer"]][:, p["off"]:p["off"] + p["ncols"]] = a
    PL, NPC = pcol_layout(cfg)
    pcol = np.zeros((128, NPC), np.float32)
    DC, FP = cfg.DC, cfg.FP
    pcol[:, PL["g"]:PL["g"] + cfg.DEPTH * 4 * DC] = _cols(inp["norm_g"])
    if cfg.NM:
        pcol[:, PL["qn"]:PL["qn"] + cfg.NM * cfg.QC] = _cols(inp["mla_q_norm"])
        pcol[:, PL["kn"]:PL["kn"] + cfg.NM * cfg.KC] = _cols(inp["mla_kv_norm"])
    if cfg.NP:
        pcol[:, PL["psc"]:PL["psc"] + cfg.NP * DC] = _cols(inp["pool_scale"])
    pcol[:, PL["cw"]:PL["cw"] + cfg.DEPTH * 3 * 2 * FP] = _cols(inp["ffn_conv_w"])
    pcol[:, PL["cb"]:PL["cb"] + cfg.DEPTH * 2 * FP] = _cols(inp["ffn_conv_b"])
    cst = np.zeros((128, 256), np.float32)
    cst[:, 0:128] = np.eye(128, dtype=np.float32)
    for gi in range(4):
        w = 2 ** (gi + 1)
        cst[:, 128 + gi * 16:128 + gi * 16 + 16] = (1.0 / np.minimum(np.float32(w), np.arange(16, dtype=np.float32) + 1.0))[None, :]
    half = 32
    inv = (np.float32(ROPE_BASE) ** (-np.arange(half, dtype=np.float32) / np.float32(half))).astype(np.float32)
    pos = np.concatenate([np.arange(cfg.SEQ, dtype=np.float32)] + [cfg.PAST + np.arange(cfg.SS, dtype=np.float32)] * cfg.NBS)
    ang = (pos[:, None] * inv[None, :]).astype(np.float32)
    c, s = np.cos(ang).astype(np.float32).T, np.sin(ang).astype(np.float32).T
    rope = np.stack([np.concatenate([c, c], 0), np.concatenate([-s, s], 0)], axis=1)
    rope = np.ascontiguousarray(rope, np.float32)
    maps = []
    NBS = cfg.NBS
    for core in range(NCORES):
        sb = slice(core * NBS, (core + 1) * NBS)
        m = dict(pcol=pcol, cst=cst, rope=rope)
        for i in range(cfg.DEPTH):
            m[f"wblob{i}"] = wblob[i]
        m["xT"] = np.ascontiguousarray(inp["x_prompt"][core].T)
        m["xsT"] = np.ascontiguousarray(inp["x_sample"][sb].reshape(cfg.NS, cfg.D).T)
        if cfg.NM:
            ck = inp["cache_ckv"][:, sb]
            m["cck"] = np.ascontiguousarray(ck)
            m["cckT"] = np.ascontiguousarray(ck.transpose(0, 1, 3, 2))
            m["ckrT"] = np.ascontiguousarray(inp["cache_krope"][:, sb].transpose(0, 1, 3, 2))
        else:
            m["cck"] = np.zeros((1, NBS, cfg.PAST, cfg.KL), np.float32)
            m["cckT"] = np.zeros((1, NBS, cfg.KL, cfg.PAST), np.float32)
            m["ckrT"] = np.zeros((1, NBS, 64, cfg.PAST), np.float32)
        if cfg.NP:
            m["spool"] = np.ascontiguousarray(inp["state_pool"][:, sb].reshape(cfg.NP, NBS, 15, cfg.DC, 128).transpose(0, 4, 3, 1, 2))
        else:
            m["spool"] = np.zeros((1, 128, cfg.DC, NBS, 15), np.float32)
        m["sconv"] = np.ascontiguousarray(inp["state_conv"][:, sb].reshape(cfg.DEPTH, NBS, 2, 2 * cfg.FP, 128).transpose(0, 4, 3, 1, 2))
        maps.append(m)
    return maps


def gather_outputs(cfg, res):
    R = res
    NM, NP = cfg.NM, cfg.NP
    y = np.stack([r["yT"].T for r in R])
    ys = np.concatenate([r["ysT"].T.reshape(cfg.NBS, cfg.SS, cfg.D) for r in R], 0)
    ckv_p = np.stack([r["ockvT"][:NM].transpose(0, 2, 1) for r in R], 1)
    kr_p = np.stack([r["okrT"][:NM].transpose(0, 2, 1) for r in R], 1)
    pool_p = np.stack([r["opool"][:NP].transpose(0, 3, 2, 1).reshape(NP, 15, cfg.D) for r in R], 1)
    conv_p = np.stack([r["oconv"].transpose(0, 3, 2, 1).reshape(cfg.DEPTH, 2, 2 * cfg.DFF) for r in R], 1)
    ckv_s = np.concatenate([r["ockvTs"][:NM].transpose(0, 2, 1).reshape(NM, cfg.NBS, cfg.SS, cfg.KL) for r in R], 1)
    kr_s = np.concatenate([r["okrTs"][:NM].transpose(0, 2, 1).reshape(NM, cfg.NBS, cfg.SS, 64) for r in R], 1)
    pool_s = np.concatenate([r["opools"][:NP].transpose(0, 3, 4, 2, 1).reshape(NP, cfg.NBS, 15, cfg.D) for r in R], 1)
    conv_s = np.concatenate([r["oconvs"].transpose(0, 3, 4, 2, 1).reshape(cfg.DEPTH, cfg.NBS, 2, 2 * cfg.DFF) for r in R], 1)
    outs = (y, ys, ckv_p, kr_p, pool_p, conv_p, ckv_s, kr_s, pool_s, conv_s)
    return tuple(np.ascontiguousarray(o, dtype=np.float32) for o in outs)


def run_cfg(cfg, inputs):
    inp = {k: np.asarray(v) for k, v in inputs.items()}
    maps = prep_inputs(cfg, inp)
    nc = build_program(cfg)
    res = run_bass_kernel_spmd(nc, maps, core_ids=list(range(NCORES)))
    return gather_outputs(cfg, res.results)


def kernel(**inputs):
    return run_cfg(Cfg(), inputs)
```
